# Optimizing a Trainium2 kernel written in Bass

```python
import math
import jax, jax.numpy as jnp
from jax import lax
import numpy as np

D_MODEL = 1024
BATCH = 8
SEQ = 4096
DEPTH = 2

MIX_WIDTH = D_MODEL
MLA_HEADS = 8
MLA_NOPE_DIM = 64
MLA_ROPE_DIM = 32
MLA_V_DIM = 64
MLA_Q_LORA = 384
MLA_KV_LORA = 256
MLA_WIDTH = MLA_HEADS * MLA_V_DIM
ROPE_THETA = 10000.0
Q_BLOCK = 128
CONV_CH = MIX_WIDTH - MLA_WIDTH
CONV_GROUPS = 8
CONV_K = 31
IN_COLS = MLA_Q_LORA + MLA_KV_LORA + MLA_ROPE_DIM + 2 * CONV_CH
D_FF = 2816
FFN_CONV_K = 3
EPS = 1e-6

kernel_name = "hybrid_mla_conformer_convffn_encoder"


def _rmsnorm(x, g):
    xf = x.astype(jnp.float32)
    y = xf * lax.rsqrt(jnp.mean(xf * xf, axis=-1, keepdims=True) + EPS)
    return (y * g.astype(jnp.float32)).astype(x.dtype)


def _layernorm(x, g, b):
    xf = x.astype(jnp.float32)
    mu = jnp.mean(xf, axis=-1, keepdims=True)
    var = jnp.mean(jnp.square(xf - mu), axis=-1, keepdims=True)
    y = (xf - mu) * lax.rsqrt(var + EPS)
    return (y * g.astype(jnp.float32) + b.astype(jnp.float32)).astype(x.dtype)


def _dwconv(x, w, b):
    k = w.shape[0]
    pad = k // 2
    y = lax.conv_general_dilated(
        x, w[:, None, :].astype(x.dtype), window_strides=(1,),
        padding=[(pad, pad)], dimension_numbers=("NWC", "WIO", "NWC"),
        feature_group_count=x.shape[-1])
    return y + b.astype(x.dtype)


def _rope_tables(positions):
    inv_freq = 1.0 / (ROPE_THETA ** (jnp.arange(0, MLA_ROPE_DIM, 2, dtype=jnp.float32) / MLA_ROPE_DIM))
    ang = positions.astype(jnp.float32)[..., None] * inv_freq
    return jnp.cos(ang), jnp.sin(ang)


def _apply_rope(t, cos, sin):
    half = t.shape[-1] // 2
    t1, t2 = t[..., :half], t[..., half:]
    cos = cos.astype(t.dtype)
    sin = sin.astype(t.dtype)
    return jnp.concatenate([t1 * cos - t2 * sin, t2 * cos + t1 * sin], axis=-1)


def _mla_attention(q_nope, q_rope, k_nope, k_rope, v):
    b, s, h, dn = q_nope.shape
    nb = s // Q_BLOCK
    scale = 1.0 / math.sqrt(dn + q_rope.shape[-1])

    def to_blocks(t):
        return jnp.moveaxis(t.reshape(b, nb, Q_BLOCK, *t.shape[2:]), 1, 0)

    def block(args):
        qn, qr = args
        sc = (jnp.einsum("bqhd,bkhd->bhqk", qn, k_nope)
              + jnp.einsum("bqhr,bkr->bhqk", qr, k_rope))
        p = jax.nn.softmax(sc.astype(jnp.float32) * scale, axis=-1).astype(v.dtype)
        return jnp.einsum("bhqk,bkhd->bqhd", p, v)

    o = lax.map(block, (to_blocks(q_nope), to_blocks(q_rope)))
    return jnp.moveaxis(o, 0, 1).reshape(b, s, h * v.shape[-1])


def setup_inputs(seed: int = 0) -> dict:
    key = jax.random.key(seed)
    ks = jax.random.split(key, 24)
    f32 = jnp.float32

    def w(k, shape, fan_in, gain=1.0):
        return jax.random.normal(k, shape, f32) * (gain * fan_in ** -0.5)

    def gain(k, shape):
        return 1.0 + 0.02 * jax.random.normal(k, shape, f32)

    def bias(k, shape):
        return 0.02 * jax.random.normal(k, shape, f32)

    x = jax.random.normal(ks[0], (BATCH, SEQ, D_MODEL), f32)
    offs = jax.random.randint(ks[1], (BATCH, 1), 0, 4096, dtype=jnp.int32)
    positions = (jnp.arange(SEQ, dtype=jnp.int32)[None, :] + offs).astype(jnp.int32)
    return {
        "x": x,
        "positions": positions,
        "g_mix": gain(ks[2], (DEPTH, D_MODEL)),
        "w_in": w(ks[3], (DEPTH, D_MODEL, IN_COLS), D_MODEL),
        "g_q": gain(ks[4], (DEPTH, MLA_Q_LORA)),
        "w_uq": w(ks[5], (DEPTH, MLA_Q_LORA, MLA_HEADS * (MLA_NOPE_DIM + MLA_ROPE_DIM)), MLA_Q_LORA),
        "g_kv": gain(ks[6], (DEPTH, MLA_KV_LORA)),
        "w_ukv": w(ks[7], (DEPTH, MLA_KV_LORA, MLA_HEADS * (MLA_NOPE_DIM + MLA_V_DIM)), MLA_KV_LORA),
        "w_dw_conv": w(ks[8], (DEPTH, CONV_K, CONV_CH), CONV_K),
        "b_dw_conv": bias(ks[9], (DEPTH, CONV_CH)),
        "g_conv_ln": gain(ks[10], (DEPTH, CONV_CH)),
        "b_conv_ln": bias(ks[11], (DEPTH, CONV_CH)),
        "w_o": w(ks[12], (DEPTH, MIX_WIDTH, D_MODEL), MIX_WIDTH, 0.5),
        "g_ffn": gain(ks[13], (DEPTH, D_MODEL)),
        "w_up": w(ks[14], (DEPTH, D_MODEL, 2 * D_FF), D_MODEL),
        "w_dw_ffn": w(ks[15], (DEPTH, FFN_CONV_K, 2 * D_FF), FFN_CONV_K),
        "b_dw_ffn": bias(ks[16], (DEPTH, 2 * D_FF)),
        "w_down": w(ks[17], (DEPTH, D_FF, D_MODEL), D_FF, 0.5),
        "g_final": gain(ks[18], (D_MODEL,)),
    }


def reference(x, positions, g_mix, w_in, g_q, w_uq, g_kv, w_ukv, w_dw_conv, b_dw_conv,
              g_conv_ln, b_conv_ln, w_o, g_ffn, w_up, w_dw_ffn, b_dw_ffn, w_down, g_final):
    b, s, _ = x.shape
    cos, sin = _rope_tables(positions)
    cos_h, sin_h = cos[:, :, None, :], sin[:, :, None, :]
    o1 = MLA_Q_LORA
    o2 = o1 + MLA_KV_LORA
    o3 = o2 + MLA_ROPE_DIM
    for l in range(DEPTH):
        h = _rmsnorm(x, g_mix[l])
        p = h @ w_in[l].astype(h.dtype)
        c_q = _rmsnorm(p[..., :o1], g_q[l])
        q = (c_q @ w_uq[l].astype(h.dtype)).reshape(b, s, MLA_HEADS, MLA_NOPE_DIM + MLA_ROPE_DIM)
        q_nope = q[..., :MLA_NOPE_DIM]
        q_rope = _apply_rope(q[..., MLA_NOPE_DIM:], cos_h, sin_h)
        c_kv = _rmsnorm(p[..., o1:o2], g_kv[l])
        kv = (c_kv @ w_ukv[l].astype(h.dtype)).reshape(b, s, MLA_HEADS, MLA_NOPE_DIM + MLA_V_DIM)
        k_nope = kv[..., :MLA_NOPE_DIM]
        v = kv[..., MLA_NOPE_DIM:]
        k_rope = _apply_rope(p[..., o2:o3], cos, sin)
        y_attn = _mla_attention(q_nope, q_rope, k_nope, k_rope, v)
        a, gt = jnp.split(p[..., o3:], 2, axis=-1)
        u = a * jax.nn.sigmoid(gt)
        u = _dwconv(u, w_dw_conv[l], b_dw_conv[l])
        u = jax.nn.silu(_layernorm(u, g_conv_ln[l], b_conv_ln[l]))
        y = jnp.concatenate([y_attn, u], axis=-1) @ w_o[l].astype(h.dtype)
        x = x + y
        h2 = _rmsnorm(x, g_ffn[l])
        z = _dwconv(h2 @ w_up[l].astype(h2.dtype), w_dw_ffn[l], b_dw_ffn[l])
        zg, zv = jnp.split(z, 2, axis=-1)
        x = x + (jax.nn.silu(zg) * zv) @ w_down[l].astype(h2.dtype)
    return _rmsnorm(x, g_final)
```

```python
import math
import sys
import numpy as np
import ml_dtypes
import concourse.bass as bass
import concourse.mybir as mybir
from concourse.bass_utils import run_bass_kernel_spmd

F32 = mybir.dt.float32
BF16 = mybir.dt.bfloat16
I32 = mybir.dt.int32
ALU = mybir.AluOpType
AF = mybir.ActivationFunctionType

S = 4096
D = 1024
NL = 2
NH = 8
TT = 512
NT = S // TT
DFF = 2816
NJ = DFF // 128
EPS = 1e-6
IN_COLS = 1696
WIN_W = 1792
SCALE = 1.0 / math.sqrt(96.0)

PL = 333
P_GMIX, P_GQ, P_GKV, P_GFFN, P_WCONV, P_BCONV, P_GLN, P_BLN, P_WFFN, P_BFFN = 0, 8, 11, 13, 21, 145, 149, 153, 157, 289
P_GFINAL = NL * PL
P_INVF = NL * PL + 8
NP = NL * PL + 9

MAGIC = 12582912.0
TWO_PI = 2.0 * math.pi
C1 = 6.28125
C2 = TWO_PI - C1
PI_LO = 3.1415925


PE_TAGS = []


class Sem:
    def __init__(self, nc, name):
        self.h = nc.alloc_semaphore(name=name)
        self.v = 0


class Res:
    __slots__ = ("name", "w", "r")

    def __init__(self, name):
        self.name = name
        self.w = None
        self.r = {}


class Eng:
    def __init__(self, nc, name, b, is_pe=False):
        self.name = name
        self.b = b
        self.sem = Sem(nc, "e_" + name)
        self.seen = {}
        self.is_pe = is_pe
        self.pending = []


class Ctx:
    def __init__(self, nc):
        self.nc = nc
        self.pe = Eng(nc, "pe", nc.tensor, True)
        self.act = Eng(nc, "act", nc.scalar)
        self.dve = Eng(nc, "dve", nc.vector)
        self.pool = Eng(nc, "pool", nc.gpsimd)
        self.sp = Eng(nc, "sp", nc.sync)
        self.engs = [self.pe, self.act, self.dve, self.pool, self.sp]
        self.dsems = {}

    def dsem(self, name):
        if name not in self.dsems:
            self.dsems[name] = Sem(self.nc, "d_" + name)
        return self.dsems[name]

    def _wait(self, E, deps):
        for s, v in deps.items():
            if E.seen.get(s, 0) < v:
                E.b.wait_ge(s.h, v)
                E.seen[s] = v

    def op(self, E, build, reads=(), writes=(), sig=True, dsem=None):
        deps = {}

        def add(ev, kind):
            if ev is None:
                return
            s, v = ev
            if v is None:
                if s is E.sem:
                    return
                raise RuntimeError("dependency on unsignalled instruction")
            if s is E.sem and (kind != "raw" or E.is_pe):
                return
            if dsem is not None and s is dsem and kind == "waw":
                return
            if deps.get(s, 0) < v:
                deps[s] = v

        for r in reads:
            add(r.w, "raw")
        for w in writes:
            for E2 in self.engs:
                if E2 is not E:
                    for res, kind in E2.pending:
                        if res is w:
                            raise RuntimeError("write to %s while %s has an unsignalled access" % (w.name, E2.name))
            add(w.w, "waw")
            for s, v in w.r.items():
                add((s, v), "war")
        self._wait(E, deps)
        ins = build()
        if dsem is not None:
            dsem.v += 16
            ins.then_inc(dsem.h, 16)
            ev = (dsem, dsem.v)
        elif sig:
            E.sem.v += 1
            ins.then_inc(E.sem.h, 1)
            ev = (E.sem, E.sem.v)
            for res, kind in E.pending:
                if kind == "r":
                    if res.r.get(E.sem, 0) < E.sem.v:
                        res.r[E.sem] = E.sem.v
                else:
                    if res.w is not None and res.w[0] is E.sem and res.w[1] is None:
                        res.w = ev
            E.pending = []
        else:
            ev = None
        for r in reads:
            if ev is None:
                E.pending.append((r, "r"))
            elif r.r.get(ev[0], 0) < ev[1]:
                r.r[ev[0]] = ev[1]
        for w in writes:
            if ev is None:
                w.w = (E.sem, None)
                E.pending.append((w, "w"))
            else:
                w.w = ev
            w.r = {}
        return ins

    def dma(self, out, in_, reads=(), writes=(), sem=None, E=None):
        E = E or self.sp
        return self.op(E, lambda: E.b.dma_start(out=out, in_=in_), reads, writes, dsem=sem)

    def barrier(self):
        allsems = [e.sem for e in self.engs] + list(self.dsems.values())
        for E in self.engs:
            assert not E.pending, E.name
            if E.is_pe:
                continue
            deps = {s: s.v for s in allsems if s is not E.sem and s.v > 0}
            self._wait(E, deps)


class Arena:
    def __init__(self, ap, nwords):
        self.ap = ap
        self.n = nwords
        self.off = 0

    def reset(self, off=0):
        self.off = off

    def alloc_at(self, off, free_shape, dtype):
        save = self.off
        self.off = off
        v = self.alloc(free_shape, dtype)
        self.off = save
        return v

    def alloc(self, free_shape, dtype):
        nel = 1
        for x in free_shape:
            nel *= x
        nbytes = nel * (4 if dtype in (F32, I32) else 2)
        nw = (nbytes + 3) // 4
        nw = (nw + 3) // 4 * 4
        assert self.off + nw <= self.n, ("arena overflow", self.off, nw, self.n)
        v = self.ap[:, self.off:self.off + nw]
        self.off += nw
        if dtype != F32:
            v = v.bitcast(dtype)
        v = v[:, 0:nel]
        if len(free_shape) == 2:
            v = v.rearrange("p (a b) -> p a b", a=free_shape[0])
        elif len(free_shape) == 3:
            v = v.rearrange("p (a b c) -> p a b c", a=free_shape[0], b=free_shape[1])
        return v


def build_program(n_layers=NL, final=True, debug=False):
    nc = bass.Bass("TRN2", target_bir_lowering=False)
    dk = "ExternalOutput" if debug else "Internal"
    xT = nc.dram_tensor("xT", [D, S], F32, kind="ExternalInput").ap()
    pos = nc.dram_tensor("pos", [1, S], I32, kind="ExternalInput").ap()
    params = nc.dram_tensor("params", [128, NP], F32, kind="ExternalInput").ap()
    w_in = nc.dram_tensor("w_in", [NL, D, IN_COLS], F32, kind="ExternalInput").ap()
    w_uq = nc.dram_tensor("w_uq", [NL, 384, 768], F32, kind="ExternalInput").ap()
    w_ukv = nc.dram_tensor("w_ukv", [NL, 256, 1024], F32, kind="ExternalInput").ap()
    w_o = nc.dram_tensor("w_o", [NL, D, D], F32, kind="ExternalInput").ap()
    w_up = nc.dram_tensor("w_up", [NL, D, 2 * DFF], F32, kind="ExternalInput").ap()
    w_down = nc.dram_tensor("w_down", [NL, DFF, D], F32, kind="ExternalInput").ap()
    outT = nc.dram_tensor("outT", [D, S], F32, kind="ExternalOutput").ap()

    Win_d = nc.dram_tensor("Win_d", [NL, 128, 8 * WIN_W], BF16).ap()
    Wuq_d = nc.dram_tensor("Wuq_d", [NL, 128, 3 * 1024], BF16).ap()
    Wk_d = nc.dram_tensor("Wk_d", [NL, 128, 2 * 512], BF16).ap()
    Wv_d = nc.dram_tensor("Wv_d", [NL, 128, 2 * 512], BF16).ap()
    Wo_d = nc.dram_tensor("Wo_d", [NL, 128, 8 * D], BF16).ap()
    Wup_d = nc.dram_tensor("Wup_d", [NL, 128, 8 * 2 * DFF], BF16).ap()
    Wdn_d = nc.dram_tensor("Wdn_d", [NL, 128, NJ * D], BF16).ap()
    rope_d = nc.dram_tensor("rope_d", [2, 32, S], F32, kind=dk).ap()
    Qd = nc.dram_tensor("Qd", [NH, 96, S], BF16, kind=dk).ap()
    Kd = nc.dram_tensor("Kd", [NH, 96, S], BF16, kind=dk).ap()
    Vd = nc.dram_tensor("Vd", [NH, 128, 32 * 65], BF16, kind=dk).ap()
    X1 = nc.dram_tensor("X1", [D, S + 2], F32, kind=dk).ap()
    X2 = nc.dram_tensor("X2", [D, S], F32, kind=dk).ap()
    dbg_u = nc.dram_tensor("dbg_u", [D // 2, S], BF16, kind=dk).ap()
    dbg_y = nc.dram_tensor("dbg_y", [D // 2, S], BF16, kind=dk).ap()
    dbg_h2 = nc.dram_tensor("dbg_h2", [D, TT], BF16, kind=dk).ap()
    dbg_g = nc.dram_tensor("dbg_g", [DFF, TT], BF16, kind=dk).ap()
    dbg_tg = nc.dram_tensor("dbg_tg", [128, TT], F32, kind=dk).ap()

    ARW = 51800
    arena_t = nc.alloc_sbuf_tensor("arena", [128, ARW], F32)
    prm_t = nc.alloc_sbuf_tensor("prm", [128, NP], F32)
    ident_t = nc.alloc_sbuf_tensor("ident", [128, 128], BF16)
    ones_t = nc.alloc_sbuf_tensor("ones", [128, 128], BF16)
    sel_t = nc.alloc_sbuf_tensor("sel", [128, 64], F32)
    zero_t = nc.alloc_sbuf_tensor("zero", [128, 16], F32)
    eps_t = nc.alloc_sbuf_tensor("epsc", [128, 1], F32)
    arena = Arena(arena_t.ap() if hasattr(arena_t, "ap") else arena_t, ARW)
    prm = prm_t.ap() if hasattr(prm_t, "ap") else prm_t
    ident = ident_t.ap() if hasattr(ident_t, "ap") else ident_t
    ones = ones_t.ap() if hasattr(ones_t, "ap") else ones_t
    sel = sel_t.ap() if hasattr(sel_t, "ap") else sel_t
    zero = zero_t.ap() if hasattr(zero_t, "ap") else zero_t
    eps_ap = (eps_t.ap() if hasattr(eps_t, "ap") else eps_t)[:, 0:1]
    ps_t = nc.alloc_psum_tensor("psall", [128, 4096], F32)
    ps_all = ps_t.ap() if hasattr(ps_t, "ap") else ps_t
    banks = [ps_all[:, i * 512:(i + 1) * 512] for i in range(8)]

    C = Ctx(nc)
    pe, act, dve, pool, sp = C.pe, C.act, C.dve, C.pool, C.sp
    bank_res = [Res("bank%d" % i) for i in range(8)]

    def mm(out, lhsT, rhs, start, stop, reads, writes, sig=None):
        PE_TAGS.append(sys._getframe(1).f_lineno)
        C.op(pe, lambda: nc.tensor.matmul(out, lhsT=lhsT, rhs=rhs, start=start, stop=stop),
             reads, writes, sig=(stop if sig is None else sig))

    def ts(E, out, in0, s1, s2, op0, op1=None, reads=(), writes=()):
        if op1 is None:
            return C.op(E, lambda: E.b.tensor_scalar(out=out, in0=in0, scalar1=s1, scalar2=None, op0=op0), reads, writes)
        return C.op(E, lambda: E.b.tensor_scalar(out=out, in0=in0, scalar1=s1, scalar2=s2, op0=op0, op1=op1), reads, writes)

    def tt(E, out, in0, in1, op, reads=(), writes=()):
        return C.op(E, lambda: E.b.tensor_tensor(out=out, in0=in0, in1=in1, op=op), reads, writes)

    def stt(E, out, in0, scalar, in1, op0, op1, reads=(), writes=()):
        return C.op(E, lambda: E.b.scalar_tensor_tensor(out=out, in0=in0, scalar=scalar, in1=in1, op0=op0, op1=op1), reads, writes)

    def cp(E, out, in_, reads=(), writes=()):
        if E is act:
            return C.op(E, lambda: nc.scalar.copy(out=out, in_=in_), reads, writes)
        return C.op(E, lambda: E.b.tensor_copy(out=out, in_=in_), reads, writes)

    def actf(out, in_, func, bias=None, scale=None, reads=(), writes=()):
        kw = {}
        if bias is not None:
            kw["bias"] = bias
        if scale is not None:
            kw["scale"] = scale
        return C.op(act, lambda: nc.scalar.activation(out=out, in_=in_, func=func, **kw), reads, writes)

    def rstd_from(psum_ap, n, out_ap, tmp_ap, reads, writes_tmp, writes_out, npart=128):
        actf(tmp_ap, psum_ap, AF.Ln, bias=eps_ap, scale=1.0 / n, reads=list(reads) + [const_r], writes=writes_tmp)
        actf(out_ap, tmp_ap, AF.Exp, scale=-0.5, reads=writes_tmp, writes=writes_out)

    prm_r = Res("prm")
    const_r = Res("const")
    C.dma(prm[:, :], params[:, :], writes=[prm_r], sem=C.dsem("prm"))
    C.op(pool, lambda: nc.gpsimd.memset(ident[:], 0.0), writes=[const_r])
    C.op(pool, lambda: nc.gpsimd.affine_select(out=ident[:], in_=ident[:], pattern=[[-1, 128]],
                                               compare_op=ALU.not_equal, fill=1.0, base=0,
                                               channel_multiplier=1), reads=[const_r], writes=[const_r])
    C.op(pool, lambda: nc.gpsimd.memset(ones[:], 1.0), writes=[const_r])
    C.op(pool, lambda: nc.gpsimd.memset(zero[:], 0.0), writes=[const_r])
    C.op(pool, lambda: nc.gpsimd.memset(eps_ap, EPS), writes=[const_r])
    C.op(pool, lambda: nc.gpsimd.memset(sel[:], 0.0), writes=[const_r])
    C.op(pool, lambda: nc.gpsimd.memset(sel[64:65, :], 1.0), reads=[const_r], writes=[const_r])
    x1pad_r = Res("x1pad")
    X1v = X1.rearrange("(c p) t -> p c t", p=128)
    with nc.allow_non_contiguous_dma(reason="two zero pad columns, once"):
        for col in (0, S + 1):
            C.dma(X1v[:, :, col:col + 1], zero[:, 0:8].rearrange("p (c o) -> p c o", o=1),
                  reads=[const_r], writes=[x1pad_r], sem=C.dsem("pad"))

    arena.reset()
    R = slice(64, 96)
    posi = arena.alloc([S], I32)
    ang = arena.alloc([S], F32)
    kk = arena.alloc([S], F32)
    rr = arena.alloc([S], F32)
    sn2 = [arena.alloc([S], F32) for _ in range(2)]
    rp = Res("rope_tmp")
    invf = prm[R, P_INVF:P_INVF + 1]
    C.dma(posi[R, :], pos.partition_broadcast(32), writes=[rp], sem=C.dsem("pos"))

    rr2 = [rr, arena.alloc([S], F32)]

    rope_ops = []

    def emit_rope_dve():
        R_ = rope_ops.append
        R_(lambda: cp(dve, ang[R, :], posi[R, :], reads=[rp], writes=[rp]))
        R_(lambda: ts(dve, ang[R, :], ang[R, :], invf, None, ALU.mult, reads=[rp, prm_r], writes=[rp]))
        for which in range(2):
            rw = rr2[which]
            if which == 0:
                R_(lambda: ts(dve, kk[R, :], ang[R, :], 1.0 / TWO_PI, MAGIC, ALU.mult, ALU.add, reads=[rp], writes=[rp]))
            else:
                R_(lambda: ts(dve, kk[R, :], ang[R, :], 1.0 / TWO_PI, 0.25, ALU.mult, ALU.add, reads=[rp], writes=[rp]))
                R_(lambda: ts(dve, kk[R, :], kk[R, :], MAGIC, None, ALU.add, reads=[rp], writes=[rp]))
            R_(lambda: ts(dve, kk[R, :], kk[R, :], -MAGIC, None, ALU.add, reads=[rp], writes=[rp]))
            R_(lambda rw=rw: stt(dve, rw[R, :], kk[R, :], -C1, ang[R, :], ALU.mult, ALU.add, reads=[rp], writes=[rp]))
            R_(lambda rw=rw: stt(dve, rw[R, :], kk[R, :], -C2, rw[R, :], ALU.mult, ALU.add, reads=[rp], writes=[rp]))
            if which == 1:
                R_(lambda rw=rw: ts(dve, rw[R, :], rw[R, :], math.pi / 2.0, None, ALU.add, reads=[rp], writes=[rp]))
            R_(lambda rw=rw: ts(dve, rw[R, :], rw[R, :], PI_LO, -PI_LO, ALU.min, ALU.max, reads=[rp], writes=[rp]))

    def emit_rope_act():
        for which in range(2):
            actf(sn2[which][R, :], rr2[which][R, :], AF.Sin, reads=[rp], writes=[rp])
            C.dma(rope_d[which, :, :], sn2[which][R, :], reads=[rp], sem=C.dsem("ropest"))

    class Conv:
        def __init__(self, tag, nstg, engs):
            self.tag = tag
            self.n = nstg
            self.stg = [arena.alloc([2816], F32) for _ in range(nstg)]
            self.obf = [arena.alloc([2816], BF16) for _ in range(nstg)]
            self.stg_r = [Res("%sstg%d" % (tag, i)) for i in range(nstg)]
            self.obf_r = [Res("%sobf%d" % (tag, i)) for i in range(nstg)]
            self.engs = engs
            self.queue = []
            self.loaded = 0
            self.done = 0

        def add(self, pcs):
            self.queue += pcs

        def step(self):
            if self.done >= len(self.queue):
                return False
            while self.loaded < min(len(self.queue), self.done + self.n):
                m = self.loaded
                i = m % self.n
                C.dma(self.stg[i][:, 0:self.queue[m][1]], self.queue[m][0], writes=[self.stg_r[i]],
                      sem=C.dsem("%sstg%d" % (self.tag, i)))
                self.loaded += 1
            m = self.done
            i = m % self.n
            E = self.engs[m % len(self.engs)]
            self.queue[m][2](E, self.stg[i], self.obf[i], [self.stg_r[i]], [self.obf_r[i]],
                             C.dsem("%sobf%d" % (self.tag, i)))
            self.done += 1
            return True

        def flush(self):
            while self.step():
                pass

    def scale_to(E, out, in_, scale_ap, rs, ws, neg=False):
        if E is act:
            if scale_ap is None:
                return actf(out, in_, AF.Copy, reads=rs, writes=ws)
            if neg:
                E = dve
            else:
                return actf(out, in_, AF.Identity, scale=scale_ap, reads=rs + [prm_r], writes=ws)
        if scale_ap is None:
            return cp(E, out, in_, reads=rs, writes=ws)
        if neg:
            ts(E, out, in_, scale_ap, None, ALU.mult, reads=rs + [prm_r], writes=ws)
            return ts(E, out, out, -1.0, None, ALU.mult, reads=ws, writes=ws)
        return ts(E, out, in_, scale_ap, None, ALU.mult, reads=rs + [prm_r], writes=ws)

    groups = {}
    cur = [None]
    SE = [dve]

    def conv_piece(src, ncols, emit):
        groups.setdefault(cur[0], []).append((src, ncols, emit))

    for l in range(n_layers):
        pb = l * PL
        cur[0] = (l, "mix")
        for kc in range(8):
            g = prm[:, pb + P_GMIX + kc:pb + P_GMIX + kc + 1]

            def emit(E, s_, o_, rs, ws, osem, g=g, kc=kc, l=l):
                scale_to(E, o_[:, 0:IN_COLS], s_[:, 0:IN_COLS], g, rs, ws)
                ts(SE[0], o_[:, 1696:1760], s_[:, 0:64], 0.0, None, ALU.mult, reads=rs, writes=ws)
                scale_to(SE[0], o_[:, 1760:1776], s_[:, 656:672], g, rs, ws, neg=True)
                scale_to(SE[0], o_[:, 1776:1792], s_[:, 640:656], g, rs, ws)
                C.dma(Win_d[l, :, kc * WIN_W:(kc + 1) * WIN_W], o_[:, 0:WIN_W], reads=ws,
                      sem=osem)
            conv_piece(w_in[l, kc * 128:(kc + 1) * 128, :], IN_COLS, emit)
        for kc in range(3):
            g = prm[:, pb + P_GQ + kc:pb + P_GQ + kc + 1]

            def emit(E, s_, o_, rs, ws, osem, g=g, kc=kc, l=l):
                s3 = s_[:, 0:768].rearrange("p (h d) -> p h d", d=96)
                on = o_[:, 0:512].rearrange("p (h d) -> p h d", d=64)
                orp = o_[:, 512:768].rearrange("p (h d) -> p h d", d=32)
                ort = o_[:, 768:1024].rearrange("p (h d) -> p h d", d=32)
                scale_to(SE[0], on, s3[:, :, 0:64], g, rs, ws)
                scale_to(SE[0], orp, s3[:, :, 64:96], g, rs, ws)
                scale_to(SE[0], ort[:, :, 0:16], s3[:, :, 80:96], g, rs, ws, neg=True)
                scale_to(SE[0], ort[:, :, 16:32], s3[:, :, 64:80], g, rs, ws)
                C.dma(Wuq_d[l, :, kc * 1024:(kc + 1) * 1024], o_[:, 0:1024], reads=ws, sem=osem)
            conv_piece(w_uq[l, kc * 128:(kc + 1) * 128, :], 768, emit)
        for kc in range(2):
            g = prm[:, pb + P_GKV + kc:pb + P_GKV + kc + 1]

            def emit(E, s_, o_, rs, ws, osem, g=g, kc=kc, l=l):
                s3 = s_[:, 0:1024].rearrange("p (h d) -> p h d", d=128)
                ok = o_[:, 0:512].rearrange("p (h d) -> p h d", d=64)
                ov = o_[:, 512:1024].rearrange("p (h d) -> p h d", d=64)
                scale_to(SE[0], ok, s3[:, :, 0:64], g, rs, ws)
                scale_to(pool, ov, s3[:, :, 64:128], g, rs, ws)
                C.dma(Wk_d[l, :, kc * 512:(kc + 1) * 512], o_[:, 0:512], reads=ws,
                      sem=osem)
                C.dma(Wv_d[l, :, kc * 512:(kc + 1) * 512], o_[:, 512:1024], reads=ws,
                      sem=osem)
            conv_piece(w_ukv[l, kc * 128:(kc + 1) * 128, :], 1024, emit)
        for kc in range(8):
            def emit(E, s_, o_, rs, ws, osem, kc=kc, l=l):
                scale_to(E, o_[:, 0:D], s_[:, 0:D], None, rs, ws)
                C.dma(Wo_d[l, :, kc * D:(kc + 1) * D], o_[:, 0:D], reads=ws, sem=osem)
            conv_piece(w_o[l, kc * 128:(kc + 1) * 128, :], D, emit)
        cur[0] = (l, "ffn")
        for kc in range(8):
            g = prm[:, pb + P_GFFN + kc:pb + P_GFFN + kc + 1]
            for hf in range(2):
                def emit(E, s_, o_, rs, ws, osem, g=g, kc=kc, hf=hf, l=l):
                    scale_to(E, o_[:, 0:DFF], s_[:, 0:DFF], g, rs, ws)
                    C.dma(Wup_d[l, :, kc * 2 * DFF + hf * DFF:kc * 2 * DFF + (hf + 1) * DFF], o_[:, 0:DFF],
                          reads=ws, sem=osem)
                conv_piece(w_up[l, kc * 128:(kc + 1) * 128, hf * DFF:(hf + 1) * DFF], DFF, emit)
        for j in range(NJ):
            def emit(E, s_, o_, rs, ws, osem, j=j, l=l):
                scale_to(E, o_[:, 0:D], s_[:, 0:D], None, rs, ws)
                C.dma(Wdn_d[l, :, j * D:(j + 1) * D], o_[:, 0:D], reads=ws, sem=osem)
            conv_piece(w_down[l, j * 128:(j + 1) * 128, :], D, emit)
    emit_rope_dve()
    cv0 = Conv("u", 3, [act])
    cv0.add(groups[(0, "mix")][0:13])
    while cv0.step():
        for _ in range(2):
            if rope_ops:
                rope_ops.pop(0)()
    while rope_ops:
        rope_ops.pop(0)()
    emit_rope_act()
    C.barrier()
    bg_queue = list(groups[(0, "mix")][13:]) + list(groups[(0, "ffn")])
    for l_ in range(1, n_layers):
        bg_queue += groups[(l_, "mix")]
    bg_split = len(bg_queue)
    for l_ in range(1, n_layers):
        bg_queue += groups[(l_, "ffn")]
    bg_done = [0]

    for l in range(n_layers):
        pb = l * PL
        xin = (xT if l == 0 else X2).rearrange("(c p) t -> p c t", p=128)
        last = (l == n_layers - 1)

        arena.reset()
        u_glu = arena.alloc([4, S + 30], BF16)
        offA = arena.off
        cosT = arena.alloc([S], F32)
        sinT = arena.alloc([S], F32)
        Win = arena.alloc([8, WIN_W], BF16)
        Wuq = arena.alloc([3, 1024], BF16)
        Wk = arena.alloc([2, 512], BF16)
        Wv = arena.alloc([2, 512], BF16)
        xt = arena.alloc([8, TT], F32)
        xbb = [arena.alloc([8, TT], BF16) for _ in range(2)]
        sq8 = arena.alloc([8, TT], BF16)
        rb = arena.alloc([TT], F32)
        tmpA = arena.alloc([TT], F32)
        cqf = arena.alloc([3, TT], F32)
        ckvf = arena.alloc([2, TT], F32)
        rq = arena.alloc([TT], F32)
        rkv = arena.alloc([TT], F32)
        cqn = arena.alloc([3, TT], BF16)
        ckvn = arena.alloc([2, TT], BF16)
        sgb = [arena.alloc([TT], F32) for _ in range(2)]
        t1b = [arena.alloc([TT], F32) for _ in range(2)]
        t2b = [arena.alloc([TT], F32) for _ in range(2)]
        Qt = arena.alloc([8, TT], BF16)
        t3 = arena.alloc([2, TT], BF16)
        Kt = arena.alloc([8, TT], BF16)
        krb = arena.alloc([TT], BF16)
        Vt = arena.alloc([8, 4, 65], BF16)

        wA_r = Res("wA")
        rope_r = Res("rope")
        uglu_r = Res("u_glu")
        xt_r, rb_r, tmpA_r = Res("xt"), Res("rb"), Res("tmpA")
        xb_r = [Res("xb0"), Res("xb1")]
        sq_r = [Res("sq%d" % i) for i in range(8)]
        cqf_r, ckvf_r, rq_r, rkv_r, cqn_r, ckvn_r = Res("cqf"), Res("ckvf"), Res("rq"), Res("rkv"), Res("cqn"), Res("ckvn")
        sgb_r = [Res("sg0"), Res("sg1")]
        t1_r = [Res("t10"), Res("t11")]
        t2_r = [Res("t20"), Res("t21")]
        Qt_r, Kt_r, krb_r, Vt_r = Res("Qt"), Res("Kt"), Res("krb"), Res("Vt")
        t3_r = Res("t3")

        ws = C.dsem("wA")
        C.dma(xt, xin[:, :, 0:TT], writes=[xt_r], sem=C.dsem("xt"))
        C.dma(Win.rearrange("p a b -> p (a b)"), Win_d[l, :, :], writes=[wA_r], sem=ws)
        C.dma(Wuq.rearrange("p a b -> p (a b)"), Wuq_d[l, :, :], writes=[wA_r], sem=ws)
        C.dma(Wk.rearrange("p a b -> p (a b)"), Wk_d[l, :, :], writes=[wA_r], sem=ws)
        C.dma(Wv.rearrange("p a b -> p (a b)"), Wv_d[l, :, :], writes=[wA_r], sem=ws)
        wsr = C.dsem("wAr")
        for qd in (2, 0, 1, 3):
            C.dma(cosT[32 * qd:32 * qd + 32, :], rope_d[1, :, :], writes=[rope_r], sem=wsr)
            C.dma(sinT[32 * qd:32 * qd + 32, :], rope_d[0, :, :], writes=[rope_r], sem=wsr)
        C.op(pool, lambda: nc.gpsimd.memset(u_glu[:, :, 0:15], 0.0), writes=[uglu_r])
        C.op(pool, lambda: nc.gpsimd.memset(u_glu[:, :, S + 15:S + 30], 0.0), writes=[uglu_r])
        C.op(pool, lambda: nc.gpsimd.memset(Vt[:, :, :, 64:65], 1.0), writes=[Vt_r])

        bk = [0]

        def nb():
            b = bk[0] % 8
            bk[0] += 1
            return b

        def loadA(i):
            C.dma(xt, xin[:, :, i * TT:(i + 1) * TT], writes=[xt_r], sem=C.dsem("xt"))

        def prepA_sq(i):
            E_ = dve if i == 0 else pool
            for c in range(8):
                tt(E_, sq8[:, c, :], xt[:, c, :], xt[:, c, :], ALU.mult, reads=[xt_r], writes=[sq_r[c]])

        def prepA(i):
            t0 = i * TT
            kb = i % 2
            bs = nb()
            for c in range(8):
                mm(banks[bs][:, :], ones[:, :], sq8[:, c, :], c == 0, c == 7, [sq_r[c], const_r], [bank_res[bs]], sig=True)
            rstd_from(banks[bs][:, :], D, rb, tmpA, [bank_res[bs]], [tmpA_r], [rb_r])
            tt(dve, xbb[kb], xt, rb.unsqueeze(1).to_broadcast([128, 8, TT]), ALU.mult,
               reads=[xt_r, rb_r], writes=[xb_r[kb]])
            if i + 1 < NT:
                loadA(i + 1)

        prepA_sq(0)
        prepA(0)
        for i in range(NT):
            t0 = i * TT
            kb = i % 2
            xb = xbb[kb]

            def proj(col0, m, b):
                for kc in range(8):
                    mm(banks[b][0:m, :], Win[:, kc, col0:col0 + m], xb[:, kc, :], kc == 0, kc == 7,
                       [wA_r, xb_r[kb]], [bank_res[b]])

            if i + 1 < NT:
                prepA_sq(i + 1)
            sl_ap = [sgb[0].bitcast(BF16)[:, 0:TT], sgb[0].bitcast(BF16)[:, TT:2 * TT], sgb[1].bitcast(BF16)[:, 0:TT],
                     sgb[1].bitcast(BF16)[:, TT:2 * TT], t1b[1].bitcast(BF16)[:, 0:TT]]
            sl_r = [sgb_r[0], sgb_r[0], sgb_r[1], sgb_r[1], t1_r[1]]
            for j in range(3):
                b = nb()
                proj(128 * j, 128, b)
                cp(act, cqf[:, j, :], banks[b][:, :], reads=[bank_res[b]], writes=[cqf_r])
            for j in range(3):
                tt(dve, sl_ap[j], cqf[:, j, :], cqf[:, j, :], ALU.mult, reads=[cqf_r], writes=[sl_r[j]])
            for j in range(2):
                b = nb()
                proj(384 + 128 * j, 128, b)
                cp(act, ckvf[:, j, :], banks[b][:, :], reads=[bank_res[b]], writes=[ckvf_r])
            for j in range(2):
                tt(dve, sl_ap[3 + j], ckvf[:, j, :], ckvf[:, j, :], ALU.mult, reads=[ckvf_r], writes=[sl_r[3 + j]])
            bs = nb()
            for j in range(3):
                mm(banks[bs][:, :], ones[:, :], sl_ap[j], j == 0, j == 2, [sl_r[j], const_r], [bank_res[bs]], sig=True)
            rstd_from(banks[bs][:, :], 384, rq, tmpA, [bank_res[bs]], [tmpA_r], [rq_r])
            tt(dve, cqn, cqf, rq.unsqueeze(1).to_broadcast([128, 3, TT]), ALU.mult, reads=[cqf_r, rq_r], writes=[cqn_r])
            bA = nb()
            proj(576, 96, bA)
            bB = nb()
            proj(1696, 96, bB)
            tt(dve, t1b[0][R, :], banks[bA][R, :], cosT[R, t0:t0 + TT], ALU.mult, reads=[bank_res[bA], rope_r], writes=[t1_r[0]])
            tt(dve, t2b[0][R, :], banks[bB][R, :], sinT[R, t0:t0 + TT], ALU.mult, reads=[bank_res[bB], rope_r], writes=[t2_r[0]])
            tt(dve, krb[R, :], t1b[0][R, :], t2b[0][R, :], ALU.add, reads=[t1_r[0], t2_r[0]], writes=[krb_r])
            bs = nb()
            for j in range(2):
                mm(banks[bs][:, :], ones[:, :], sl_ap[3 + j], j == 0, j == 1, [sl_r[3 + j], const_r], [bank_res[bs]], sig=True)
            rstd_from(banks[bs][:, :], 256, rkv, tmpA, [bank_res[bs]], [tmpA_r], [rkv_r])
            tt(dve, ckvn, ckvf, rkv.unsqueeze(1).to_broadcast([128, 2, TT]), ALU.mult, reads=[ckvf_r, rkv_r], writes=[ckvn_r])
            if i + 1 < NT:
                prepA(i + 1)
            for j in range(4):
                k = j % 2
                ba = nb()
                proj(672 + 128 * j, 128, ba)
                bg = nb()
                proj(1184 + 128 * j, 128, bg)
                actf(sgb[k], banks[bg][:, :], AF.Sigmoid, reads=[bank_res[bg]], writes=[sgb_r[k]])
                tt(dve, u_glu[:, j, 15 + t0:15 + t0 + TT], banks[ba][:, :], sgb[k], ALU.mult,
                   reads=[bank_res[ba], sgb_r[k]], writes=[uglu_r])
            for hp in range(4):
                b = nb()
                for kc in range(3):
                    mm(banks[b][:, :], Wuq[:, kc, 128 * hp:128 * hp + 128], cqn[:, kc, :], kc == 0, kc == 2,
                       [wA_r, cqn_r], [bank_res[b]])
                cp(act, Qt[:, hp, :], banks[b][:, :], reads=[bank_res[b]], writes=[Qt_r])
            for hq in range(2):
                bA = nb()
                for kc in range(3):
                    mm(banks[bA][:, :], Wuq[:, kc, 512 + 128 * hq:512 + 128 * hq + 128], cqn[:, kc, :], kc == 0, kc == 2,
                       [wA_r, cqn_r], [bank_res[bA]])
                bB = nb()
                for kc in range(3):
                    mm(banks[bB][:, :], Wuq[:, kc, 768 + 128 * hq:768 + 128 * hq + 128], cqn[:, kc, :], kc == 0, kc == 2,
                       [wA_r, cqn_r], [bank_res[bB]])
                tt(dve, t1b[hq], banks[bA][:, :], cosT[:, t0:t0 + TT], ALU.mult, reads=[bank_res[bA], rope_r], writes=[t1_r[hq]])
                tt(dve, t2b[hq], banks[bB][:, :], sinT[:, t0:t0 + TT], ALU.mult, reads=[bank_res[bB], rope_r], writes=[t2_r[hq]])
                tt(dve, t3[:, hq, :], t1b[hq], t2b[hq], ALU.add, reads=[t1_r[hq], t2_r[hq]], writes=[t3_r])
            for hp in range(4):
                b = nb()
                for kc in range(2):
                    mm(banks[b][:, :], Wk[:, kc, 128 * hp:128 * hp + 128], ckvn[:, kc, :], kc == 0, kc == 1,
                       [wA_r, ckvn_r], [bank_res[b]])
                E = act if hp % 2 == 0 else dve
                cp(E, Kt[:, hp, :], banks[b][:, :], reads=[bank_res[b]], writes=[Kt_r])
            for c in range(4):
                b = nb()
                for kc in range(2):
                    mm(banks[b][:, :], ckvn[:, kc, c * 128:(c + 1) * 128], Wv[:, kc, :], kc == 0, kc == 1,
                       [wA_r, ckvn_r], [bank_res[b]])
                E = act if c % 2 == 0 else dve
                cp(E, Vt[:, :, c, 0:64], banks[b][:, :].rearrange("p (h d) -> p h d", d=64),
                   reads=[bank_res[b]], writes=[Vt_r])
            for m in range(2):
                C.dma(Qd[:, 0:64, t0:t0 + TT].rearrange("(hp m) p t -> m p hp t", m=2)[m],
                      Qt[64 * m:64 * m + 64, 0:4, :], reads=[Qt_r], sem=C.dsem("Qst"))
            for hq in range(2):
                for jq in range(4):
                    C.dma(Qd[4 * hq + jq, 64:96, t0:t0 + TT], t3[32 * jq:32 * jq + 32, hq, :], reads=[t3_r], sem=C.dsem("Qrst"))
            for m in range(2):
                C.dma(Kd[:, 0:64, t0:t0 + TT].rearrange("(hp m) p t -> m p hp t", m=2)[m],
                      Kt[64 * m:64 * m + 64, 0:4, :], reads=[Kt_r], sem=C.dsem("Kst"))
            C.dma(Kd[:, 64:96, t0:t0 + TT].rearrange("h p t -> p h t"),
                  krb[R, :].unsqueeze(1).to_broadcast([32, 8, TT]), reads=[krb_r], sem=C.dsem("krst"))
            C.dma(Vd[:, :, 4 * i * 65:(4 * i + 4) * 65].rearrange("h p x -> p h x"),
                  Vt.rearrange("p h c d -> p h (c d)"), reads=[Vt_r], sem=C.dsem("Vst"))
        C.barrier()

        TAIL_Q0 = ARW - 5136
        TAIL_WO = TAIL_Q0 - 4096 - 128
        Qh0 = arena.alloc_at(TAIL_Q0, [S], BF16)
        Kh0 = arena.alloc_at(TAIL_Q0 + 2048, [S], BF16)
        Vh0 = arena.alloc_at(TAIL_Q0 + 4096, [32 * 65], BF16)
        Wo = arena.alloc_at(TAIL_WO, [8, D], BF16)
        Qh_r = [Res("hd0"), Res("hd1")]
        Kh_r = Qh_r
        Vh_r = Qh_r
        Wo_r = Res("Wo")
        sm0 = C.dsem("hd0")
        C.dma(Qh0[0:96, :], Qd[0, :, :], writes=[Qh_r[0]], sem=sm0)
        C.dma(Kh0[0:96, :], Kd[0, :, :], writes=[Kh_r[0]], sem=sm0)
        C.dma(Vh0[:, :], Vd[0, :, :], writes=[Vh_r[0]], sem=sm0)

        arena.reset(offA)
        u_final = arena.alloc([4, S], BF16)
        y_attn = arena.alloc([4, S], BF16)
        offBC = arena.off
        Dg = arena.alloc([4, 31, 128], BF16)
        vf = arena.alloc([4, TT], F32)
        meanb = arena.alloc([TT], F32)
        m2b = arena.alloc([TT], F32)
        varb = arena.alloc([TT], F32)
        rstdb = arena.alloc([TT], F32)
        tmpB = arena.alloc([TT], F32)
        ddb = [arena.alloc([TT], F32) for _ in range(2)]
        nnb = [arena.alloc([TT], F32) for _ in range(4)]
        Dg_r, vf_r = Res("Dg"), Res("vf")
        Dg_cc_r = [Res("Dg%d" % i) for i in range(4)]
        vb_r = [Res("vb0"), Res("vb1")]
        vs_r = [Res("vs0"), Res("vs1")]
        mean_r, m2_r, var_r, rstdb_r, tmpB_r = Res("mean"), Res("m2"), Res("var"), Res("rstdb"), Res("tmpB")
        dd_r = [Res("dd0"), Res("dd1")]
        nn_r = [Res("nn%d" % i) for i in range(4)]
        ufin_r, yattn_r = Res("u_final"), Res("y_attn")
        for cc in range(4):
            col = pb + P_WCONV + cc * 31
            tt(dve, Dg[:, cc, :, :], ident[:, :].unsqueeze(1).to_broadcast([128, 31, 128]),
               prm[:, col:col + 31].unsqueeze(2).to_broadcast([128, 31, 128]), ALU.mult,
               reads=[const_r, prm_r], writes=[Dg_cc_r[cc]])
        vbq = [arena.alloc([TT], BF16) for _ in range(4)]
        vsq = [arena.alloc([TT], BF16) for _ in range(4)]
        vf2 = arena.alloc([4, TT], F32)
        vfs = [vf, vf2]
        vfs_r = [vf_r, Res("vf2")]
        vbq_r = [Res("vbq%d" % i) for i in range(4)]
        vsq_r = [Res("vsq%d" % i) for i in range(4)]
        assert arena.off <= TAIL_Q0, "phase B arena collides with head slot 0"
        stat_banks = {}
        cbk = [0]

        def conv_unit(i, cc):
            t0 = i * TT
            kv = i % 2
            if cc == 0:
                stat_banks[i] = (4, 5) if i % 2 == 0 else (6, 7)
            b = cbk[0] % 4
            cbk[0] += 1
            for k in range(31):
                mm(banks[b], Dg[:, cc, k, :], u_glu[:, cc, t0 + k:t0 + k + TT], k == 0, k == 30,
                   [Dg_cc_r[cc], uglu_r], [bank_res[b]])
            actf(vfs[kv][:, cc, :], banks[b], AF.Identity, bias=prm[:, pb + P_BCONV + cc:pb + P_BCONV + cc + 1],
                 reads=[bank_res[b], prm_r], writes=[vfs_r[kv]])
            cp(act, vbq[cc], vfs[kv][:, cc, :], reads=[vfs_r[kv]], writes=[vbq_r[cc]])
            tt(dve, vsq[cc], vfs[kv][:, cc, :], vfs[kv][:, cc, :], ALU.mult, reads=[vfs_r[kv]], writes=[vsq_r[cc]])

        def stat_unit(i, cc):
            bs1, bs2 = stat_banks[i]
            mm(banks[bs1], ones[:, :], vbq[cc], cc == 0, cc == 3, [vbq_r[cc], const_r], [bank_res[bs1]], sig=True)
            mm(banks[bs2], ones[:, :], vsq[cc], cc == 0, cc == 3, [vsq_r[cc], const_r], [bank_res[bs2]], sig=True)

        def ln_finish(i):
            t0 = i * TT
            kv = i % 2
            bs1, bs2 = stat_banks.pop(i)
            ts(dve, meanb, banks[bs1], 1.0 / 512, None, ALU.mult, reads=[bank_res[bs1]], writes=[mean_r])
            tt(dve, m2b, meanb, meanb, ALU.mult, reads=[mean_r], writes=[m2_r])
            stt(dve, varb, banks[bs2], 1.0 / 512, m2b, ALU.mult, ALU.subtract, reads=[bank_res[bs2], m2_r], writes=[var_r])
            actf(tmpB, varb, AF.Ln, bias=eps_ap, reads=[var_r, const_r], writes=[tmpB_r])
            actf(rstdb, tmpB, AF.Exp, scale=-0.5, reads=[tmpB_r], writes=[rstdb_r])
            for cc in range(4):
                k2 = cc % 2
                tt(dve, ddb[k2], vfs[kv][:, cc, :], meanb, ALU.subtract, reads=[vfs_r[kv], mean_r], writes=[dd_r[k2]])
                tt(dve, nnb[cc], ddb[k2], rstdb, ALU.mult, reads=[dd_r[k2], rstdb_r], writes=[nn_r[cc]])

        def ln_silu(i):
            t0 = i * TT
            for cc in range(4):
                actf(u_final[:, cc, t0:t0 + TT], nnb[cc], AF.Silu,
                     bias=prm[:, pb + P_BLN + cc:pb + P_BLN + cc + 1],
                     scale=prm[:, pb + P_GLN + cc:pb + P_GLN + cc + 1],
                     reads=[nn_r[cc], prm_r], writes=[ufin_r])

        units = [(i, cc) for i in range(NT) for cc in range(4)]
        prev = None
        pend_silu = None
        for (i, cc) in units:
            conv_unit(i, cc)
            if pend_silu is not None and cc == 1:
                ln_silu(pend_silu)
                pend_silu = None
            if prev is not None:
                stat_unit(*prev)
                if prev[1] == 3:
                    ln_finish(prev[0])
                    pend_silu = prev[0]
            prev = (i, cc)
        stat_unit(*prev)
        ln_finish(prev[0])
        ln_silu(prev[0])
        C.barrier()
        if debug:
            C.dma(dbg_u.rearrange("(c p) t -> p c t", p=128), u_final, reads=[ufin_r], sem=C.dsem("dbg"))

        arena.reset(offBC)
        Qh = [Qh0, arena.alloc([S], BF16)]
        Kh = [Kh0, arena.alloc([S], BF16)]
        Vh = [Vh0, arena.alloc([32 * 65], BF16)]
        PT = [arena.alloc([1024], BF16) for _ in range(3)]
        Osb = [arena.alloc([TT], F32) for _ in range(2)]
        recb = [arena.alloc([TT], F32) for _ in range(2)]
        PT_r = [Res("PT%d" % i) for i in range(3)]
        Osb_r = [Res("Osb0"), Res("Osb1")]
        rec_r = [Res("rec0"), Res("rec1")]
        hl_r = [Res("hl0"), Res("hl1")]
        pair_ps = [ps_all[:, p * 1024:(p + 1) * 1024] for p in range(3)]
        pair_r = [Res("pair%d" % p) for p in range(3)]
        ob_r = {6: bank_res[6], 7: bank_res[7]}

        def load_head(h):
            sl = h % 2
            sm = C.dsem("hd%d" % sl)
            C.dma(Qh[sl][0:96, :], Qd[h, :, :], writes=[Qh_r[sl]], sem=sm)
            C.dma(Kh[sl][0:96, :], Kd[h, :, :], writes=[Kh_r[sl]], sem=sm)
            C.dma(Vh[sl][:, :], Vd[h, :, :], writes=[Vh_r[sl]], sem=sm)

        cvb = Conv("b%d" % l, 2, [dve])
        if l == 0:
            cvb.add(bg_queue[0:bg_split])
        elif l == 1:
            cvb.add(bg_queue[bg_split:])
        pc = [0]
        NIT = NH * 8
        assert arena.off <= TAIL_WO, ("attention arena collides with tail buffers", arena.off, TAIL_WO)
        pair_of = {}

        def geom(n):
            h, qb = n // 8, n % 8
            return h, h % 2, qb * TT, 6 + (n % 2), n % 2

        def S_unit(n, u):
            h, sl, q0, ob, ko = geom(n)
            p = pc[0] % 3
            pc[0] += 1
            pair_of[(n, u)] = p
            for hf in range(2):
                kc = 2 * u + hf
                mm(pair_ps[p][:, hf * 512:(hf + 1) * 512], Kh[sl][0:96, kc * 128:(kc + 1) * 128],
                   Qh[sl][0:96, q0:q0 + TT], True, True, [Kh_r[sl], Qh_r[sl]], [pair_r[p]], sig=(hf == 1))

        def E_unit(n, u):
            p = pair_of[(n, u)]
            actf(PT[p], pair_ps[p], AF.Exp, scale=SCALE, reads=[pair_r[p]], writes=[PT_r[p]])

        def PV_unit(n, u):
            h, sl, q0, ob, ko = geom(n)
            p = pair_of.pop((n, u))
            for hf in range(2):
                kc = 2 * u + hf
                mm(banks[ob][0:65, :], Vh[sl][:, kc * 65:(kc + 1) * 65], PT[p][:, hf * 512:(hf + 1) * 512],
                   kc == 0, kc == 31, [Vh_r[sl], PT_r[p]], [ob_r[ob]])

        def tail1(n):
            h, sl, q0, ob, ko = geom(n)
            cp(dve, Osb[ko][0:65, :], banks[ob][0:65, :], reads=[ob_r[ob]], writes=[Osb_r[ko]])

        def tail2(n):
            h, sl, q0, ob, ko = geom(n)
            rb16 = recb[ko].bitcast(BF16)
            dhi, dlo = rb16[64:65, 0:TT], rb16[64:65, TT:2 * TT]
            cp(dve, dhi, Osb[ko][64:65, :], reads=[Osb_r[ko]], writes=[hl_r[ko]])
            tt(dve, dlo, Osb[ko][64:65, :], dhi, ALU.subtract, reads=[Osb_r[ko], hl_r[ko]], writes=[hl_r[ko]])
            mm(banks[ob][0:64, :], ones[64:65, 0:64], dhi, True, False, [const_r, hl_r[ko]], [ob_r[ob]])
            mm(banks[ob][0:64, :], ones[64:65, 0:64], dlo, False, True, [const_r, hl_r[ko]], [ob_r[ob]])
            C.op(dve, lambda: nc.vector.reciprocal(out=recb[ko][0:64, :], in_=banks[ob][0:64, :]),
                 [ob_r[ob]], [rec_r[ko]])
            po = (h % 2) * 64
            tt(dve, y_attn[po:po + 64, h // 2, q0:q0 + TT], Osb[ko][0:64, :], recb[ko][0:64, :], ALU.mult,
               reads=[Osb_r[ko], rec_r[ko]], writes=[yattn_r])

        G = [(n, u) for n in range(NIT) for u in range(16)]
        deferred = None
        S_unit(*G[0])
        S_unit(*G[1])
        for idx, (n, u) in enumerate(G):
            if u == 0 and n % 8 == 0 and n // 8 + 1 < NH:
                load_head(n // 8 + 1)
            if idx + 2 < len(G):
                S_unit(*G[idx + 2])
            E_unit(n, u)
            if deferred is not None and u == 0:
                tail1(deferred)
            if deferred is not None and u == 2:
                tail2(deferred)
                deferred = None
            PV_unit(n, u)
            if u == 8:
                cvb.step()
            if u == 0 and n == 16:
                C._wait(sp, {d_: d_.v for d_ in (C.dsem("b%dobf0" % l), C.dsem("b%dobf1" % l)) if d_.v > 0})
                C.dma(Wo.rearrange("p a b -> p (a b)"), Wo_d[l, :, :], writes=[Wo_r], sem=C.dsem("wD"))
            if u == 15:
                deferred = n
        tail1(deferred)
        tail2(deferred)
        cvb.flush()
        C.barrier()
        if debug:
            C.dma(dbg_y.rearrange("(c p) t -> p c t", p=128), y_attn, reads=[yattn_r], sem=C.dsem("dbg"))

        arena.reset(offBC)
        xd = [arena.alloc([8, TT], F32) for _ in range(2)]
        xd_r = [Res("xd0"), Res("xd1")]
        arena.reset()
        blk_pairs = [(0, 2), (2, 8), (8, 15), (15, 22)]
        Wup_b = [arena.alloc([8, 2 * 128 * (jb_ - ja_)], BF16) for (ja_, jb_) in blk_pairs]
        assert arena.off == 2 * 8 * DFF // 2, arena.off
        Wup_blk_r = [Res("Wup%d" % bi) for bi in range(4)]
        Wup_dv = Wup_d[l, :, :].rearrange("p (k n) -> p k n", k=8)

        def load_wup_block(bi):
            ja_, jb_ = blk_pairs[bi]
            ncb = 128 * (jb_ - ja_)
            for hf, base in enumerate((0, DFF)):
                C.dma(Wup_b[bi][:, :, hf * ncb:(hf + 1) * ncb], Wup_dv[:, :, base + 128 * ja_:base + 128 * jb_],
                      writes=[Wup_blk_r[bi]], sem=C.dsem("wE%d" % bi))

        def wup_cols(j, half):
            for bi, (ja_, jb_) in enumerate(blk_pairs):
                if ja_ <= j < jb_:
                    ncb = 128 * (jb_ - ja_)
                    o = half * ncb + 128 * (j - ja_)
                    return Wup_b[bi], o, Wup_blk_r[bi]
        Wdn = arena.alloc([NJ, D], BF16)
        xe = arena.alloc([8, TT], F32)
        h2 = arena.alloc([8, TT], BF16)
        rbe = arena.alloc([TT], F32)
        tmpE = arena.alloc([TT], F32)
        gE = arena.alloc([NJ, TT], BF16)
        tgb = [arena.alloc([TT], F32) for _ in range(2)]
        tvb = [arena.alloc([TT], F32) for _ in range(2)]
        sge = [arena.alloc([TT], F32) for _ in range(2)]
        xres = [arena.alloc([TT], F32) for _ in range(2)]
        Wup_r, Wdn_r, xe_r, h2_r, rbe_r, tmpE_r, gE_r = Res("Wup"), Res("Wdn"), Res("xe"), Res("h2"), Res("rbe"), Res("tmpE"), Res("gE")
        gE_rj = [Res("gE%d" % j_) for j_ in range(NJ)]
        tg_r = [Res("tg0"), Res("tg1")]
        tv_r = [Res("tv0"), Res("tv1")]
        sge_r = [Res("sge0"), Res("sge1")]
        xres_r = [Res("xres0"), Res("xres1")]
        X2v = X2.rearrange("(c p) t -> p c t", p=128)
        NTE = 9

        def tile_geom(i):
            c0 = 510 * i
            W = min(512, S + 2 - c0)
            return c0, W, W - 2

        sq_bufs = [tgb[0], tgb[1], tvb[0], tvb[1]]
        sq_res = [tg_r[0], tg_r[1], tv_r[0], tv_r[1]]

        def sq_slot(c, W):
            v = sq_bufs[c // 2].bitcast(BF16)
            o = (c % 2) * TT
            return v[:, o:o + W], sq_res[c // 2]

        def prep_load(i):
            c0, W, Wo_ = tile_geom(i)
            C.dma(xe[:, :, 0:W], X1v[:, :, c0:c0 + W], writes=[xe_r], sem=C.dsem("xe"))

        def prep_sq(i):
            c0, W, Wo_ = tile_geom(i)
            for c in range(8):
                ap_, r_ = sq_slot(c, W)
                tt(dve, ap_, xe[:, c, 0:W], xe[:, c, 0:W], ALU.mult, reads=[xe_r], writes=[r_])

        def prep_stats(i):
            c0, W, Wo_ = tile_geom(i)
            bs = nb()
            for c in range(8):
                ap_, r_ = sq_slot(c, W)
                mm(banks[bs][:, 0:W], ones[:, :], ap_, c == 0, c == 7, [r_, const_r], [bank_res[bs]], sig=True)
            rstd_from(banks[bs][:, 0:W], D, rbe[:, 0:W], tmpE[:, 0:W], [bank_res[bs]], [tmpE_r], [rbe_r])
            tt(dve, h2[:, :, 0:W], xe[:, :, 0:W], rbe[:, 0:W].unsqueeze(1).to_broadcast([128, 8, W]), ALU.mult,
               reads=[xe_r, rbe_r], writes=[h2_r])

        assert 33792 >= offBC + 2 * 4096, "FFN prologue buffers overlap phase D"
        C.dma(xd[0], xin[:, :, 0:TT], writes=[xd_r[0]], sem=C.dsem("xd0"))
        for i in range(NT):
            t0 = i * TT
            k = i % 2
            if i + 1 < NT:
                C.dma(xd[1 - k], xin[:, :, t0 + TT:t0 + 2 * TT], writes=[xd_r[1 - k]], sem=C.dsem("xd%d" % (1 - k)))
            for oc in range(8):
                b = nb()
                for kc in range(8):
                    rhs = y_attn[:, kc, t0:t0 + TT] if kc < 4 else u_final[:, kc - 4, t0:t0 + TT]
                    mm(banks[b], Wo[:, kc, oc * 128:(oc + 1) * 128], rhs, kc == 0, kc == 7,
                       [Wo_r, yattn_r, ufin_r], [bank_res[b]])
                tt(dve, xd[k][:, oc, :], xd[k][:, oc, :], banks[b], ALU.add, reads=[xd_r[k], bank_res[b]], writes=[xd_r[k]])
            C.dma(X1v[:, :, 1 + t0:1 + t0 + TT], xd[k], reads=[xd_r[k]], sem=C.dsem("xd%d" % k))
            if i == 0:
                st0_ev = (C.dsem("xd0"), C.dsem("xd0").v)
            if i == 2:
                C._wait(sp, {st0_ev[0]: st0_ev[1], C.dsem("pad"): C.dsem("pad").v})
                prep_load(0)
            if i == 4:
                prep_sq(0)
                assert 2 * 8 * 128 * 8 // 2 <= offA, "prefetched FFN weight blocks must stay inside the dead u_glu region"
            if i == 5:
                load_wup_block(0)
            if i == 6:
                prep_stats(0)
                load_wup_block(1)
        C.barrier()

        load_wup_block(2)
        load_wup_block(3)
        for q in range(2):
            C.dma(Wdn[:, q * 11:(q + 1) * 11, :].rearrange("p a b -> p (a b)"),
                  Wdn_d[l, :, q * 11 * D:(q + 1) * 11 * D], writes=[Wdn_r], sem=C.dsem("wE2"))
        prep_load(1)

        def down(i, oc):
            c0, W, Wo_ = tile_geom(i)
            k2 = oc % 2
            C.dma(xres[k2][:, 0:Wo_], X1v[:, oc, c0 + 1:c0 + 1 + Wo_], writes=[xres_r[k2]], sem=C.dsem("xr%d" % k2))
            b = nb()
            for j in range(NJ):
                mm(banks[b][:, 0:Wo_], Wdn[:, j, oc * 128:(oc + 1) * 128], gE[:, j, 0:Wo_], j == 0, j == NJ - 1,
                   [Wdn_r, gE_rj[j]], [bank_res[b]])
            tt(dve, xres[k2][:, 0:Wo_], xres[k2][:, 0:Wo_], banks[b][:, 0:Wo_], ALU.add,
               reads=[xres_r[k2], bank_res[b]], writes=[xres_r[k2]])
            C.dma(X2v[:, oc, c0:c0 + Wo_], xres[k2][:, 0:Wo_], reads=[xres_r[k2]], sem=C.dsem("xr%d" % k2))

        outv = outT.rearrange("(c p) t -> p c t", p=128)
        fsq = [xres[0].bitcast(BF16), xres[1].bitcast(BF16)]
        fsq_r = [Res("fsq0"), Res("fsq1")]
        rbF_r, tmpF2_r = Res("rbF"), Res("tmpF2")

        dl_bank = {}

        def dl_mm(i, oc, b=None):
            c0, W, Wo_ = tile_geom(i)
            if b is None:
                b = nb()
            dl_bank[oc] = b
            for j in range(NJ):
                mm(banks[b][:, 0:Wo_], Wdn[:, j, oc * 128:(oc + 1) * 128], gE[:, j, 0:Wo_], j == 0, j == NJ - 1,
                   [Wdn_r, gE_rj[j]], [bank_res[b]])

        def dl_res_loads(i):
            c0, W, Wo_ = tile_geom(i)
            for oc in range(8):
                C.dma(xe[:, oc, 0:Wo_], X1v[:, oc, c0 + 1:c0 + 1 + Wo_], writes=[xe_r], sem=C.dsem("xeo"))

        def dl_add(i, oc):
            c0, W, Wo_ = tile_geom(i)
            b = dl_bank[oc]
            tt(dve, xe[:, oc, 0:Wo_], xe[:, oc, 0:Wo_], banks[b][:, 0:Wo_], ALU.add,
               reads=[xe_r, bank_res[b]], writes=[xe_r])
            k2 = oc % 2
            actf(fsq[k2][:, 0:Wo_], xe[:, oc, 0:Wo_], AF.Square, reads=[xe_r], writes=[fsq_r[k2]])
            mm(banks[7][:, 0:Wo_], ones[:, :], fsq[k2][:, 0:Wo_], oc == 0, oc == 7, [fsq_r[k2], const_r], [bank_res[7]],
               sig=True)

        def dl_finish(i):
            c0, W, Wo_ = tile_geom(i)
            rstd_from(banks[7][:, 0:Wo_], D, rbe[:, 0:Wo_], tmpE[:, 0:Wo_], [bank_res[7]], [tmpE_r], [rbe_r])
            for oc in range(8):
                stt(dve, xe[:, oc, 0:Wo_], xe[:, oc, 0:Wo_], prm[:, P_GFINAL + oc:P_GFINAL + oc + 1], rbe[:, 0:Wo_],
                    ALU.mult, ALU.mult, reads=[xe_r, rbe_r, prm_r], writes=[xe_r])
            C.dma(outv[:, :, c0:c0 + Wo_], xe[:, :, 0:Wo_], reads=[xe_r], sem=C.dsem("xeo"))

        if last and final:
            def nb():
                b = bk[0] % 7
                bk[0] += 1
                return b

        pend_gate = [None]
        for i in range(NTE):
            c0, W, Wo_ = tile_geom(i)
            for j in range(NJ):
                k = j % 2
                wg_t, wg_o, wr = wup_cols(j, 0)
                wv_t, wv_o, _ = wup_cols(j, 1)
                bG = nb()
                for kc in range(8):
                    mm(banks[bG][:, 0:W], wg_t[:, kc, wg_o:wg_o + 128], h2[:, kc, 0:W], kc == 0, kc == 7,
                       [wr, h2_r], [bank_res[bG]])
                bV = nb()
                for kc in range(8):
                    mm(banks[bV][:, 0:W], wv_t[:, kc, wv_o:wv_o + 128], h2[:, kc, 0:W], kc == 0, kc == 7,
                       [wr, h2_r], [bank_res[bV]])
                cg = pb + P_WFFN + j * 3
                cv = pb + P_WFFN + (NJ + j) * 3
                bgc = pb + P_BFFN + j
                bvc = pb + P_BFFN + NJ + j
                tg, tv, sg = tgb[k][:, 0:Wo_], tvb[k][:, 0:Wo_], sge[k][:, 0:Wo_]
                actf(tg, banks[bG][:, 0:Wo_], AF.Identity, scale=prm[:, cg:cg + 1], reads=[bank_res[bG], prm_r], writes=[tg_r[k]])
                actf(tv, banks[bV][:, 0:Wo_], AF.Identity, scale=prm[:, cv:cv + 1], bias=prm[:, bvc:bvc + 1],
                     reads=[bank_res[bV], prm_r], writes=[tv_r[k]])
                if pend_gate[0] is not None:
                    pend_gate[0]()
                stt(dve, tg, banks[bG][:, 1:Wo_ + 1], prm[:, cg + 1:cg + 2], tg, ALU.mult, ALU.add,
                    reads=[bank_res[bG], tg_r[k], prm_r], writes=[tg_r[k]])
                stt(dve, tv, banks[bV][:, 1:Wo_ + 1], prm[:, cv + 1:cv + 2], tv, ALU.mult, ALU.add,
                    reads=[bank_res[bV], tv_r[k], prm_r], writes=[tv_r[k]])
                stt(dve, tg, banks[bG][:, 2:Wo_ + 2], prm[:, cg + 2:cg + 3], tg, ALU.mult, ALU.add,
                    reads=[bank_res[bG], tg_r[k], prm_r], writes=[tg_r[k]])
                stt(dve, tv, banks[bV][:, 2:Wo_ + 2], prm[:, cv + 2:cv + 3], tv, ALU.mult, ALU.add,
                    reads=[bank_res[bV], tv_r[k], prm_r], writes=[tv_r[k]])

                def fin(j=j, k=k, tg=tg, tv=tv, sg=sg, bgc=bgc, Wo_=Wo_, i=i):
                    if debug and i == 2 and l == 0 and j == 0:
                        C.dma(dbg_tg[:, 0:Wo_], tg, reads=[tg_r[k]], sem=C.dsem("dbg"))
                    actf(sg, tg, AF.Silu, bias=prm[:, bgc:bgc + 1], reads=[tg_r[k], prm_r], writes=[sge_r[k]])
                    tt(pool, gE[:, j, 0:Wo_], tv, sg, ALU.mult, reads=[tv_r[k], sge_r[k]], writes=[gE_rj[j]])
                pend_gate[0] = fin
            pend_gate[0]()
            pend_gate[0] = None
            if debug and i == 2 and l == 0:
                C.dma(dbg_g.rearrange("(c p) t -> p c t", p=128), gE, reads=gE_rj, sem=C.dsem("dbg"))
                C.dma(dbg_h2.rearrange("(c p) t -> p c t", p=128), h2, reads=[h2_r], sem=C.dsem("dbg"))
                C.barrier()
            if last and final:
                if i + 1 < NTE:
                    prep_sq(i + 1)
                for oc in range(3):
                    dl_mm(i, oc)
                stat_b = bk[0] % 7
                if i + 1 < NTE:
                    prep_stats(i + 1)
                else:
                    bk[0] += 1
                dl_res_loads(i)
                dl_mm(i, 3)
                dl_mm(i, 4)
                dl_mm(i, 5)
                dl_mm(i, 6, b=stat_b)
                dl_add(i, 0)
                dl_add(i, 1)
                dl_mm(i, 7, b=dl_bank[0])
                for oc in range(2, 8):
                    dl_add(i, oc)
                dl_finish(i)
                if i + 2 < NTE:
                    prep_load(i + 2)
            else:
                if i + 1 < NTE:
                    prep_sq(i + 1)
                for oc in range(4):
                    down(i, oc)
                if i + 1 < NTE:
                    prep_stats(i + 1)
                    if i + 2 < NTE:
                        prep_load(i + 2)
                for oc in range(4, 8):
                    down(i, oc)
        C.barrier()

    if final:
        return nc
    arena.reset()
    xf = [arena.alloc([8, TT], F32) for _ in range(2)]
    sqf = [arena.alloc([8, TT], BF16) for _ in range(2)]
    rbf = arena.alloc([TT], F32)
    tmpF = arena.alloc([TT], F32)
    xf_r = [Res("xf0"), Res("xf1")]
    sqf_r = [Res("sqf0"), Res("sqf1")]
    rbf_r, tmpF_r = Res("rbf"), Res("tmpF")
    X2v = X2.rearrange("(c p) t -> p c t", p=128)
    outv = outT.rearrange("(c p) t -> p c t", p=128)
    C.dma(xf[0], X2v[:, :, 0:TT], writes=[xf_r[0]], sem=C.dsem("xf0"))
    for i in range(NT):
        t0 = i * TT
        k = i % 2
        if i + 1 < NT:
            C.dma(xf[1 - k], X2v[:, :, t0 + TT:t0 + 2 * TT], writes=[xf_r[1 - k]], sem=C.dsem("xf%d" % (1 - k)))
        bs = i % 8
        for c in range(8):
            tt(dve, sqf[k][:, c, :], xf[k][:, c, :], xf[k][:, c, :], ALU.mult, reads=[xf_r[k]], writes=[sqf_r[k]])
        for c in range(8):
            mm(banks[bs], ones[:, :], sqf[k][:, c, :], c == 0, c == 7, [sqf_r[k], const_r], [bank_res[bs]], sig=True)
        rstd_from(banks[bs], D, rbf, tmpF, [bank_res[bs]], [tmpF_r], [rbf_r])
        for c in range(8):
            stt(dve, xf[k][:, c, :], xf[k][:, c, :], prm[:, P_GFINAL + c:P_GFINAL + c + 1], rbf, ALU.mult, ALU.mult,
                reads=[xf_r[k], rbf_r, prm_r], writes=[xf_r[k]])
        C.dma(outv[:, :, t0:t0 + TT], xf[k], reads=[xf_r[k]], sem=C.dsem("xf%d" % k))
    C.barrier()
    return nc


def _pack_params(inp):
    P = np.zeros((128, NP), np.float32)

    def colmajor(v, nchunk):
        return np.ascontiguousarray(v.reshape(nchunk, 128).T)

    for l in range(NL):
        pb = l * PL
        P[:, pb + P_GMIX:pb + P_GMIX + 8] = colmajor(inp["g_mix"][l], 8)
        P[:, pb + P_GQ:pb + P_GQ + 3] = colmajor(inp["g_q"][l], 3)
        P[:, pb + P_GKV:pb + P_GKV + 2] = colmajor(inp["g_kv"][l], 2)
        P[:, pb + P_GFFN:pb + P_GFFN + 8] = colmajor(inp["g_ffn"][l], 8)
        wc = inp["w_dw_conv"][l]
        P[:, pb + P_WCONV:pb + P_WCONV + 124] = wc.reshape(31, 4, 128).transpose(2, 1, 0).reshape(128, 124)
        P[:, pb + P_BCONV:pb + P_BCONV + 4] = colmajor(inp["b_dw_conv"][l], 4)
        P[:, pb + P_GLN:pb + P_GLN + 4] = colmajor(inp["g_conv_ln"][l], 4)
        P[:, pb + P_BLN:pb + P_BLN + 4] = colmajor(inp["b_conv_ln"][l], 4)
        wf = inp["w_dw_ffn"][l]
        P[:, pb + P_WFFN:pb + P_WFFN + 132] = wf.reshape(3, 44, 128).transpose(2, 1, 0).reshape(128, 132)
        P[:, pb + P_BFFN:pb + P_BFFN + 44] = colmajor(inp["b_dw_ffn"][l], 44)
    P[:, P_GFINAL:P_GFINAL + 8] = colmajor(inp["g_final"], 8)
    inv_freq = (1.0 / (np.float32(10000.0) ** (np.arange(0, 32, 2, dtype=np.float32) / np.float32(32.0)))).astype(np.float32)
    for p in range(64, 96):
        P[p, P_INVF] = inv_freq[(p - 64) % 16]
    return P


_NC_CACHE = {}


def make_in_maps(inp):
    inp = {k: np.asarray(v) for k, v in inp.items()}
    P = _pack_params(inp)
    maps = []
    for b in range(8):
        maps.append({
            "xT": np.ascontiguousarray(inp["x"][b].T),
            "pos": np.ascontiguousarray(inp["positions"][b].reshape(1, S).astype(np.int32)),
            "params": P,
            "w_in": inp["w_in"], "w_uq": inp["w_uq"], "w_ukv": inp["w_ukv"], "w_o": inp["w_o"],
            "w_up": inp["w_up"], "w_down": inp["w_down"],
        })
    return maps


def kernel(**inputs):
    if "nc" not in _NC_CACHE:
        _NC_CACHE["nc"] = build_program()
    nc = _NC_CACHE["nc"]
    maps = make_in_maps(inputs)
    res = run_bass_kernel_spmd(nc, maps, core_ids=list(range(8)))
    out = np.stack([np.ascontiguousarray(res.results[b]["outT"].T) for b in range(8)], axis=0)
    return out.astype(np.float32)
```

```python
import math
import sys
import numpy as np
import ml_dtypes
import concourse.bass as bass
import concourse.mybir as mybir
from concourse.bass_utils import run_bass_kernel_spmd

F32 = mybir.dt.float32
BF16 = mybir.dt.bfloat16
I32 = mybir.dt.int32
ALU = mybir.AluOpType
AF = mybir.ActivationFunctionType

S = 4096
D = 1024
NL = 2
NH = 8
TT = 512
NT = S // TT
DFF = 2816
NJ = DFF // 128
EPS = 1e-6
IN_COLS = 1696
WIN_W = 1792
SCALE = 1.0 / math.sqrt(96.0)

PL = 333
P_GMIX, P_GQ, P_GKV, P_GFFN, P_WCONV, P_BCONV, P_GLN, P_BLN, P_WFFN, P_BFFN = 0, 8, 11, 13, 21, 145, 149, 153, 157, 289
P_GFINAL = NL * PL
P_INVF = NL * PL + 8
NP = NL * PL + 9

MAGIC = 12582912.0
TWO_PI = 2.0 * math.pi
C1 = 6.28125
C2 = TWO_PI - C1
PI_LO = 3.1415925


PE_TAGS = []


class Sem:
    def __init__(self, nc, name):
        self.h = nc.alloc_semaphore(name=name)
        self.v = 0


class Res:
    __slots__ = ("name", "w", "r")

    def __init__(self, name):
        self.name = name
        self.w = None
        self.r = {}


class Eng:
    def __init__(self, nc, name, b, is_pe=False):
        self.name = name
        self.b = b
        self.sem = Sem(nc, "e_" + name)
        self.seen = {}
        self.is_pe = is_pe
        self.pending = []


class Ctx:
    def __init__(self, nc):
        self.nc = nc
        self.pe = Eng(nc, "pe", nc.tensor, True)
        self.act = Eng(nc, "act", nc.scalar)
        self.dve = Eng(nc, "dve", nc.vector)
        self.pool = Eng(nc, "pool", nc.gpsimd)
        self.sp = Eng(nc, "sp", nc.sync)
        self.engs = [self.pe, self.act, self.dve, self.pool, self.sp]
        self.dsems = {}

    def dsem(self, name):
        if name not in self.dsems:
            self.dsems[name] = Sem(self.nc, "d_" + name)
        return self.dsems[name]

    def _wait(self, E, deps):
        for s, v in deps.items():
            if E.seen.get(s, 0) < v:
                E.b.wait_ge(s.h, v)
                E.seen[s] = v

    def op(self, E, build, reads=(), writes=(), sig=True, dsem=None):
        deps = {}

        def add(ev, kind):
            if ev is None:
                return
            s, v = ev
            if v is None:
                if s is E.sem:
                    return
                raise RuntimeError("dependency on unsignalled instruction")
            if s is E.sem and (kind != "raw" or E.is_pe):
                return
            if dsem is not None and s is dsem and kind == "waw":
                return
            if deps.get(s, 0) < v:
                deps[s] = v

        for r in reads:
            add(r.w, "raw")
        for w in writes:
            for E2 in self.engs:
                if E2 is not E:
                    for res, kind in E2.pending:
                        if res is w:
                            raise RuntimeError("write to %s while %s has an unsignalled access" % (w.name, E2.name))
            add(w.w, "waw")
            for s, v in w.r.items():
                add((s, v), "war")
        self._wait(E, deps)
        ins = build()
        if dsem is not None:
            dsem.v += 16
            ins.then_inc(dsem.h, 16)
            ev = (dsem, dsem.v)
        elif sig:
            E.sem.v += 1
            ins.then_inc(E.sem.h, 1)
            ev = (E.sem, E.sem.v)
            for res, kind in E.pending:
                if kind == "r":
                    if res.r.get(E.sem, 0) < E.sem.v:
                        res.r[E.sem] = E.sem.v
                else:
                    if res.w is not None and res.w[0] is E.sem and res.w[1] is None:
                        res.w = ev
            E.pending = []
        else:
            ev = None
        for r in reads:
            if ev is None:
                E.pending.append((r, "r"))
            elif r.r.get(ev[0], 0) < ev[1]:
                r.r[ev[0]] = ev[1]
        for w in writes:
            if ev is None:
                w.w = (E.sem, None)
                E.pending.append((w, "w"))
            else:
                w.w = ev
            w.r = {}
        return ins

    def dma(self, out, in_, reads=(), writes=(), sem=None, E=None):
        E = E or self.sp
        return self.op(E, lambda: E.b.dma_start(out=out, in_=in_), reads, writes, dsem=sem)

    def barrier(self):
        allsems = [e.sem for e in self.engs] + list(self.dsems.values())
        for E in self.engs:
            assert not E.pending, E.name
            if E.is_pe:
                continue
            deps = {s: s.v for s in allsems if s is not E.sem and s.v > 0}
            self._wait(E, deps)


class Arena:
    def __init__(self, ap, nwords):
        self.ap = ap
        self.n = nwords
        self.off = 0

    def reset(self, off=0):
        self.off = off

    def alloc_at(self, off, free_shape, dtype):
        save = self.off
        self.off = off
        v = self.alloc(free_shape, dtype)
        self.off = save
        return v

    def alloc(self, free_shape, dtype):
        nel = 1
        for x in free_shape:
            nel *= x
        nbytes = nel * (4 if dtype in (F32, I32) else 2)
        nw = (nbytes + 3) // 4
        nw = (nw + 3) // 4 * 4
        assert self.off + nw <= self.n, ("arena overflow", self.off, nw, self.n)
        v = self.ap[:, self.off:self.off + nw]
        self.off += nw
        if dtype != F32:
            v = v.bitcast(dtype)
        v = v[:, 0:nel]
        if len(free_shape) == 2:
            v = v.rearrange("p (a b) -> p a b", a=free_shape[0])
        elif len(free_shape) == 3:
            v = v.rearrange("p (a b c) -> p a b c", a=free_shape[0], b=free_shape[1])
        return v


def build_program(n_layers=NL, final=True, debug=False):
    nc = bass.Bass("TRN2", target_bir_lowering=False)
    dk = "ExternalOutput" if debug else "Internal"
    xT = nc.dram_tensor("xT", [D, S], F32, kind="ExternalInput").ap()
    pos = nc.dram_tensor("pos", [1, S], I32, kind="ExternalInput").ap()
    params = nc.dram_tensor("params", [128, NP], F32, kind="ExternalInput").ap()
    w_in = nc.dram_tensor("w_in", [NL, D, IN_COLS], F32, kind="ExternalInput").ap()
    w_uq = nc.dram_tensor("w_uq", [NL, 384, 768], F32, kind="ExternalInput").ap()
    w_ukv = nc.dram_tensor("w_ukv", [NL, 256, 1024], F32, kind="ExternalInput").ap()
    w_o = nc.dram_tensor("w_o", [NL, D, D], F32, kind="ExternalInput").ap()
    w_up = nc.dram_tensor("w_up", [NL, D, 2 * DFF], F32, kind="ExternalInput").ap()
    w_down = nc.dram_tensor("w_down", [NL, DFF, D], F32, kind="ExternalInput").ap()
    outT = nc.dram_tensor("outT", [D, S], F32, kind="ExternalOutput").ap()

    Win_d = nc.dram_tensor("Win_d", [NL, 128, 8 * WIN_W], BF16).ap()
    Wuq_d = nc.dram_tensor("Wuq_d", [NL, 128, 3 * 1024], BF16).ap()
    Wk_d = nc.dram_tensor("Wk_d", [NL, 128, 2 * 512], BF16).ap()
    Wv_d = nc.dram_tensor("Wv_d", [NL, 128, 2 * 512], BF16).ap()
    Wo_d = nc.dram_tensor("Wo_d", [NL, 128, 8 * D], BF16).ap()
    Wup_d = nc.dram_tensor("Wup_d", [NL, 128, 8 * 2 * DFF], BF16).ap()
    Wdn_d = nc.dram_tensor("Wdn_d", [NL, 128, NJ * D], BF16).ap()
    rope_d = nc.dram_tensor("rope_d", [2, 32, S], F32, kind=dk).ap()
    Qd = nc.dram_tensor("Qd", [NH, 96, S], BF16, kind=dk).ap()
    Kd = nc.dram_tensor("Kd", [NH, 96, S], BF16, kind=dk).ap()
    Vd = nc.dram_tensor("Vd", [NH, 128, 32 * 65], BF16, kind=dk).ap()
    X1 = nc.dram_tensor("X1", [D, S + 2], F32, kind=dk).ap()
    X2 = nc.dram_tensor("X2", [D, S], F32, kind=dk).ap()
    dbg_u = nc.dram_tensor("dbg_u", [D // 2, S], BF16, kind=dk).ap()
    dbg_y = nc.dram_tensor("dbg_y", [D // 2, S], BF16, kind=dk).ap()
    dbg_h2 = nc.dram_tensor("dbg_h2", [D, TT], BF16, kind=dk).ap()
    dbg_g = nc.dram_tensor("dbg_g", [DFF, TT], BF16, kind=dk).ap()
    dbg_tg = nc.dram_tensor("dbg_tg", [128, TT], F32, kind=dk).ap()

    ARW = 51800
    arena_t = nc.alloc_sbuf_tensor("arena", [128, ARW], F32)
    prm_t = nc.alloc_sbuf_tensor("prm", [128, NP], F32)
    ident_t = nc.alloc_sbuf_tensor("ident", [128, 128], BF16)
    ones_t = nc.alloc_sbuf_tensor("ones", [128, 128], BF16)
    sel_t = nc.alloc_sbuf_tensor("sel", [128, 64], F32)
    zero_t = nc.alloc_sbuf_tensor("zero", [128, 16], F32)
    eps_t = nc.alloc_sbuf_tensor("epsc", [128, 1], F32)
    arena = Arena(arena_t.ap() if hasattr(arena_t, "ap") else arena_t, ARW)
    prm = prm_t.ap() if hasattr(prm_t, "ap") else prm_t
    ident = ident_t.ap() if hasattr(ident_t, "ap") else ident_t
    ones = ones_t.ap() if hasattr(ones_t, "ap") else ones_t
    sel = sel_t.ap() if hasattr(sel_t, "ap") else sel_t
    zero = zero_t.ap() if hasattr(zero_t, "ap") else zero_t
    eps_ap = (eps_t.ap() if hasattr(eps_t, "ap") else eps_t)[:, 0:1]
    ps_t = nc.alloc_psum_tensor("psall", [128, 4096], F32)
    ps_all = ps_t.ap() if hasattr(ps_t, "ap") else ps_t
    banks = [ps_all[:, i * 512:(i + 1) * 512] for i in range(8)]

    C = Ctx(nc)
    pe, act, dve, pool, sp = C.pe, C.act, C.dve, C.pool, C.sp
    bank_res = [Res("bank%d" % i) for i in range(8)]

    def mm(out, lhsT, rhs, start, stop, reads, writes, sig=None):
        PE_TAGS.append(sys._getframe(1).f_lineno)
        C.op(pe, lambda: nc.tensor.matmul(out, lhsT=lhsT, rhs=rhs, start=start, stop=stop),
             reads, writes, sig=(stop if sig is None else sig))

    def ts(E, out, in0, s1, s2, op0, op1=None, reads=(), writes=()):
        if op1 is None:
            return C.op(E, lambda: E.b.tensor_scalar(out=out, in0=in0, scalar1=s1, scalar2=None, op0=op0), reads, writes)
        return C.op(E, lambda: E.b.tensor_scalar(out=out, in0=in0, scalar1=s1, scalar2=s2, op0=op0, op1=op1), reads, writes)

    def tt(E, out, in0, in1, op, reads=(), writes=()):
        return C.op(E, lambda: E.b.tensor_tensor(out=out, in0=in0, in1=in1, op=op), reads, writes)

    def stt(E, out, in0, scalar, in1, op0, op1, reads=(), writes=()):
        return C.op(E, lambda: E.b.scalar_tensor_tensor(out=out, in0=in0, scalar=scalar, in1=in1, op0=op0, op1=op1), reads, writes)

    def cp(E, out, in_, reads=(), writes=()):
        if E is act:
            return C.op(E, lambda: nc.scalar.copy(out=out, in_=in_), reads, writes)
        return C.op(E, lambda: E.b.tensor_copy(out=out, in_=in_), reads, writes)

    def actf(out, in_, func, bias=None, scale=None, reads=(), writes=()):
        kw = {}
        if bias is not None:
            kw["bias"] = bias
        if scale is not None:
            kw["scale"] = scale
        return C.op(act, lambda: nc.scalar.activation(out=out, in_=in_, func=func, **kw), reads, writes)

    def rstd_from(psum_ap, n, out_ap, tmp_ap, reads, writes_tmp, writes_out, npart=128):
        actf(tmp_ap, psum_ap, AF.Ln, bias=eps_ap, scale=1.0 / n, reads=list(reads) + [const_r], writes=writes_tmp)
        actf(out_ap, tmp_ap, AF.Exp, scale=-0.5, reads=writes_tmp, writes=writes_out)

    prm_r = Res("prm")
    const_r = Res("const")
    C.dma(prm[:, :], params[:, :], writes=[prm_r], sem=C.dsem("prm"))
    C.op(pool, lambda: nc.gpsimd.memset(ident[:], 0.0), writes=[const_r])
    C.op(pool, lambda: nc.gpsimd.affine_select(out=ident[:], in_=ident[:], pattern=[[-1, 128]],
                                               compare_op=ALU.not_equal, fill=1.0, base=0,
                                               channel_multiplier=1), reads=[const_r], writes=[const_r])
    C.op(pool, lambda: nc.gpsimd.memset(ones[:], 1.0), writes=[const_r])
    C.op(pool, lambda: nc.gpsimd.memset(zero[:], 0.0), writes=[const_r])
    C.op(pool, lambda: nc.gpsimd.memset(eps_ap, EPS), writes=[const_r])
    C.op(pool, lambda: nc.gpsimd.memset(sel[:], 0.0), writes=[const_r])
    C.op(pool, lambda: nc.gpsimd.memset(sel[64:65, :], 1.0), reads=[const_r], writes=[const_r])
    x1pad_r = Res("x1pad")
    X1v = X1.rearrange("(c p) t -> p c t", p=128)
    with nc.allow_non_contiguous_dma(reason="two zero pad columns, once"):
        for col in (0, S + 1):
            C.dma(X1v[:, :, col:col + 1], zero[:, 0:8].rearrange("p (c o) -> p c o", o=1),
                  reads=[const_r], writes=[x1pad_r], sem=C.dsem("pad"))

    arena.reset()
    R = slice(64, 96)
    posi = arena.alloc([S], I32)
    ang = arena.alloc([S], F32)
    kk = arena.alloc([S], F32)
    rr = arena.alloc([S], F32)
    sn2 = [arena.alloc([S], F32) for _ in range(2)]
    rp = Res("rope_tmp")
    invf = prm[R, P_INVF:P_INVF + 1]
    C.dma(posi[R, :], pos.partition_broadcast(32), writes=[rp], sem=C.dsem("pos"))

    rr2 = [rr, arena.alloc([S], F32)]

    rope_ops = []

    def emit_rope_dve():
        R_ = rope_ops.append
        R_(lambda: cp(dve, ang[R, :], posi[R, :], reads=[rp], writes=[rp]))
        R_(lambda: ts(dve, ang[R, :], ang[R, :], invf, None, ALU.mult, reads=[rp, prm_r], writes=[rp]))
        for which in range(2):
            rw = rr2[which]
            if which == 0:
                R_(lambda: ts(dve, kk[R, :], ang[R, :], 1.0 / TWO_PI, MAGIC, ALU.mult, ALU.add, reads=[rp], writes=[rp]))
            else:
                R_(lambda: ts(dve, kk[R, :], ang[R, :], 1.0 / TWO_PI, 0.25, ALU.mult, ALU.add, reads=[rp], writes=[rp]))
                R_(lambda: ts(dve, kk[R, :], kk[R, :], MAGIC, None, ALU.add, reads=[rp], writes=[rp]))
            R_(lambda: ts(dve, kk[R, :], kk[R, :], -MAGIC, None, ALU.add, reads=[rp], writes=[rp]))
            R_(lambda rw=rw: stt(dve, rw[R, :], kk[R, :], -C1, ang[R, :], ALU.mult, ALU.add, reads=[rp], writes=[rp]))
            R_(lambda rw=rw: stt(dve, rw[R, :], kk[R, :], -C2, rw[R, :], ALU.mult, ALU.add, reads=[rp], writes=[rp]))
            if which == 1:
                R_(lambda rw=rw: ts(dve, rw[R, :], rw[R, :], math.pi / 2.0, None, ALU.add, reads=[rp], writes=[rp]))
            R_(lambda rw=rw: ts(dve, rw[R, :], rw[R, :], PI_LO, -PI_LO, ALU.min, ALU.max, reads=[rp], writes=[rp]))

    def emit_rope_act():
        for which in range(2):
            actf(sn2[which][R, :], rr2[which][R, :], AF.Sin, reads=[rp], writes=[rp])
            C.dma(rope_d[which, :, :], sn2[which][R, :], reads=[rp], sem=C.dsem("ropest"))

    class Conv:
        def __init__(self, tag, nstg, engs):
            self.tag = tag
            self.n = nstg
            self.stg = [arena.alloc([2816], F32) for _ in range(nstg)]
            self.obf = [arena.alloc([2816], BF16) for _ in range(nstg)]
            self.stg_r = [Res("%sstg%d" % (tag, i)) for i in range(nstg)]
            self.obf_r = [Res("%sobf%d" % (tag, i)) for i in range(nstg)]
            self.engs = engs
            self.queue = []
            self.loaded = 0
            self.done = 0

        def add(self, pcs):
            self.queue += pcs

        def step(self):
            if self.done >= len(self.queue):
                return False
            while self.loaded < min(len(self.queue), self.done + self.n):
                m = self.loaded
                i = m % self.n
                C.dma(self.stg[i][:, 0:self.queue[m][1]], self.queue[m][0], writes=[self.stg_r[i]],
                      sem=C.dsem("%sstg%d" % (self.tag, i)))
                self.loaded += 1
            m = self.done
            i = m % self.n
            E = self.engs[m % len(self.engs)]
            self.queue[m][2](E, self.stg[i], self.obf[i], [self.stg_r[i]], [self.obf_r[i]],
                             C.dsem("%sobf%d" % (self.tag, i)))
            self.done += 1
            return True

        def flush(self):
            while self.step():
                pass

    def scale_to(E, out, in_, scale_ap, rs, ws, neg=False):
        if E is act:
            if scale_ap is None:
                return actf(out, in_, AF.Copy, reads=rs, writes=ws)
            if neg:
                E = dve
            else:
                return actf(out, in_, AF.Identity, scale=scale_ap, reads=rs + [prm_r], writes=ws)
        if scale_ap is None:
            return cp(E, out, in_, reads=rs, writes=ws)
        if neg:
            ts(E, out, in_, scale_ap, None, ALU.mult, reads=rs + [prm_r], writes=ws)
            return ts(E, out, out, -1.0, None, ALU.mult, reads=ws, writes=ws)
        return ts(E, out, in_, scale_ap, None, ALU.mult, reads=rs + [prm_r], writes=ws)

    groups = {}
    cur = [None]
    SE = [dve]

    def conv_piece(src, ncols, emit):
        groups.setdefault(cur[0], []).append((src, ncols, emit))

    for l in range(n_layers):
        pb = l * PL
        cur[0] = (l, "mix")
        for kc in range(8):
            g = prm[:, pb + P_GMIX + kc:pb + P_GMIX + kc + 1]

            def emit(E, s_, o_, rs, ws, osem, g=g, kc=kc, l=l):
                scale_to(E, o_[:, 0:IN_COLS], s_[:, 0:IN_COLS], g, rs, ws)
                ts(SE[0], o_[:, 1696:1760], s_[:, 0:64], 0.0, None, ALU.mult, reads=rs, writes=ws)
                scale_to(SE[0], o_[:, 1760:1776], s_[:, 656:672], g, rs, ws, neg=True)
                scale_to(SE[0], o_[:, 1776:1792], s_[:, 640:656], g, rs, ws)
                C.dma(Win_d[l, :, kc * WIN_W:(kc + 1) * WIN_W], o_[:, 0:WIN_W], reads=ws,
                      sem=osem)
            conv_piece(w_in[l, kc * 128:(kc + 1) * 128, :], IN_COLS, emit)
        for kc in range(3):
            g = prm[:, pb + P_GQ + kc:pb + P_GQ + kc + 1]

            def emit(E, s_, o_, rs, ws, osem, g=g, kc=kc, l=l):
                s3 = s_[:, 0:768].rearrange("p (h d) -> p h d", d=96)
                on = o_[:, 0:512].rearrange("p (h d) -> p h d", d=64)
                orp = o_[:, 512:768].rearrange("p (h d) -> p h d", d=32)
                ort = o_[:, 768:1024].rearrange("p (h d) -> p h d", d=32)
                scale_to(SE[0], on, s3[:, :, 0:64], g, rs, ws)
                scale_to(SE[0], orp, s3[:, :, 64:96], g, rs, ws)
                scale_to(SE[0], ort[:, :, 0:16], s3[:, :, 80:96], g, rs, ws, neg=True)
                scale_to(SE[0], ort[:, :, 16:32], s3[:, :, 64:80], g, rs, ws)
                C.dma(Wuq_d[l, :, kc * 1024:(kc + 1) * 1024], o_[:, 0:1024], reads=ws, sem=osem)
            conv_piece(w_uq[l, kc * 128:(kc + 1) * 128, :], 768, emit)
        for kc in range(2):
            g = prm[:, pb + P_GKV + kc:pb + P_GKV + kc + 1]

            def emit(E, s_, o_, rs, ws, osem, g=g, kc=kc, l=l):
                s3 = s_[:, 0:1024].rearrange("p (h d) -> p h d", d=128)
                ok = o_[:, 0:512].rearrange("p (h d) -> p h d", d=64)
                ov = o_[:, 512:1024].rearrange("p (h d) -> p h d", d=64)
                scale_to(SE[0], ok, s3[:, :, 0:64], g, rs, ws)
                scale_to(pool, ov, s3[:, :, 64:128], g, rs, ws)
                C.dma(Wk_d[l, :, kc * 512:(kc + 1) * 512], o_[:, 0:512], reads=ws,
                      sem=osem)
                C.dma(Wv_d[l, :, kc * 512:(kc + 1) * 512], o_[:, 512:1024], reads=ws,
                      sem=osem)
            conv_piece(w_ukv[l, kc * 128:(kc + 1) * 128, :], 1024, emit)
        for kc in range(8):
            def emit(E, s_, o_, rs, ws, osem, kc=kc, l=l):
                scale_to(E, o_[:, 0:D], s_[:, 0:D], None, rs, ws)
                C.dma(Wo_d[l, :, kc * D:(kc + 1) * D], o_[:, 0:D], reads=ws, sem=osem)
            conv_piece(w_o[l, kc * 128:(kc + 1) * 128, :], D, emit)
        cur[0] = (l, "ffn")
        for kc in range(8):
            g = prm[:, pb + P_GFFN + kc:pb + P_GFFN + kc + 1]
            for hf in range(2):
                def emit(E, s_, o_, rs, ws, osem, g=g, kc=kc, hf=hf, l=l):
                    scale_to(E, o_[:, 0:DFF], s_[:, 0:DFF], g, rs, ws)
                    C.dma(Wup_d[l, :, kc * 2 * DFF + hf * DFF:kc * 2 * DFF + (hf + 1) * DFF], o_[:, 0:DFF],
                          reads=ws, sem=osem)
                conv_piece(w_up[l, kc * 128:(kc + 1) * 128, hf * DFF:(hf + 1) * DFF], DFF, emit)
        for j in range(NJ):
            def emit(E, s_, o_, rs, ws, osem, j=j, l=l):
                scale_to(E, o_[:, 0:D], s_[:, 0:D], None, rs, ws)
                C.dma(Wdn_d[l, :, j * D:(j + 1) * D], o_[:, 0:D], reads=ws, sem=osem)
            conv_piece(w_down[l, j * 128:(j + 1) * 128, :], D, emit)
    emit_rope_dve()
    cv0 = Conv("u", 3, [act])
    cv0.add(groups[(0, "mix")][0:13])
    while cv0.step():
        for _ in range(2):
            if rope_ops:
                rope_ops.pop(0)()
    while rope_ops:
        rope_ops.pop(0)()
    emit_rope_act()
    C.barrier()
    bg_queue = list(groups[(0, "mix")][13:]) + list(groups[(0, "ffn")])
    for l_ in range(1, n_layers):
        bg_queue += groups[(l_, "mix")]
    bg_split = len(bg_queue)
    for l_ in range(1, n_layers):
        bg_queue += groups[(l_, "ffn")]
    bg_done = [0]

    for l in range(n_layers):
        pb = l * PL
        xin = (xT if l == 0 else X2).rearrange("(c p) t -> p c t", p=128)
        last = (l == n_layers - 1)

        arena.reset()
        u_glu = arena.alloc([4, S + 30], BF16)
        offA = arena.off
        cosT = arena.alloc([S], F32)
        sinT = arena.alloc([S], F32)
        Win = arena.alloc([8, WIN_W], BF16)
        Wuq = arena.alloc([3, 1024], BF16)
        Wk = arena.alloc([2, 512], BF16)
        Wv = arena.alloc([2, 512], BF16)
        xt = arena.alloc([8, TT], F32)
        xbb = [arena.alloc([8, TT], BF16) for _ in range(2)]
        sq8 = arena.alloc([8, TT], BF16)
        rb = arena.alloc([TT], F32)
        tmpA = arena.alloc([TT], F32)
        cqf = arena.alloc([3, TT], F32)
        ckvf = arena.alloc([2, TT], F32)
        rq = arena.alloc([TT], F32)
        rkv = arena.alloc([TT], F32)
        cqn = arena.alloc([3, TT], BF16)
        ckvn = arena.alloc([2, TT], BF16)
        sgb = [arena.alloc([TT], F32) for _ in range(2)]
        t1b = [arena.alloc([TT], F32) for _ in range(2)]
        t2b = [arena.alloc([TT], F32) for _ in range(2)]
        Qt = arena.alloc([8, TT], BF16)
        t3 = arena.alloc([2, TT], BF16)
        Kt = arena.alloc([8, TT], BF16)
        krb = arena.alloc([TT], BF16)
        Vt = arena.alloc([8, 4, 65], BF16)

        wA_r = Res("wA")
        rope_r = Res("rope")
        uglu_r = Res("u_glu")
        xt_r, rb_r, tmpA_r = Res("xt"), Res("rb"), Res("tmpA")
        xb_r = [Res("xb0"), Res("xb1")]
        sq_r = [Res("sq%d" % i) for i in range(8)]
        cqf_r, ckvf_r, rq_r, rkv_r, cqn_r, ckvn_r = Res("cqf"), Res("ckvf"), Res("rq"), Res("rkv"), Res("cqn"), Res("ckvn")
        sgb_r = [Res("sg0"), Res("sg1")]
        t1_r = [Res("t10"), Res("t11")]
        t2_r = [Res("t20"), Res("t21")]
        Qt_r, Kt_r, krb_r, Vt_r = Res("Qt"), Res("Kt"), Res("krb"), Res("Vt")
        t3_r = Res("t3")

        ws = C.dsem("wA")
        C.dma(xt, xin[:, :, 0:TT], writes=[xt_r], sem=C.dsem("xt"))
        C.dma(Win.rearrange("p a b -> p (a b)"), Win_d[l, :, :], writes=[wA_r], sem=ws)
        C.dma(Wuq.rearrange("p a b -> p (a b)"), Wuq_d[l, :, :], writes=[wA_r], sem=ws)
        C.dma(Wk.rearrange("p a b -> p (a b)"), Wk_d[l, :, :], writes=[wA_r], sem=ws)
        C.dma(Wv.rearrange("p a b -> p (a b)"), Wv_d[l, :, :], writes=[wA_r], sem=ws)
        wsr = C.dsem("wAr")
        for qd in (2, 0, 1, 3):
            C.dma(cosT[32 * qd:32 * qd + 32, :], rope_d[1, :, :], writes=[rope_r], sem=wsr)
            C.dma(sinT[32 * qd:32 * qd + 32, :], rope_d[0, :, :], writes=[rope_r], sem=wsr)
        C.op(pool, lambda: nc.gpsimd.memset(u_glu[:, :, 0:15], 0.0), writes=[uglu_r])
        C.op(pool, lambda: nc.gpsimd.memset(u_glu[:, :, S + 15:S + 30], 0.0), writes=[uglu_r])
        C.op(pool, lambda: nc.gpsimd.memset(Vt[:, :, :, 64:65], 1.0), writes=[Vt_r])

        bk = [0]

        def nb():
            b = bk[0] % 8
            bk[0] += 1
            return b

        def loadA(i):
            C.dma(xt, xin[:, :, i * TT:(i + 1) * TT], writes=[xt_r], sem=C.dsem("xt"))

        def prepA_sq(i):
            E_ = dve if i == 0 else pool
            for c in range(8):
                tt(E_, sq8[:, c, :], xt[:, c, :], xt[:, c, :], ALU.mult, reads=[xt_r], writes=[sq_r[c]])

        def prepA(i):
            t0 = i * TT
            kb = i % 2
            bs = nb()
            for c in range(8):
                mm(banks[bs][:, :], ones[:, :], sq8[:, c, :], c == 0, c == 7, [sq_r[c], const_r], [bank_res[bs]], sig=True)
            rstd_from(banks[bs][:, :], D, rb, tmpA, [bank_res[bs]], [tmpA_r], [rb_r])
            tt(dve, xbb[kb], xt, rb.unsqueeze(1).to_broadcast([128, 8, TT]), ALU.mult,
               reads=[xt_r, rb_r], writes=[xb_r[kb]])
            if i + 1 < NT:
                loadA(i + 1)

        prepA_sq(0)
        prepA(0)
        for i in range(NT):
            t0 = i * TT
            kb = i % 2
            xb = xbb[kb]

            def proj(col0, m, b):
                for kc in range(8):
                    mm(banks[b][0:m, :], Win[:, kc, col0:col0 + m], xb[:, kc, :], kc == 0, kc == 7,
                       [wA_r, xb_r[kb]], [bank_res[b]])

            if i + 1 < NT:
                prepA_sq(i + 1)
            sl_ap = [sgb[0].bitcast(BF16)[:, 0:TT], sgb[0].bitcast(BF16)[:, TT:2 * TT], sgb[1].bitcast(BF16)[:, 0:TT],
                     sgb[1].bitcast(BF16)[:, TT:2 * TT], t1b[1].bitcast(BF16)[:, 0:TT]]
            sl_r = [sgb_r[0], sgb_r[0], sgb_r[1], sgb_r[1], t1_r[1]]
            for j in range(3):
                b = nb()
                proj(128 * j, 128, b)
                cp(act, cqf[:, j, :], banks[b][:, :], reads=[bank_res[b]], writes=[cqf_r])
            for j in range(3):
                tt(dve, sl_ap[j], cqf[:, j, :], cqf[:, j, :], ALU.mult, reads=[cqf_r], writes=[sl_r[j]])
            for j in range(2):
                b = nb()
                proj(384 + 128 * j, 128, b)
                cp(act, ckvf[:, j, :], banks[b][:, :], reads=[bank_res[b]], writes=[ckvf_r])
            for j in range(2):
                tt(dve, sl_ap[3 + j], ckvf[:, j, :], ckvf[:, j, :], ALU.mult, reads=[ckvf_r], writes=[sl_r[3 + j]])
            bs = nb()
            for j in range(3):
                mm(banks[bs][:, :], ones[:, :], sl_ap[j], j == 0, j == 2, [sl_r[j], const_r], [bank_res[bs]], sig=True)
            rstd_from(banks[bs][:, :], 384, rq, tmpA, [bank_res[bs]], [tmpA_r], [rq_r])
            tt(dve, cqn, cqf, rq.unsqueeze(1).to_broadcast([128, 3, TT]), ALU.mult, reads=[cqf_r, rq_r], writes=[cqn_r])
            bA = nb()
            proj(576, 96, bA)
            bB = nb()
            proj(1696, 96, bB)
            tt(dve, t1b[0][R, :], banks[bA][R, :], cosT[R, t0:t0 + TT], ALU.mult, reads=[bank_res[bA], rope_r], writes=[t1_r[0]])
            tt(dve, t2b[0][R, :], banks[bB][R, :], sinT[R, t0:t0 + TT], ALU.mult, reads=[bank_res[bB], rope_r], writes=[t2_r[0]])
            tt(dve, krb[R, :], t1b[0][R, :], t2b[0][R, :], ALU.add, reads=[t1_r[0], t2_r[0]], writes=[krb_r])
            bs = nb()
            for j in range(2):
                mm(banks[bs][:, :], ones[:, :], sl_ap[3 + j], j == 0, j == 1, [sl_r[3 + j], const_r], [bank_res[bs]], sig=True)
            rstd_from(banks[bs][:, :], 256, rkv, tmpA, [bank_res[bs]], [tmpA_r], [rkv_r])
            tt(dve, ckvn, ckvf, rkv.unsqueeze(1).to_broadcast([128, 2, TT]), ALU.mult, reads=[ckvf_r, rkv_r], writes=[ckvn_r])
            if i + 1 < NT:
                prepA(i + 1)
            for j in range(4):
                k = j % 2
                ba = nb()
                proj(672 + 128 * j, 128, ba)
                bg = nb()
                proj(1184 + 128 * j, 128, bg)
                actf(sgb[k], banks[bg][:, :], AF.Sigmoid, reads=[bank_res[bg]], writes=[sgb_r[k]])
                tt(dve, u_glu[:, j, 15 + t0:15 + t0 + TT], banks[ba][:, :], sgb[k], ALU.mult,
                   reads=[bank_res[ba], sgb_r[k]], writes=[uglu_r])
            for hp in range(4):
                b = nb()
                for kc in range(3):
                    mm(banks[b][:, :], Wuq[:, kc, 128 * hp:128 * hp + 128], cqn[:, kc, :], kc == 0, kc == 2,
                       [wA_r, cqn_r], [bank_res[b]])
                cp(act, Qt[:, hp, :], banks[b][:, :], reads=[bank_res[b]], writes=[Qt_r])
            for hq in range(2):
                bA = nb()
                for kc in range(3):
                    mm(banks[bA][:, :], Wuq[:, kc, 512 + 128 * hq:512 + 128 * hq + 128], cqn[:, kc, :], kc == 0, kc == 2,
                       [wA_r, cqn_r], [bank_res[bA]])
                bB = nb()
                for kc in range(3):
                    mm(banks[bB][:, :], Wuq[:, kc, 768 + 128 * hq:768 + 128 * hq + 128], cqn[:, kc, :], kc == 0, kc == 2,
                       [wA_r, cqn_r], [bank_res[bB]])
                tt(dve, t1b[hq], banks[bA][:, :], cosT[:, t0:t0 + TT], ALU.mult, reads=[bank_res[bA], rope_r], writes=[t1_r[hq]])
                tt(dve, t2b[hq], banks[bB][:, :], sinT[:, t0:t0 + TT], ALU.mult, reads=[bank_res[bB], rope_r], writes=[t2_r[hq]])
                tt(dve, t3[:, hq, :], t1b[hq], t2b[hq], ALU.add, reads=[t1_r[hq], t2_r[hq]], writes=[t3_r])
            for hp in range(4):
                b = nb()
                for kc in range(2):
                    mm(banks[b][:, :], Wk[:, kc, 128 * hp:128 * hp + 128], ckvn[:, kc, :], kc == 0, kc == 1,
                       [wA_r, ckvn_r], [bank_res[b]])
                E = act if hp % 2 == 0 else dve
                cp(E, Kt[:, hp, :], banks[b][:, :], reads=[bank_res[b]], writes=[Kt_r])
            for c in range(4):
                b = nb()
                for kc in range(2):
                    mm(banks[b][:, :], ckvn[:, kc, c * 128:(c + 1) * 128], Wv[:, kc, :], kc == 0, kc == 1,
                       [wA_r, ckvn_r], [bank_res[b]])
                E = act if c % 2 == 0 else dve
                cp(E, Vt[:, :, c, 0:64], banks[b][:, :].rearrange("p (h d) -> p h d", d=64),
                   reads=[bank_res[b]], writes=[Vt_r])
            for m in range(2):
                C.dma(Qd[:, 0:64, t0:t0 + TT].rearrange("(hp m) p t -> m p hp t", m=2)[m],
                      Qt[64 * m:64 * m + 64, 0:4, :], reads=[Qt_r], sem=C.dsem("Qst"))
            for hq in range(2):
                for jq in range(4):
                    C.dma(Qd[4 * hq + jq, 64:96, t0:t0 + TT], t3[32 * jq:32 * jq + 32, hq, :], reads=[t3_r], sem=C.dsem("Qrst"))
            for m in range(2):
                C.dma(Kd[:, 0:64, t0:t0 + TT].rearrange("(hp m) p t -> m p hp t", m=2)[m],
                      Kt[64 * m:64 * m + 64, 0:4, :], reads=[Kt_r], sem=C.dsem("Kst"))
            C.dma(Kd[:, 64:96, t0:t0 + TT].rearrange("h p t -> p h t"),
                  krb[R, :].unsqueeze(1).to_broadcast([32, 8, TT]), reads=[krb_r], sem=C.dsem("krst"))
            C.dma(Vd[:, :, 4 * i * 65:(4 * i + 4) * 65].rearrange("h p x -> p h x"),
                  Vt.rearrange("p h c d -> p h (c d)"), reads=[Vt_r], sem=C.dsem("Vst"))
        C.barrier()

        TAIL_Q0 = ARW - 5136
        TAIL_WO = TAIL_Q0 - 4096 - 128
        Qh0 = arena.alloc_at(TAIL_Q0, [S], BF16)
        Kh0 = arena.alloc_at(TAIL_Q0 + 2048, [S], BF16)
        Vh0 = arena.alloc_at(TAIL_Q0 + 4096, [32 * 65], BF16)
        Wo = arena.alloc_at(TAIL_WO, [8, D], BF16)
        Qh_r = [Res("hd0"), Res("hd1")]
        Kh_r = Qh_r
        Vh_r = Qh_r
        Wo_r = Res("Wo")
        sm0 = C.dsem("hd0")
        C.dma(Qh0[0:96, :], Qd[0, :, :], writes=[Qh_r[0]], sem=sm0)
        C.dma(Kh0[0:96, :], Kd[0, :, :], writes=[Kh_r[0]], sem=sm0)
        C.dma(Vh0[:, :], Vd[0, :, :], writes=[Vh_r[0]], sem=sm0)

        arena.reset(offA)
        u_final = arena.alloc([4, S], BF16)
        y_attn = arena.alloc([4, S], BF16)
        offBC = arena.off
        Dg = arena.alloc([4, 31, 128], BF16)
        vf = arena.alloc([4, TT], F32)
        meanb = arena.alloc([TT], F32)
        m2b = arena.alloc([TT], F32)
        varb = arena.alloc([TT], F32)
        rstdb = arena.alloc([TT], F32)
        tmpB = arena.alloc([TT], F32)
        ddb = [arena.alloc([TT], F32) for _ in range(2)]
        nnb = [arena.alloc([TT], F32) for _ in range(4)]
        Dg_r, vf_r = Res("Dg"), Res("vf")
        Dg_cc_r = [Res("Dg%d" % i) for i in range(4)]
        vb_r = [Res("vb0"), Res("vb1")]
        vs_r = [Res("vs0"), Res("vs1")]
        mean_r, m2_r, var_r, rstdb_r, tmpB_r = Res("mean"), Res("m2"), Res("var"), Res("rstdb"), Res("tmpB")
        dd_r = [Res("dd0"), Res("dd1")]
        nn_r = [Res("nn%d" % i) for i in range(4)]
        ufin_r, yattn_r = Res("u_final"), Res("y_attn")
        for cc in range(4):
            col = pb + P_WCONV + cc * 31
            tt(dve, Dg[:, cc, :, :], ident[:, :].unsqueeze(1).to_broadcast([128, 31, 128]),
               prm[:, col:col + 31].unsqueeze(2).to_broadcast([128, 31, 128]), ALU.mult,
               reads=[const_r, prm_r], writes=[Dg_cc_r[cc]])
        vbq = [arena.alloc([TT], BF16) for _ in range(4)]
        vsq = [arena.alloc([TT], BF16) for _ in range(4)]
        vf2 = arena.alloc([4, TT], F32)
        vfs = [vf, vf2]
        vfs_r = [vf_r, Res("vf2")]
        vbq_r = [Res("vbq%d" % i) for i in range(4)]
        vsq_r = [Res("vsq%d" % i) for i in range(4)]
        assert arena.off <= TAIL_Q0, "phase B arena collides with head slot 0"
        stat_banks = {}
        cbk = [0]

        def conv_unit(i, cc):
            t0 = i * TT
            kv = i % 2
            if cc == 0:
                stat_banks[i] = (4, 5) if i % 2 == 0 else (6, 7)
            b = cbk[0] % 4
            cbk[0] += 1
            for k in range(31):
                mm(banks[b], Dg[:, cc, k, :], u_glu[:, cc, t0 + k:t0 + k + TT], k == 0, k == 30,
                   [Dg_cc_r[cc], uglu_r], [bank_res[b]])
            actf(vfs[kv][:, cc, :], banks[b], AF.Identity, bias=prm[:, pb + P_BCONV + cc:pb + P_BCONV + cc + 1],
                 reads=[bank_res[b], prm_r], writes=[vfs_r[kv]])
            cp(act, vbq[cc], vfs[kv][:, cc, :], reads=[vfs_r[kv]], writes=[vbq_r[cc]])
            tt(dve, vsq[cc], vfs[kv][:, cc, :], vfs[kv][:, cc, :], ALU.mult, reads=[vfs_r[kv]], writes=[vsq_r[cc]])

        def stat_unit(i, cc):
            bs1, bs2 = stat_banks[i]
            mm(banks[bs1], ones[:, :], vbq[cc], cc == 0, cc == 3, [vbq_r[cc], const_r], [bank_res[bs1]], sig=True)
            mm(banks[bs2], ones[:, :], vsq[cc], cc == 0, cc == 3, [vsq_r[cc], const_r], [bank_res[bs2]], sig=True)

        def ln_finish(i):
            t0 = i * TT
            kv = i % 2
            bs1, bs2 = stat_banks.pop(i)
            ts(dve, meanb, banks[bs1], 1.0 / 512, None, ALU.mult, reads=[bank_res[bs1]], writes=[mean_r])
            tt(dve, m2b, meanb, meanb, ALU.mult, reads=[mean_r], writes=[m2_r])
            stt(dve, varb, banks[bs2], 1.0 / 512, m2b, ALU.mult, ALU.subtract, reads=[bank_res[bs2], m2_r], writes=[var_r])
            actf(tmpB, varb, AF.Ln, bias=eps_ap, reads=[var_r, const_r], writes=[tmpB_r])
            actf(rstdb, tmpB, AF.Exp, scale=-0.5, reads=[tmpB_r], writes=[rstdb_r])
            for cc in range(4):
                k2 = cc % 2
                tt(dve, ddb[k2], vfs[kv][:, cc, :], meanb, ALU.subtract, reads=[vfs_r[kv], mean_r], writes=[dd_r[k2]])
                tt(dve, nnb[cc], ddb[k2], rstdb, ALU.mult, reads=[dd_r[k2], rstdb_r], writes=[nn_r[cc]])

        def ln_silu(i):
            t0 = i * TT
            for cc in range(4):
                actf(u_final[:, cc, t0:t0 + TT], nnb[cc], AF.Silu,
                     bias=prm[:, pb + P_BLN + cc:pb + P_BLN + cc + 1],
                     scale=prm[:, pb + P_GLN + cc:pb + P_GLN + cc + 1],
                     reads=[nn_r[cc], prm_r], writes=[ufin_r])

        units = [(i, cc) for i in range(NT) for cc in range(4)]
        prev = None
        pend_silu = None
        for (i, cc) in units:
            conv_unit(i, cc)
            if pend_silu is not None and cc == 1:
                ln_silu(pend_silu)
                pend_silu = None
            if prev is not None:
                stat_unit(*prev)
                if prev[1] == 3:
                    ln_finish(prev[0])
                    pend_silu = prev[0]
            prev = (i, cc)
        stat_unit(*prev)
        ln_finish(prev[0])
        ln_silu(prev[0])
        C.barrier()
        if debug:
            C.dma(dbg_u.rearrange("(c p) t -> p c t", p=128), u_final, reads=[ufin_r], sem=C.dsem("dbg"))

        arena.reset(offBC)
        Qh = [Qh0, arena.alloc([S], BF16)]
        Kh = [Kh0, arena.alloc([S], BF16)]
        Vh = [Vh0, arena.alloc([32 * 65], BF16)]
        PT = [arena.alloc([1024], BF16) for _ in range(3)]
        Osb = [arena.alloc([TT], F32) for _ in range(2)]
        recb = [arena.alloc([TT], F32) for _ in range(2)]
        PT_r = [Res("PT%d" % i) for i in range(3)]
        Osb_r = [Res("Osb0"), Res("Osb1")]
        rec_r = [Res("rec0"), Res("rec1")]
        hl_r = [Res("hl0"), Res("hl1")]
        pair_ps = [ps_all[:, p * 1024:(p + 1) * 1024] for p in range(3)]
        pair_r = [Res("pair%d" % p) for p in range(3)]
        ob_r = {6: bank_res[6], 7: bank_res[7]}

        def load_head(h):
            sl = h % 2
            sm = C.dsem("hd%d" % sl)
            C.dma(Qh[sl][0:96, :], Qd[h, :, :], writes=[Qh_r[sl]], sem=sm)
            C.dma(Kh[sl][0:96, :], Kd[h, :, :], writes=[Kh_r[sl]], sem=sm)
            C.dma(Vh[sl][:, :], Vd[h, :, :], writes=[Vh_r[sl]], sem=sm)

        cvb = Conv("b%d" % l, 2, [dve])
        if l == 0:
            cvb.add(bg_queue[0:bg_split])
        elif l == 1:
            cvb.add(bg_queue[bg_split:])
        pc = [0]
        NIT = NH * 8
        assert arena.off <= TAIL_WO, ("attention arena collides with tail buffers", arena.off, TAIL_WO)
        pair_of = {}

        def geom(n):
            h, qb = n // 8, n % 8
            return h, h % 2, qb * TT, 6 + (n % 2), n % 2

        def S_unit(n, u):
            h, sl, q0, ob, ko = geom(n)
            p = pc[0] % 3
            pc[0] += 1
            pair_of[(n, u)] = p
            for hf in range(2):
                kc = 2 * u + hf
                mm(pair_ps[p][:, hf * 512:(hf + 1) * 512], Kh[sl][0:96, kc * 128:(kc + 1) * 128],
                   Qh[sl][0:96, q0:q0 + TT], True, True, [Kh_r[sl], Qh_r[sl]], [pair_r[p]], sig=(hf == 1))

        def E_unit(n, u):
            p = pair_of[(n, u)]
            actf(PT[p], pair_ps[p], AF.Exp, scale=SCALE, reads=[pair_r[p]], writes=[PT_r[p]])

        def PV_unit(n, u):
            h, sl, q0, ob, ko = geom(n)
            p = pair_of.pop((n, u))
            for hf in range(2):
                kc = 2 * u + hf
                mm(banks[ob][0:65, :], Vh[sl][:, kc * 65:(kc + 1) * 65], PT[p][:, hf * 512:(hf + 1) * 512],
                   kc == 0, kc == 31, [Vh_r[sl], PT_r[p]], [ob_r[ob]])

        def tail1(n):
            h, sl, q0, ob, ko = geom(n)
            cp(dve, Osb[ko][0:65, :], banks[ob][0:65, :], reads=[ob_r[ob]], writes=[Osb_r[ko]])

        def tail2(n):
            h, sl, q0, ob, ko = geom(n)
            rb16 = recb[ko].bitcast(BF16)
            dhi, dlo = rb16[64:65, 0:TT], rb16[64:65, TT:2 * TT]
            cp(dve, dhi, Osb[ko][64:65, :], reads=[Osb_r[ko]], writes=[hl_r[ko]])
            tt(dve, dlo, Osb[ko][64:65, :], dhi, ALU.subtract, reads=[Osb_r[ko], hl_r[ko]], writes=[hl_r[ko]])
            mm(banks[ob][0:64, :], ones[64:65, 0:64], dhi, True, False, [const_r, hl_r[ko]], [ob_r[ob]])
            mm(banks[ob][0:64, :], ones[64:65, 0:64], dlo, False, True, [const_r, hl_r[ko]], [ob_r[ob]])
            C.op(dve, lambda: nc.vector.reciprocal(out=recb[ko][0:64, :], in_=banks[ob][0:64, :]),
                 [ob_r[ob]], [rec_r[ko]])
            po = (h % 2) * 64
            tt(dve, y_attn[po:po + 64, h // 2, q0:q0 + TT], Osb[ko][0:64, :], recb[ko][0:64, :], ALU.mult,
               reads=[Osb_r[ko], rec_r[ko]], writes=[yattn_r])

        G = [(n, u) for n in range(NIT) for u in range(16)]
        deferred = None
        S_unit(*G[0])
        S_unit(*G[1])
        for idx, (n, u) in enumerate(G):
            if u == 0 and n % 8 == 0 and n // 8 + 1 < NH:
                load_head(n // 8 + 1)
            if idx + 2 < len(G):
                S_unit(*G[idx + 2])
            E_unit(n, u)
            if deferred is not None and u == 0:
                tail1(deferred)
            if deferred is not None and u == 2:
                tail2(deferred)
                deferred = None
            PV_unit(n, u)
            if u == 8:
                cvb.step()
            if u == 0 and n == 16:
                C._wait(sp, {d_: d_.v for d_ in (C.dsem("b%dobf0" % l), C.dsem("b%dobf1" % l)) if d_.v > 0})
                C.dma(Wo.rearrange("p a b -> p (a b)"), Wo_d[l, :, :], writes=[Wo_r], sem=C.dsem("wD"))
            if u == 15:
                deferred = n
        tail1(deferred)
        tail2(deferred)
        cvb.flush()
        C.barrier()
        if debug:
            C.dma(dbg_y.rearrange("(c p) t -> p c t", p=128), y_attn, reads=[yattn_r], sem=C.dsem("dbg"))

        arena.reset(offBC)
        xd = [arena.alloc([8, TT], F32) for _ in range(2)]
        xd_r = [Res("xd0"), Res("xd1")]
        arena.reset()
        blk_pairs = [(0, 2), (2, 8), (8, 15), (15, 22)]
        Wup_b = [arena.alloc([8, 2 * 128 * (jb_ - ja_)], BF16) for (ja_, jb_) in blk_pairs]
        assert arena.off == 2 * 8 * DFF // 2, arena.off
        Wup_blk_r = [Res("Wup%d" % bi) for bi in range(4)]
        Wup_dv = Wup_d[l, :, :].rearrange("p (k n) -> p k n", k=8)

        def load_wup_block(bi, E=None):
            ja_, jb_ = blk_pairs[bi]
            ncb = 128 * (jb_ - ja_)
            for hf, base in enumerate((0, DFF)):
                C.dma(Wup_b[bi][:, :, hf * ncb:(hf + 1) * ncb], Wup_dv[:, :, base + 128 * ja_:base + 128 * jb_],
                      writes=[Wup_blk_r[bi]], sem=C.dsem("wE%d" % bi), E=E)

        def wup_cols(j, half):
            for bi, (ja_, jb_) in enumerate(blk_pairs):
                if ja_ <= j < jb_:
                    ncb = 128 * (jb_ - ja_)
                    o = half * ncb + 128 * (j - ja_)
                    return Wup_b[bi], o, Wup_blk_r[bi]
        Wdn = arena.alloc([NJ, D], BF16)
        xe = arena.alloc([8, TT], F32)
        h2 = arena.alloc([8, TT], BF16)
        rbe = arena.alloc([TT], F32)
        tmpE = arena.alloc([TT], F32)
        gE = arena.alloc([NJ, TT], BF16)
        tgb = [arena.alloc([TT], F32) for _ in range(2)]
        tvb = [arena.alloc([TT], F32) for _ in range(2)]
        sge = [arena.alloc([TT], F32) for _ in range(2)]
        xres = [arena.alloc([TT], F32) for _ in range(2)]
        Wup_r, Wdn_r, xe_r, h2_r, rbe_r, tmpE_r, gE_r = Res("Wup"), Res("Wdn"), Res("xe"), Res("h2"), Res("rbe"), Res("tmpE"), Res("gE")
        gE_rj = [Res("gE%d" % j_) for j_ in range(NJ)]
        tg_r = [Res("tg0"), Res("tg1")]
        tv_r = [Res("tv0"), Res("tv1")]
        sge_r = [Res("sge0"), Res("sge1")]
        xres_r = [Res("xres0"), Res("xres1")]
        X2v = X2.rearrange("(c p) t -> p c t", p=128)
        NTE = 9

        def tile_geom(i):
            c0 = 510 * i
            W = min(512, S + 2 - c0)
            return c0, W, W - 2

        sq_bufs = [tgb[0], tgb[1], tvb[0], tvb[1]]
        sq_res = [tg_r[0], tg_r[1], tv_r[0], tv_r[1]]

        def sq_slot(c, W):
            v = sq_bufs[c // 2].bitcast(BF16)
            o = (c % 2) * TT
            return v[:, o:o + W], sq_res[c // 2]

        def prep_load(i):
            c0, W, Wo_ = tile_geom(i)
            C.dma(xe[:, :, 0:W], X1v[:, :, c0:c0 + W], writes=[xe_r], sem=C.dsem("xe"))

        def prep_sq(i):
            c0, W, Wo_ = tile_geom(i)
            for c in range(8):
                ap_, r_ = sq_slot(c, W)
                tt(dve, ap_, xe[:, c, 0:W], xe[:, c, 0:W], ALU.mult, reads=[xe_r], writes=[r_])

        def prep_stats(i):
            c0, W, Wo_ = tile_geom(i)
            bs = nb()
            for c in range(8):
                ap_, r_ = sq_slot(c, W)
                mm(banks[bs][:, 0:W], ones[:, :], ap_, c == 0, c == 7, [r_, const_r], [bank_res[bs]], sig=True)
            rstd_from(banks[bs][:, 0:W], D, rbe[:, 0:W], tmpE[:, 0:W], [bank_res[bs]], [tmpE_r], [rbe_r])
            tt(dve, h2[:, :, 0:W], xe[:, :, 0:W], rbe[:, 0:W].unsqueeze(1).to_broadcast([128, 8, W]), ALU.mult,
               reads=[xe_r, rbe_r], writes=[h2_r])

        assert 33792 >= offBC + 2 * 4096, "FFN prologue buffers overlap phase D"
        C.dma(xd[0], xin[:, :, 0:TT], writes=[xd_r[0]], sem=C.dsem("xd0"))
        for i in range(NT):
            t0 = i * TT
            k = i % 2
            if i + 1 < NT:
                C.dma(xd[1 - k], xin[:, :, t0 + TT:t0 + 2 * TT], writes=[xd_r[1 - k]], sem=C.dsem("xd%d" % (1 - k)))
            for oc in range(8):
                b = nb()
                for kc in range(8):
                    rhs = y_attn[:, kc, t0:t0 + TT] if kc < 4 else u_final[:, kc - 4, t0:t0 + TT]
                    mm(banks[b], Wo[:, kc, oc * 128:(oc + 1) * 128], rhs, kc == 0, kc == 7,
                       [Wo_r, yattn_r, ufin_r], [bank_res[b]])
                tt(dve, xd[k][:, oc, :], xd[k][:, oc, :], banks[b], ALU.add, reads=[xd_r[k], bank_res[b]], writes=[xd_r[k]])
            C.dma(X1v[:, :, 1 + t0:1 + t0 + TT], xd[k], reads=[xd_r[k]], sem=C.dsem("xd%d" % k))
            if i == 0:
                st0_ev = (C.dsem("xd0"), C.dsem("xd0").v)
            if i == 2:
                C._wait(sp, {st0_ev[0]: st0_ev[1], C.dsem("pad"): C.dsem("pad").v})
                prep_load(0)
            if i == 4:
                prep_sq(0)
                assert 2 * 8 * 128 * 8 // 2 <= offA, "prefetched FFN weight blocks must stay inside the dead u_glu region"
            if i == 6:
                prep_stats(0)
                ev_ = xd_r[1].w
                C._wait(act, {ev_[0]: ev_[1]})
                load_wup_block(0, E=act)
                load_wup_block(1, E=act)
        C.barrier()

        load_wup_block(2)
        load_wup_block(3)
        for q in range(2):
            C.dma(Wdn[:, q * 11:(q + 1) * 11, :].rearrange("p a b -> p (a b)"),
                  Wdn_d[l, :, q * 11 * D:(q + 1) * 11 * D], writes=[Wdn_r], sem=C.dsem("wE2"))
        prep_load(1)

        def down(i, oc):
            c0, W, Wo_ = tile_geom(i)
            k2 = oc % 2
            C.dma(xres[k2][:, 0:Wo_], X1v[:, oc, c0 + 1:c0 + 1 + Wo_], writes=[xres_r[k2]], sem=C.dsem("xr%d" % k2))
            b = nb()
            for j in range(NJ):
                mm(banks[b][:, 0:Wo_], Wdn[:, j, oc * 128:(oc + 1) * 128], gE[:, j, 0:Wo_], j == 0, j == NJ - 1,
                   [Wdn_r, gE_rj[j]], [bank_res[b]])
            tt(dve, xres[k2][:, 0:Wo_], xres[k2][:, 0:Wo_], banks[b][:, 0:Wo_], ALU.add,
               reads=[xres_r[k2], bank_res[b]], writes=[xres_r[k2]])
            C.dma(X2v[:, oc, c0:c0 + Wo_], xres[k2][:, 0:Wo_], reads=[xres_r[k2]], sem=C.dsem("xr%d" % k2))

        outv = outT.rearrange("(c p) t -> p c t", p=128)
        fsq = [xres[0].bitcast(BF16), xres[1].bitcast(BF16)]
        fsq_r = [Res("fsq0"), Res("fsq1")]
        rbF_r, tmpF2_r = Res("rbF"), Res("tmpF2")

        dl_bank = {}

        def dl_mm(i, oc, b=None):
            c0, W, Wo_ = tile_geom(i)
            if b is None:
                b = nb()
            dl_bank[oc] = b
            for j in range(NJ):
                mm(banks[b][:, 0:Wo_], Wdn[:, j, oc * 128:(oc + 1) * 128], gE[:, j, 0:Wo_], j == 0, j == NJ - 1,
                   [Wdn_r, gE_rj[j]], [bank_res[b]])

        def dl_res_loads(i):
            c0, W, Wo_ = tile_geom(i)
            for oc in range(8):
                C.dma(xe[:, oc, 0:Wo_], X1v[:, oc, c0 + 1:c0 + 1 + Wo_], writes=[xe_r], sem=C.dsem("xeo"))

        def dl_add(i, oc):
            c0, W, Wo_ = tile_geom(i)
            b = dl_bank[oc]
            tt(dve, xe[:, oc, 0:Wo_], xe[:, oc, 0:Wo_], banks[b][:, 0:Wo_], ALU.add,
               reads=[xe_r, bank_res[b]], writes=[xe_r])
            k2 = oc % 2
            actf(fsq[k2][:, 0:Wo_], xe[:, oc, 0:Wo_], AF.Square, reads=[xe_r], writes=[fsq_r[k2]])
            mm(banks[7][:, 0:Wo_], ones[:, :], fsq[k2][:, 0:Wo_], oc == 0, oc == 7, [fsq_r[k2], const_r], [bank_res[7]],
               sig=True)

        def dl_finish(i):
            c0, W, Wo_ = tile_geom(i)
            rstd_from(banks[7][:, 0:Wo_], D, rbe[:, 0:Wo_], tmpE[:, 0:Wo_], [bank_res[7]], [tmpE_r], [rbe_r])
            for oc in range(8):
                stt(dve, xe[:, oc, 0:Wo_], xe[:, oc, 0:Wo_], prm[:, P_GFINAL + oc:P_GFINAL + oc + 1], rbe[:, 0:Wo_],
                    ALU.mult, ALU.mult, reads=[xe_r, rbe_r, prm_r], writes=[xe_r])
            C.dma(outv[:, :, c0:c0 + Wo_], xe[:, :, 0:Wo_], reads=[xe_r], sem=C.dsem("xeo"))

        if last and final:
            def nb():
                b = bk[0] % 7
                bk[0] += 1
                return b

        pend_gate = [None]
        for i in range(NTE):
            c0, W, Wo_ = tile_geom(i)
            for j in range(NJ):
                k = j % 2
                wg_t, wg_o, wr = wup_cols(j, 0)
                wv_t, wv_o, _ = wup_cols(j, 1)
                bG = nb()
                for kc in range(8):
                    mm(banks[bG][:, 0:W], wg_t[:, kc, wg_o:wg_o + 128], h2[:, kc, 0:W], kc == 0, kc == 7,
                       [wr, h2_r], [bank_res[bG]])
                bV = nb()
                for kc in range(8):
                    mm(banks[bV][:, 0:W], wv_t[:, kc, wv_o:wv_o + 128], h2[:, kc, 0:W], kc == 0, kc == 7,
                       [wr, h2_r], [bank_res[bV]])
                cg = pb + P_WFFN + j * 3
                cv = pb + P_WFFN + (NJ + j) * 3
                bgc = pb + P_BFFN + j
                bvc = pb + P_BFFN + NJ + j
                tg, tv, sg = tgb[k][:, 0:Wo_], tvb[k][:, 0:Wo_], sge[k][:, 0:Wo_]
                actf(tg, banks[bG][:, 0:Wo_], AF.Identity, scale=prm[:, cg:cg + 1], reads=[bank_res[bG], prm_r], writes=[tg_r[k]])
                actf(tv, banks[bV][:, 0:Wo_], AF.Identity, scale=prm[:, cv:cv + 1], bias=prm[:, bvc:bvc + 1],
                     reads=[bank_res[bV], prm_r], writes=[tv_r[k]])
                if pend_gate[0] is not None:
                    pend_gate[0]()
                stt(dve, tg, banks[bG][:, 1:Wo_ + 1], prm[:, cg + 1:cg + 2], tg, ALU.mult, ALU.add,
                    reads=[bank_res[bG], tg_r[k], prm_r], writes=[tg_r[k]])
                stt(dve, tv, banks[bV][:, 1:Wo_ + 1], prm[:, cv + 1:cv + 2], tv, ALU.mult, ALU.add,
                    reads=[bank_res[bV], tv_r[k], prm_r], writes=[tv_r[k]])
                stt(dve, tg, banks[bG][:, 2:Wo_ + 2], prm[:, cg + 2:cg + 3], tg, ALU.mult, ALU.add,
                    reads=[bank_res[bG], tg_r[k], prm_r], writes=[tg_r[k]])
                stt(dve, tv, banks[bV][:, 2:Wo_ + 2], prm[:, cv + 2:cv + 3], tv, ALU.mult, ALU.add,
                    reads=[bank_res[bV], tv_r[k], prm_r], writes=[tv_r[k]])

                def fin(j=j, k=k, tg=tg, tv=tv, sg=sg, bgc=bgc, Wo_=Wo_, i=i):
                    if debug and i == 2 and l == 0 and j == 0:
                        C.dma(dbg_tg[:, 0:Wo_], tg, reads=[tg_r[k]], sem=C.dsem("dbg"))
                    actf(sg, tg, AF.Silu, bias=prm[:, bgc:bgc + 1], reads=[tg_r[k], prm_r], writes=[sge_r[k]])
                    tt(pool, gE[:, j, 0:Wo_], tv, sg, ALU.mult, reads=[tv_r[k], sge_r[k]], writes=[gE_rj[j]])
                pend_gate[0] = fin
            pend_gate[0]()
            pend_gate[0] = None
            if debug and i == 2 and l == 0:
                C.dma(dbg_g.rearrange("(c p) t -> p c t", p=128), gE, reads=gE_rj, sem=C.dsem("dbg"))
                C.dma(dbg_h2.rearrange("(c p) t -> p c t", p=128), h2, reads=[h2_r], sem=C.dsem("dbg"))
                C.barrier()
            if last and final:
                if i + 1 < NTE:
                    prep_sq(i + 1)
                for oc in range(3):
                    dl_mm(i, oc)
                stat_b = bk[0] % 7
                if i + 1 < NTE:
                    prep_stats(i + 1)
                else:
                    bk[0] += 1
                dl_res_loads(i)
                dl_mm(i, 3)
                dl_mm(i, 4)
                dl_mm(i, 5)
                dl_mm(i, 6, b=stat_b)
                dl_add(i, 0)
                dl_add(i, 1)
                dl_mm(i, 7, b=dl_bank[0])
                for oc in range(2, 8):
                    dl_add(i, oc)
                dl_finish(i)
                if i + 2 < NTE:
                    prep_load(i + 2)
            else:
                if i + 1 < NTE:
                    prep_sq(i + 1)
                for oc in range(4):
                    down(i, oc)
                if i + 1 < NTE:
                    prep_stats(i + 1)
                    if i + 2 < NTE:
                        prep_load(i + 2)
                for oc in range(4, 8):
                    down(i, oc)
        C.barrier()

    if final:
        return nc
    arena.reset()
    xf = [arena.alloc([8, TT], F32) for _ in range(2)]
    sqf = [arena.alloc([8, TT], BF16) for _ in range(2)]
    rbf = arena.alloc([TT], F32)
    tmpF = arena.alloc([TT], F32)
    xf_r = [Res("xf0"), Res("xf1")]
    sqf_r = [Res("sqf0"), Res("sqf1")]
    rbf_r, tmpF_r = Res("rbf"), Res("tmpF")
    X2v = X2.rearrange("(c p) t -> p c t", p=128)
    outv = outT.rearrange("(c p) t -> p c t", p=128)
    C.dma(xf[0], X2v[:, :, 0:TT], writes=[xf_r[0]], sem=C.dsem("xf0"))
    for i in range(NT):
        t0 = i * TT
        k = i % 2
        if i + 1 < NT:
            C.dma(xf[1 - k], X2v[:, :, t0 + TT:t0 + 2 * TT], writes=[xf_r[1 - k]], sem=C.dsem("xf%d" % (1 - k)))
        bs = i % 8
        for c in range(8):
            tt(dve, sqf[k][:, c, :], xf[k][:, c, :], xf[k][:, c, :], ALU.mult, reads=[xf_r[k]], writes=[sqf_r[k]])
        for c in range(8):
            mm(banks[bs], ones[:, :], sqf[k][:, c, :], c == 0, c == 7, [sqf_r[k], const_r], [bank_res[bs]], sig=True)
        rstd_from(banks[bs], D, rbf, tmpF, [bank_res[bs]], [tmpF_r], [rbf_r])
        for c in range(8):
            stt(dve, xf[k][:, c, :], xf[k][:, c, :], prm[:, P_GFINAL + c:P_GFINAL + c + 1], rbf, ALU.mult, ALU.mult,
                reads=[xf_r[k], rbf_r, prm_r], writes=[xf_r[k]])
        C.dma(outv[:, :, t0:t0 + TT], xf[k], reads=[xf_r[k]], sem=C.dsem("xf%d" % k))
    C.barrier()
    return nc


def _pack_params(inp):
    P = np.zeros((128, NP), np.float32)

    def colmajor(v, nchunk):
        return np.ascontiguousarray(v.reshape(nchunk, 128).T)

    for l in range(NL):
        pb = l * PL
        P[:, pb + P_GMIX:pb + P_GMIX + 8] = colmajor(inp["g_mix"][l], 8)
        P[:, pb + P_GQ:pb + P_GQ + 3] = colmajor(inp["g_q"][l], 3)
        P[:, pb + P_GKV:pb + P_GKV + 2] = colmajor(inp["g_kv"][l], 2)
        P[:, pb + P_GFFN:pb + P_GFFN + 8] = colmajor(inp["g_ffn"][l], 8)
        wc = inp["w_dw_conv"][l]
        P[:, pb + P_WCONV:pb + P_WCONV + 124] = wc.reshape(31, 4, 128).transpose(2, 1, 0).reshape(128, 124)
        P[:, pb + P_BCONV:pb + P_BCONV + 4] = colmajor(inp["b_dw_conv"][l], 4)
        P[:, pb + P_GLN:pb + P_GLN + 4] = colmajor(inp["g_conv_ln"][l], 4)
        P[:, pb + P_BLN:pb + P_BLN + 4] = colmajor(inp["b_conv_ln"][l], 4)
        wf = inp["w_dw_ffn"][l]
        P[:, pb + P_WFFN:pb + P_WFFN + 132] = wf.reshape(3, 44, 128).transpose(2, 1, 0).reshape(128, 132)
        P[:, pb + P_BFFN:pb + P_BFFN + 44] = colmajor(inp["b_dw_ffn"][l], 44)
    P[:, P_GFINAL:P_GFINAL + 8] = colmajor(inp["g_final"], 8)
    inv_freq = (1.0 / (np.float32(10000.0) ** (np.arange(0, 32, 2, dtype=np.float32) / np.float32(32.0)))).astype(np.float32)
    for p in range(64, 96):
        P[p, P_INVF] = inv_freq[(p - 64) % 16]
    return P


_NC_CACHE = {}


def make_in_maps(inp):
    inp = {k: np.asarray(v) for k, v in inp.items()}
    P = _pack_params(inp)
    maps = []
    for b in range(8):
        maps.append({
            "xT": np.ascontiguousarray(inp["x"][b].T),
            "pos": np.ascontiguousarray(inp["positions"][b].reshape(1, S).astype(np.int32)),
            "params": P,
            "w_in": inp["w_in"], "w_uq": inp["w_uq"], "w_ukv": inp["w_ukv"], "w_o": inp["w_o"],
            "w_up": inp["w_up"], "w_down": inp["w_down"],
        })
    return maps


def kernel(**inputs):
    if "nc" not in _NC_CACHE:
        _NC_CACHE["nc"] = build_program()
    nc = _NC_CACHE["nc"]
    maps = make_in_maps(inputs)
    res = run_bass_kernel_spmd(nc, maps, core_ids=list(range(8)))
    out = np.stack([np.ascontiguousarray(res.results[b]["outT"].T) for b in range(8)], axis=0)
    return out.astype(np.float32)
```

```python
import math
import sys
import numpy as np
import ml_dtypes
import concourse.bass as bass
import concourse.mybir as mybir
from concourse.bass_utils import run_bass_kernel_spmd

F32 = mybir.dt.float32
BF16 = mybir.dt.bfloat16
I32 = mybir.dt.int32
ALU = mybir.AluOpType
AF = mybir.ActivationFunctionType

S = 4096
D = 1024
NL = 2
NH = 8
TT = 512
NT = S // TT
DFF = 2816
NJ = DFF // 128
EPS = 1e-6
IN_COLS = 1696
WIN_W = 1792
SCALE = 1.0 / math.sqrt(96.0)

PL = 333
P_GMIX, P_GQ, P_GKV, P_GFFN, P_WCONV, P_BCONV, P_GLN, P_BLN, P_WFFN, P_BFFN = 0, 8, 11, 13, 21, 145, 149, 153, 157, 289
P_GFINAL = NL * PL
P_INVF = NL * PL + 8
NP = NL * PL + 9

MAGIC = 12582912.0
TWO_PI = 2.0 * math.pi
C1 = 6.28125
C2 = TWO_PI - C1
PI_LO = 3.1415925


PE_TAGS = []


class Sem:
    def __init__(self, nc, name):
        self.h = nc.alloc_semaphore(name=name)
        self.v = 0


class Res:
    __slots__ = ("name", "w", "r")

    def __init__(self, name):
        self.name = name
        self.w = None
        self.r = {}


class Eng:
    def __init__(self, nc, name, b, is_pe=False):
        self.name = name
        self.b = b
        self.sem = Sem(nc, "e_" + name)
        self.seen = {}
        self.is_pe = is_pe
        self.pending = []


class Ctx:
    def __init__(self, nc):
        self.nc = nc
        self.pe = Eng(nc, "pe", nc.tensor, True)
        self.act = Eng(nc, "act", nc.scalar)
        self.dve = Eng(nc, "dve", nc.vector)
        self.pool = Eng(nc, "pool", nc.gpsimd)
        self.sp = Eng(nc, "sp", nc.sync)
        self.engs = [self.pe, self.act, self.dve, self.pool, self.sp]
        self.dsems = {}

    def dsem(self, name):
        if name not in self.dsems:
            self.dsems[name] = Sem(self.nc, "d_" + name)
        return self.dsems[name]

    def _wait(self, E, deps):
        for s, v in deps.items():
            if E.seen.get(s, 0) < v:
                E.b.wait_ge(s.h, v)
                E.seen[s] = v

    def op(self, E, build, reads=(), writes=(), sig=True, dsem=None):
        deps = {}

        def add(ev, kind):
            if ev is None:
                return
            s, v = ev
            if v is None:
                if s is E.sem:
                    return
                raise RuntimeError("dependency on unsignalled instruction")
            if s is E.sem and (kind != "raw" or E.is_pe):
                return
            if dsem is not None and s is dsem and kind == "waw":
                return
            if deps.get(s, 0) < v:
                deps[s] = v

        for r in reads:
            add(r.w, "raw")
        for w in writes:
            for E2 in self.engs:
                if E2 is not E:
                    for res, kind in E2.pending:
                        if res is w:
                            raise RuntimeError("write to %s while %s has an unsignalled access" % (w.name, E2.name))
            add(w.w, "waw")
            for s, v in w.r.items():
                add((s, v), "war")
        self._wait(E, deps)
        ins = build()
        if dsem is not None:
            dsem.v += 16
            ins.then_inc(dsem.h, 16)
            ev = (dsem, dsem.v)
        elif sig:
            E.sem.v += 1
            ins.then_inc(E.sem.h, 1)
            ev = (E.sem, E.sem.v)
            for res, kind in E.pending:
                if kind == "r":
                    if res.r.get(E.sem, 0) < E.sem.v:
                        res.r[E.sem] = E.sem.v
                else:
                    if res.w is not None and res.w[0] is E.sem and res.w[1] is None:
                        res.w = ev
            E.pending = []
        else:
            ev = None
        for r in reads:
            if ev is None:
                E.pending.append((r, "r"))
            elif r.r.get(ev[0], 0) < ev[1]:
                r.r[ev[0]] = ev[1]
        for w in writes:
            if ev is None:
                w.w = (E.sem, None)
                E.pending.append((w, "w"))
            else:
                w.w = ev
            w.r = {}
        return ins

    def dma(self, out, in_, reads=(), writes=(), sem=None, E=None):
        E = E or self.sp
        return self.op(E, lambda: E.b.dma_start(out=out, in_=in_), reads, writes, dsem=sem)

    def barrier(self):
        allsems = [e.sem for e in self.engs] + list(self.dsems.values())
        for E in self.engs:
            assert not E.pending, E.name
            if E.is_pe:
                continue
            deps = {s: s.v for s in allsems if s is not E.sem and s.v > 0}
            self._wait(E, deps)


class Arena:
    def __init__(self, ap, nwords):
        self.ap = ap
        self.n = nwords
        self.off = 0

    def reset(self, off=0):
        self.off = off

    def alloc_at(self, off, free_shape, dtype):
        save = self.off
        self.off = off
        v = self.alloc(free_shape, dtype)
        self.off = save
        return v

    def alloc(self, free_shape, dtype):
        nel = 1
        for x in free_shape:
            nel *= x
        nbytes = nel * (4 if dtype in (F32, I32) else 2)
        nw = (nbytes + 3) // 4
        nw = (nw + 3) // 4 * 4
        assert self.off + nw <= self.n, ("arena overflow", self.off, nw, self.n)
        v = self.ap[:, self.off:self.off + nw]
        self.off += nw
        if dtype != F32:
            v = v.bitcast(dtype)
        v = v[:, 0:nel]
        if len(free_shape) == 2:
            v = v.rearrange("p (a b) -> p a b", a=free_shape[0])
        elif len(free_shape) == 3:
            v = v.rearrange("p (a b c) -> p a b c", a=free_shape[0], b=free_shape[1])
        return v


def build_program(n_layers=NL, final=True, debug=False):
    nc = bass.Bass("TRN2", target_bir_lowering=False)
    dk = "ExternalOutput" if debug else "Internal"
    xT = nc.dram_tensor("xT", [D, S], F32, kind="ExternalInput").ap()
    pos = nc.dram_tensor("pos", [1, S], I32, kind="ExternalInput").ap()
    params = nc.dram_tensor("params", [128, NP], F32, kind="ExternalInput").ap()
    w_in = nc.dram_tensor("w_in", [NL, D, IN_COLS], F32, kind="ExternalInput").ap()
    w_uq = nc.dram_tensor("w_uq", [NL, 384, 768], F32, kind="ExternalInput").ap()
    w_ukv = nc.dram_tensor("w_ukv", [NL, 256, 1024], F32, kind="ExternalInput").ap()
    w_o = nc.dram_tensor("w_o", [NL, D, D], F32, kind="ExternalInput").ap()
    w_up = nc.dram_tensor("w_up", [NL, D, 2 * DFF], F32, kind="ExternalInput").ap()
    w_down = nc.dram_tensor("w_down", [NL, DFF, D], F32, kind="ExternalInput").ap()
    outT = nc.dram_tensor("outT", [D, S], F32, kind="ExternalOutput").ap()

    Win_d = nc.dram_tensor("Win_d", [NL, 128, 8 * WIN_W], BF16).ap()
    Wuq_d = nc.dram_tensor("Wuq_d", [NL, 128, 3 * 1024], BF16).ap()
    Wk_d = nc.dram_tensor("Wk_d", [NL, 128, 2 * 512], BF16).ap()
    Wv_d = nc.dram_tensor("Wv_d", [NL, 128, 2 * 512], BF16).ap()
    Wo_d = nc.dram_tensor("Wo_d", [NL, 128, 8 * D], BF16).ap()
    Wup_d = nc.dram_tensor("Wup_d", [NL, 128, 8 * 2 * DFF], BF16).ap()
    Wdn_d = nc.dram_tensor("Wdn_d", [NL, 128, NJ * D], BF16).ap()
    rope_d = nc.dram_tensor("rope_d", [2, 32, S], F32, kind=dk).ap()
    Qd = nc.dram_tensor("Qd", [NH, 96, S], BF16, kind=dk).ap()
    Kd = nc.dram_tensor("Kd", [NH, 96, S], BF16, kind=dk).ap()
    Vd = nc.dram_tensor("Vd", [NH, 128, 32 * 65], BF16, kind=dk).ap()
    X1 = nc.dram_tensor("X1", [D, S + 2], F32, kind=dk).ap()
    X2 = nc.dram_tensor("X2", [D, S], F32, kind=dk).ap()
    dbg_u = nc.dram_tensor("dbg_u", [D // 2, S], BF16, kind=dk).ap()
    dbg_y = nc.dram_tensor("dbg_y", [D // 2, S], BF16, kind=dk).ap()
    dbg_h2 = nc.dram_tensor("dbg_h2", [D, TT], BF16, kind=dk).ap()
    dbg_g = nc.dram_tensor("dbg_g", [DFF, TT], BF16, kind=dk).ap()
    dbg_tg = nc.dram_tensor("dbg_tg", [128, TT], F32, kind=dk).ap()

    ARW = 51800
    arena_t = nc.alloc_sbuf_tensor("arena", [128, ARW], F32)
    prm_t = nc.alloc_sbuf_tensor("prm", [128, NP], F32)
    ident_t = nc.alloc_sbuf_tensor("ident", [128, 128], BF16)
    ones_t = nc.alloc_sbuf_tensor("ones", [128, 128], BF16)
    sel_t = nc.alloc_sbuf_tensor("sel", [128, 64], F32)
    zero_t = nc.alloc_sbuf_tensor("zero", [128, 16], F32)
    eps_t = nc.alloc_sbuf_tensor("epsc", [128, 1], F32)
    arena = Arena(arena_t.ap() if hasattr(arena_t, "ap") else arena_t, ARW)
    prm = prm_t.ap() if hasattr(prm_t, "ap") else prm_t
    ident = ident_t.ap() if hasattr(ident_t, "ap") else ident_t
    ones = ones_t.ap() if hasattr(ones_t, "ap") else ones_t
    sel = sel_t.ap() if hasattr(sel_t, "ap") else sel_t
    zero = zero_t.ap() if hasattr(zero_t, "ap") else zero_t
    eps_ap = (eps_t.ap() if hasattr(eps_t, "ap") else eps_t)[:, 0:1]
    ps_t = nc.alloc_psum_tensor("psall", [128, 4096], F32)
    ps_all = ps_t.ap() if hasattr(ps_t, "ap") else ps_t
    banks = [ps_all[:, i * 512:(i + 1) * 512] for i in range(8)]

    C = Ctx(nc)
    pe, act, dve, pool, sp = C.pe, C.act, C.dve, C.pool, C.sp
    bank_res = [Res("bank%d" % i) for i in range(8)]

    def mm(out, lhsT, rhs, start, stop, reads, writes, sig=None):
        PE_TAGS.append(sys._getframe(1).f_lineno)
        C.op(pe, lambda: nc.tensor.matmul(out, lhsT=lhsT, rhs=rhs, start=start, stop=stop),
             reads, writes, sig=(stop if sig is None else sig))

    def ts(E, out, in0, s1, s2, op0, op1=None, reads=(), writes=()):
        if op1 is None:
            return C.op(E, lambda: E.b.tensor_scalar(out=out, in0=in0, scalar1=s1, scalar2=None, op0=op0), reads, writes)
        return C.op(E, lambda: E.b.tensor_scalar(out=out, in0=in0, scalar1=s1, scalar2=s2, op0=op0, op1=op1), reads, writes)

    def tt(E, out, in0, in1, op, reads=(), writes=()):
        return C.op(E, lambda: E.b.tensor_tensor(out=out, in0=in0, in1=in1, op=op), reads, writes)

    def stt(E, out, in0, scalar, in1, op0, op1, reads=(), writes=()):
        return C.op(E, lambda: E.b.scalar_tensor_tensor(out=out, in0=in0, scalar=scalar, in1=in1, op0=op0, op1=op1), reads, writes)

    def cp(E, out, in_, reads=(), writes=()):
        if E is act:
            return C.op(E, lambda: nc.scalar.copy(out=out, in_=in_), reads, writes)
        return C.op(E, lambda: E.b.tensor_copy(out=out, in_=in_), reads, writes)

    def actf(out, in_, func, bias=None, scale=None, reads=(), writes=()):
        kw = {}
        if bias is not None:
            kw["bias"] = bias
        if scale is not None:
            kw["scale"] = scale
        return C.op(act, lambda: nc.scalar.activation(out=out, in_=in_, func=func, **kw), reads, writes)

    def rstd_from(psum_ap, n, out_ap, tmp_ap, reads, writes_tmp, writes_out, npart=128):
        actf(tmp_ap, psum_ap, AF.Ln, bias=eps_ap, scale=1.0 / n, reads=list(reads) + [const_r], writes=writes_tmp)
        actf(out_ap, tmp_ap, AF.Exp, scale=-0.5, reads=writes_tmp, writes=writes_out)

    prm_r = Res("prm")
    const_r = Res("const")
    C.dma(prm[:, :], params[:, :], writes=[prm_r], sem=C.dsem("prm"))
    C.op(pool, lambda: nc.gpsimd.memset(ident[:], 0.0), writes=[const_r])
    C.op(pool, lambda: nc.gpsimd.affine_select(out=ident[:], in_=ident[:], pattern=[[-1, 128]],
                                               compare_op=ALU.not_equal, fill=1.0, base=0,
                                               channel_multiplier=1), reads=[const_r], writes=[const_r])
    C.op(pool, lambda: nc.gpsimd.memset(ones[:], 1.0), writes=[const_r])
    C.op(pool, lambda: nc.gpsimd.memset(zero[:], 0.0), writes=[const_r])
    C.op(pool, lambda: nc.gpsimd.memset(eps_ap, EPS), writes=[const_r])
    C.op(pool, lambda: nc.gpsimd.memset(sel[:], 0.0), writes=[const_r])
    C.op(pool, lambda: nc.gpsimd.memset(sel[64:65, :], 1.0), reads=[const_r], writes=[const_r])
    x1pad_r = Res("x1pad")
    X1v = X1.rearrange("(c p) t -> p c t", p=128)
    with nc.allow_non_contiguous_dma(reason="two zero pad columns, once"):
        for col in (0, S + 1):
            C.dma(X1v[:, :, col:col + 1], zero[:, 0:8].rearrange("p (c o) -> p c o", o=1),
                  reads=[const_r], writes=[x1pad_r], sem=C.dsem("pad"))

    arena.reset()
    R = slice(64, 96)
    posi = arena.alloc([S], I32)
    ang = arena.alloc([S], F32)
    kk = arena.alloc([S], F32)
    rr = arena.alloc([S], F32)
    sn2 = [arena.alloc([S], F32) for _ in range(2)]
    rp = Res("rope_tmp")
    invf = prm[R, P_INVF:P_INVF + 1]
    C.dma(posi[R, :], pos.partition_broadcast(32), writes=[rp], sem=C.dsem("pos"))

    rr2 = [rr, arena.alloc([S], F32)]

    rope_ops = []

    def emit_rope_dve():
        R_ = rope_ops.append
        R_(lambda: cp(dve, ang[R, :], posi[R, :], reads=[rp], writes=[rp]))
        R_(lambda: ts(dve, ang[R, :], ang[R, :], invf, None, ALU.mult, reads=[rp, prm_r], writes=[rp]))
        for which in range(2):
            rw = rr2[which]
            if which == 0:
                R_(lambda: ts(dve, kk[R, :], ang[R, :], 1.0 / TWO_PI, MAGIC, ALU.mult, ALU.add, reads=[rp], writes=[rp]))
            else:
                R_(lambda: ts(dve, kk[R, :], ang[R, :], 1.0 / TWO_PI, 0.25, ALU.mult, ALU.add, reads=[rp], writes=[rp]))
                R_(lambda: ts(dve, kk[R, :], kk[R, :], MAGIC, None, ALU.add, reads=[rp], writes=[rp]))
            R_(lambda: ts(dve, kk[R, :], kk[R, :], -MAGIC, None, ALU.add, reads=[rp], writes=[rp]))
            R_(lambda rw=rw: stt(dve, rw[R, :], kk[R, :], -C1, ang[R, :], ALU.mult, ALU.add, reads=[rp], writes=[rp]))
            R_(lambda rw=rw: stt(dve, rw[R, :], kk[R, :], -C2, rw[R, :], ALU.mult, ALU.add, reads=[rp], writes=[rp]))
            if which == 1:
                R_(lambda rw=rw: ts(dve, rw[R, :], rw[R, :], math.pi / 2.0, None, ALU.add, reads=[rp], writes=[rp]))
            R_(lambda rw=rw: ts(dve, rw[R, :], rw[R, :], PI_LO, -PI_LO, ALU.min, ALU.max, reads=[rp], writes=[rp]))

    def emit_rope_act():
        for which in range(2):
            actf(sn2[which][R, :], rr2[which][R, :], AF.Sin, reads=[rp], writes=[rp])
            C.dma(rope_d[which, :, :], sn2[which][R, :], reads=[rp], sem=C.dsem("ropest"))

    class Conv:
        def __init__(self, tag, nstg, engs):
            self.tag = tag
            self.n = nstg
            self.stg = [arena.alloc([2816], F32) for _ in range(nstg)]
            self.obf = [arena.alloc([2816], BF16) for _ in range(nstg)]
            self.stg_r = [Res("%sstg%d" % (tag, i)) for i in range(nstg)]
            self.obf_r = [Res("%sobf%d" % (tag, i)) for i in range(nstg)]
            self.engs = engs
            self.queue = []
            self.loaded = 0
            self.done = 0

        def add(self, pcs):
            self.queue += pcs

        def step(self):
            if self.done >= len(self.queue):
                return False
            while self.loaded < min(len(self.queue), self.done + self.n):
                m = self.loaded
                i = m % self.n
                C.dma(self.stg[i][:, 0:self.queue[m][1]], self.queue[m][0], writes=[self.stg_r[i]],
                      sem=C.dsem("%sstg%d" % (self.tag, i)))
                self.loaded += 1
            m = self.done
            i = m % self.n
            E = self.engs[m % len(self.engs)]
            self.queue[m][2](E, self.stg[i], self.obf[i], [self.stg_r[i]], [self.obf_r[i]],
                             C.dsem("%sobf%d" % (self.tag, i)))
            self.done += 1
            return True

        def flush(self):
            while self.step():
                pass

    def scale_to(E, out, in_, scale_ap, rs, ws, neg=False):
        if E is act:
            if scale_ap is None:
                return actf(out, in_, AF.Copy, reads=rs, writes=ws)
            if neg:
                E = dve
            else:
                return actf(out, in_, AF.Identity, scale=scale_ap, reads=rs + [prm_r], writes=ws)
        if scale_ap is None:
            return cp(E, out, in_, reads=rs, writes=ws)
        if neg:
            ts(E, out, in_, scale_ap, None, ALU.mult, reads=rs + [prm_r], writes=ws)
            return ts(E, out, out, -1.0, None, ALU.mult, reads=ws, writes=ws)
        return ts(E, out, in_, scale_ap, None, ALU.mult, reads=rs + [prm_r], writes=ws)

    groups = {}
    cur = [None]
    SE = [dve]

    def conv_piece(src, ncols, emit):
        groups.setdefault(cur[0], []).append((src, ncols, emit))

    for l in range(n_layers):
        pb = l * PL
        cur[0] = (l, "mix")
        for kc in range(8):
            g = prm[:, pb + P_GMIX + kc:pb + P_GMIX + kc + 1]

            def emit(E, s_, o_, rs, ws, osem, g=g, kc=kc, l=l):
                scale_to(E, o_[:, 0:IN_COLS], s_[:, 0:IN_COLS], g, rs, ws)
                ts(SE[0], o_[:, 1696:1760], s_[:, 0:64], 0.0, None, ALU.mult, reads=rs, writes=ws)
                scale_to(SE[0], o_[:, 1760:1776], s_[:, 656:672], g, rs, ws, neg=True)
                scale_to(SE[0], o_[:, 1776:1792], s_[:, 640:656], g, rs, ws)
                C.dma(Win_d[l, :, kc * WIN_W:(kc + 1) * WIN_W], o_[:, 0:WIN_W], reads=ws,
                      sem=osem)
            conv_piece(w_in[l, kc * 128:(kc + 1) * 128, :], IN_COLS, emit)
        for kc in range(3):
            g = prm[:, pb + P_GQ + kc:pb + P_GQ + kc + 1]

            def emit(E, s_, o_, rs, ws, osem, g=g, kc=kc, l=l):
                s3 = s_[:, 0:768].rearrange("p (h d) -> p h d", d=96)
                on = o_[:, 0:512].rearrange("p (h d) -> p h d", d=64)
                orp = o_[:, 512:768].rearrange("p (h d) -> p h d", d=32)
                ort = o_[:, 768:1024].rearrange("p (h d) -> p h d", d=32)
                scale_to(SE[0], on, s3[:, :, 0:64], g, rs, ws)
                scale_to(SE[0], orp, s3[:, :, 64:96], g, rs, ws)
                scale_to(SE[0], ort[:, :, 0:16], s3[:, :, 80:96], g, rs, ws, neg=True)
                scale_to(SE[0], ort[:, :, 16:32], s3[:, :, 64:80], g, rs, ws)
                C.dma(Wuq_d[l, :, kc * 1024:(kc + 1) * 1024], o_[:, 0:1024], reads=ws, sem=osem)
            conv_piece(w_uq[l, kc * 128:(kc + 1) * 128, :], 768, emit)
        for kc in range(2):
            g = prm[:, pb + P_GKV + kc:pb + P_GKV + kc + 1]

            def emit(E, s_, o_, rs, ws, osem, g=g, kc=kc, l=l):
                s3 = s_[:, 0:1024].rearrange("p (h d) -> p h d", d=128)
                ok = o_[:, 0:512].rearrange("p (h d) -> p h d", d=64)
                ov = o_[:, 512:1024].rearrange("p (h d) -> p h d", d=64)
                scale_to(SE[0], ok, s3[:, :, 0:64], g, rs, ws)
                scale_to(pool, ov, s3[:, :, 64:128], g, rs, ws)
                C.dma(Wk_d[l, :, kc * 512:(kc + 1) * 512], o_[:, 0:512], reads=ws,
                      sem=osem)
                C.dma(Wv_d[l, :, kc * 512:(kc + 1) * 512], o_[:, 512:1024], reads=ws,
                      sem=osem)
            conv_piece(w_ukv[l, kc * 128:(kc + 1) * 128, :], 1024, emit)
        for kc in range(8):
            def emit(E, s_, o_, rs, ws, osem, kc=kc, l=l):
                scale_to(E, o_[:, 0:D], s_[:, 0:D], None, rs, ws)
                C.dma(Wo_d[l, :, kc * D:(kc + 1) * D], o_[:, 0:D], reads=ws, sem=osem)
            conv_piece(w_o[l, kc * 128:(kc + 1) * 128, :], D, emit)
        cur[0] = (l, "ffn")
        for kc in range(8):
            g = prm[:, pb + P_GFFN + kc:pb + P_GFFN + kc + 1]
            for hf in range(2):
                def emit(E, s_, o_, rs, ws, osem, g=g, kc=kc, hf=hf, l=l):
                    scale_to(E, o_[:, 0:DFF], s_[:, 0:DFF], g, rs, ws)
                    C.dma(Wup_d[l, :, kc * 2 * DFF + hf * DFF:kc * 2 * DFF + (hf + 1) * DFF], o_[:, 0:DFF],
                          reads=ws, sem=osem)
                conv_piece(w_up[l, kc * 128:(kc + 1) * 128, hf * DFF:(hf + 1) * DFF], DFF, emit)
        for j in range(NJ):
            def emit(E, s_, o_, rs, ws, osem, j=j, l=l):
                scale_to(E, o_[:, 0:D], s_[:, 0:D], None, rs, ws)
                C.dma(Wdn_d[l, :, j * D:(j + 1) * D], o_[:, 0:D], reads=ws, sem=osem)
            conv_piece(w_down[l, j * 128:(j + 1) * 128, :], D, emit)
    emit_rope_dve()
    cv0 = Conv("u", 3, [act])
    cv0.add(groups[(0, "mix")][0:13])
    while cv0.step():
        for _ in range(2):
            if rope_ops:
                rope_ops.pop(0)()
    while rope_ops:
        rope_ops.pop(0)()
    emit_rope_act()
    C.barrier()
    bg_queue = list(groups[(0, "mix")][13:]) + list(groups[(0, "ffn")])
    for l_ in range(1, n_layers):
        bg_queue += groups[(l_, "mix")]
    bg_split = len(bg_queue)
    for l_ in range(1, n_layers):
        bg_queue += groups[(l_, "ffn")]
    bg_done = [0]

    for l in range(n_layers):
        pb = l * PL
        xin = (xT if l == 0 else X2).rearrange("(c p) t -> p c t", p=128)
        last = (l == n_layers - 1)

        arena.reset()
        u_glu = arena.alloc([4, S + 30], BF16)
        offA = arena.off
        cosT = arena.alloc([S], F32)
        sinT = arena.alloc([S], F32)
        Win = arena.alloc([8, WIN_W], BF16)
        Wuq = arena.alloc([3, 1024], BF16)
        Wk = arena.alloc([2, 512], BF16)
        Wv = arena.alloc([2, 512], BF16)
        xt = arena.alloc([8, TT], F32)
        xbb = [arena.alloc([8, TT], BF16) for _ in range(2)]
        sq8 = arena.alloc([8, TT], BF16)
        rb = arena.alloc([TT], F32)
        tmpA = arena.alloc([TT], F32)
        cqf = arena.alloc([3, TT], F32)
        ckvf = arena.alloc([2, TT], F32)
        rq = arena.alloc([TT], F32)
        rkv = arena.alloc([TT], F32)
        cqn = arena.alloc([3, TT], BF16)
        ckvn = arena.alloc([2, TT], BF16)
        sgb = [arena.alloc([TT], F32) for _ in range(2)]
        t1b = [arena.alloc([TT], F32) for _ in range(2)]
        t2b = [arena.alloc([TT], F32) for _ in range(2)]
        Qt = arena.alloc([8, TT], BF16)
        t3 = arena.alloc([2, TT], BF16)
        Kt = arena.alloc([8, TT], BF16)
        krb = arena.alloc([TT], BF16)
        Vt = arena.alloc([8, 4, 65], BF16)

        wA_r = Res("wA")
        rope_r = Res("rope")
        uglu_r = Res("u_glu")
        xt_r, rb_r, tmpA_r = Res("xt"), Res("rb"), Res("tmpA")
        xb_r = [Res("xb0"), Res("xb1")]
        sq_r = [Res("sq%d" % i) for i in range(8)]
        cqf_r, ckvf_r, rq_r, rkv_r, cqn_r, ckvn_r = Res("cqf"), Res("ckvf"), Res("rq"), Res("rkv"), Res("cqn"), Res("ckvn")
        sgb_r = [Res("sg0"), Res("sg1")]
        t1_r = [Res("t10"), Res("t11")]
        t2_r = [Res("t20"), Res("t21")]
        Qt_r, Kt_r, krb_r, Vt_r = Res("Qt"), Res("Kt"), Res("krb"), Res("Vt")
        t3_r = Res("t3")

        ws = C.dsem("wA")
        C.dma(xt, xin[:, :, 0:TT], writes=[xt_r], sem=C.dsem("xt"))
        C.dma(Win.rearrange("p a b -> p (a b)"), Win_d[l, :, :], writes=[wA_r], sem=ws)
        C.dma(Wuq.rearrange("p a b -> p (a b)"), Wuq_d[l, :, :], writes=[wA_r], sem=ws)
        C.dma(Wk.rearrange("p a b -> p (a b)"), Wk_d[l, :, :], writes=[wA_r], sem=ws)
        C.dma(Wv.rearrange("p a b -> p (a b)"), Wv_d[l, :, :], writes=[wA_r], sem=ws)
        wsr = C.dsem("wAr")
        for qd in (2, 0, 1, 3):
            C.dma(cosT[32 * qd:32 * qd + 32, :], rope_d[1, :, :], writes=[rope_r], sem=wsr)
            C.dma(sinT[32 * qd:32 * qd + 32, :], rope_d[0, :, :], writes=[rope_r], sem=wsr)
        C.op(pool, lambda: nc.gpsimd.memset(u_glu[:, :, 0:15], 0.0), writes=[uglu_r])
        C.op(pool, lambda: nc.gpsimd.memset(u_glu[:, :, S + 15:S + 30], 0.0), writes=[uglu_r])
        C.op(pool, lambda: nc.gpsimd.memset(Vt[:, :, :, 64:65], 1.0), writes=[Vt_r])

        bk = [0]

        def nb():
            b = bk[0] % 8
            bk[0] += 1
            return b

        def loadA(i):
            C.dma(xt, xin[:, :, i * TT:(i + 1) * TT], writes=[xt_r], sem=C.dsem("xt"))

        def prepA_sq(i):
            E_ = dve if i == 0 else pool
            for c in range(8):
                tt(E_, sq8[:, c, :], xt[:, c, :], xt[:, c, :], ALU.mult, reads=[xt_r], writes=[sq_r[c]])

        def prepA(i):
            t0 = i * TT
            kb = i % 2
            bs = nb()
            for c in range(8):
                mm(banks[bs][:, :], ones[:, :], sq8[:, c, :], c == 0, c == 7, [sq_r[c], const_r], [bank_res[bs]], sig=True)
            rstd_from(banks[bs][:, :], D, rb, tmpA, [bank_res[bs]], [tmpA_r], [rb_r])
            tt(dve, xbb[kb], xt, rb.unsqueeze(1).to_broadcast([128, 8, TT]), ALU.mult,
               reads=[xt_r, rb_r], writes=[xb_r[kb]])
            if i + 1 < NT:
                loadA(i + 1)

        prepA_sq(0)
        prepA(0)
        for i in range(NT):
            t0 = i * TT
            kb = i % 2
            xb = xbb[kb]

            def proj(col0, m, b):
                for kc in range(8):
                    mm(banks[b][0:m, :], Win[:, kc, col0:col0 + m], xb[:, kc, :], kc == 0, kc == 7,
                       [wA_r, xb_r[kb]], [bank_res[b]])

            if i + 1 < NT:
                prepA_sq(i + 1)
            sl_ap = [sgb[0].bitcast(BF16)[:, 0:TT], sgb[0].bitcast(BF16)[:, TT:2 * TT], sgb[1].bitcast(BF16)[:, 0:TT],
                     sgb[1].bitcast(BF16)[:, TT:2 * TT], t1b[1].bitcast(BF16)[:, 0:TT]]
            sl_r = [sgb_r[0], sgb_r[0], sgb_r[1], sgb_r[1], t1_r[1]]
            for j in range(3):
                b = nb()
                proj(128 * j, 128, b)
                cp(act, cqf[:, j, :], banks[b][:, :], reads=[bank_res[b]], writes=[cqf_r])
            for j in range(3):
                tt(dve, sl_ap[j], cqf[:, j, :], cqf[:, j, :], ALU.mult, reads=[cqf_r], writes=[sl_r[j]])
            for j in range(2):
                b = nb()
                proj(384 + 128 * j, 128, b)
                cp(act, ckvf[:, j, :], banks[b][:, :], reads=[bank_res[b]], writes=[ckvf_r])
            for j in range(2):
                tt(dve, sl_ap[3 + j], ckvf[:, j, :], ckvf[:, j, :], ALU.mult, reads=[ckvf_r], writes=[sl_r[3 + j]])
            bs = nb()
            for j in range(3):
                mm(banks[bs][:, :], ones[:, :], sl_ap[j], j == 0, j == 2, [sl_r[j], const_r], [bank_res[bs]], sig=True)
            rstd_from(banks[bs][:, :], 384, rq, tmpA, [bank_res[bs]], [tmpA_r], [rq_r])
            tt(dve, cqn, cqf, rq.unsqueeze(1).to_broadcast([128, 3, TT]), ALU.mult, reads=[cqf_r, rq_r], writes=[cqn_r])
            bA = nb()
            proj(576, 96, bA)
            bB = nb()
            proj(1696, 96, bB)
            tt(dve, t1b[0][R, :], banks[bA][R, :], cosT[R, t0:t0 + TT], ALU.mult, reads=[bank_res[bA], rope_r], writes=[t1_r[0]])
            tt(dve, t2b[0][R, :], banks[bB][R, :], sinT[R, t0:t0 + TT], ALU.mult, reads=[bank_res[bB], rope_r], writes=[t2_r[0]])
            tt(dve, krb[R, :], t1b[0][R, :], t2b[0][R, :], ALU.add, reads=[t1_r[0], t2_r[0]], writes=[krb_r])
            bs = nb()
            for j in range(2):
                mm(banks[bs][:, :], ones[:, :], sl_ap[3 + j], j == 0, j == 1, [sl_r[3 + j], const_r], [bank_res[bs]], sig=True)
            rstd_from(banks[bs][:, :], 256, rkv, tmpA, [bank_res[bs]], [tmpA_r], [rkv_r])
            tt(dve, ckvn, ckvf, rkv.unsqueeze(1).to_broadcast([128, 2, TT]), ALU.mult, reads=[ckvf_r, rkv_r], writes=[ckvn_r])
            if i + 1 < NT:
                prepA(i + 1)
            for j in range(4):
                k = j % 2
                ba = nb()
                proj(672 + 128 * j, 128, ba)
                bg = nb()
                proj(1184 + 128 * j, 128, bg)
                actf(sgb[k], banks[bg][:, :], AF.Sigmoid, reads=[bank_res[bg]], writes=[sgb_r[k]])
                tt(dve, u_glu[:, j, 15 + t0:15 + t0 + TT], banks[ba][:, :], sgb[k], ALU.mult,
                   reads=[bank_res[ba], sgb_r[k]], writes=[uglu_r])
            for hp in range(4):
                b = nb()
                for kc in range(3):
                    mm(banks[b][:, :], Wuq[:, kc, 128 * hp:128 * hp + 128], cqn[:, kc, :], kc == 0, kc == 2,
                       [wA_r, cqn_r], [bank_res[b]])
                cp(act, Qt[:, hp, :], banks[b][:, :], reads=[bank_res[b]], writes=[Qt_r])
            for hq in range(2):
                bA = nb()
                for kc in range(3):
                    mm(banks[bA][:, :], Wuq[:, kc, 512 + 128 * hq:512 + 128 * hq + 128], cqn[:, kc, :], kc == 0, kc == 2,
                       [wA_r, cqn_r], [bank_res[bA]])
                bB = nb()
                for kc in range(3):
                    mm(banks[bB][:, :], Wuq[:, kc, 768 + 128 * hq:768 + 128 * hq + 128], cqn[:, kc, :], kc == 0, kc == 2,
                       [wA_r, cqn_r], [bank_res[bB]])
                tt(dve, t1b[hq], banks[bA][:, :], cosT[:, t0:t0 + TT], ALU.mult, reads=[bank_res[bA], rope_r], writes=[t1_r[hq]])
                tt(dve, t2b[hq], banks[bB][:, :], sinT[:, t0:t0 + TT], ALU.mult, reads=[bank_res[bB], rope_r], writes=[t2_r[hq]])
                tt(dve, t3[:, hq, :], t1b[hq], t2b[hq], ALU.add, reads=[t1_r[hq], t2_r[hq]], writes=[t3_r])
            for hp in range(4):
                b = nb()
                for kc in range(2):
                    mm(banks[b][:, :], Wk[:, kc, 128 * hp:128 * hp + 128], ckvn[:, kc, :], kc == 0, kc == 1,
                       [wA_r, ckvn_r], [bank_res[b]])
                E = act if hp % 2 == 0 else dve
                cp(E, Kt[:, hp, :], banks[b][:, :], reads=[bank_res[b]], writes=[Kt_r])
            for c in range(4):
                b = nb()
                for kc in range(2):
                    mm(banks[b][:, :], ckvn[:, kc, c * 128:(c + 1) * 128], Wv[:, kc, :], kc == 0, kc == 1,
                       [wA_r, ckvn_r], [bank_res[b]])
                E = act if c % 2 == 0 else dve
                cp(E, Vt[:, :, c, 0:64], banks[b][:, :].rearrange("p (h d) -> p h d", d=64),
                   reads=[bank_res[b]], writes=[Vt_r])
            for m in range(2):
                C.dma(Qd[:, 0:64, t0:t0 + TT].rearrange("(hp m) p t -> m p hp t", m=2)[m],
                      Qt[64 * m:64 * m + 64, 0:4, :], reads=[Qt_r], sem=C.dsem("Qst"))
            for hq in range(2):
                for jq in range(4):
                    C.dma(Qd[4 * hq + jq, 64:96, t0:t0 + TT], t3[32 * jq:32 * jq + 32, hq, :], reads=[t3_r], sem=C.dsem("Qrst"))
            for m in range(2):
                C.dma(Kd[:, 0:64, t0:t0 + TT].rearrange("(hp m) p t -> m p hp t", m=2)[m],
                      Kt[64 * m:64 * m + 64, 0:4, :], reads=[Kt_r], sem=C.dsem("Kst"))
            C.dma(Kd[:, 64:96, t0:t0 + TT].rearrange("h p t -> p h t"),
                  krb[R, :].unsqueeze(1).to_broadcast([32, 8, TT]), reads=[krb_r], sem=C.dsem("krst"))
            C.dma(Vd[:, :, 4 * i * 65:(4 * i + 4) * 65].rearrange("h p x -> p h x"),
                  Vt.rearrange("p h c d -> p h (c d)"), reads=[Vt_r], sem=C.dsem("Vst"))
        C.barrier()

        TAIL_Q0 = ARW - 5136
        TAIL_WO = TAIL_Q0 - 4096 - 128
        Qh0 = arena.alloc_at(TAIL_Q0, [S], BF16)
        Kh0 = arena.alloc_at(TAIL_Q0 + 2048, [S], BF16)
        Vh0 = arena.alloc_at(TAIL_Q0 + 4096, [32 * 65], BF16)
        Wo = arena.alloc_at(TAIL_WO, [8, D], BF16)
        Qh_r = [Res("hd0"), Res("hd1")]
        Kh_r = Qh_r
        Vh_r = Qh_r
        Wo_r = Res("Wo")
        sm0 = C.dsem("hd0")
        C.dma(Qh0[0:96, :], Qd[0, :, :], writes=[Qh_r[0]], sem=sm0)
        C.dma(Kh0[0:96, :], Kd[0, :, :], writes=[Kh_r[0]], sem=sm0)
        C.dma(Vh0[:, :], Vd[0, :, :], writes=[Vh_r[0]], sem=sm0)

        arena.reset(offA)
        u_final = arena.alloc([4, S], BF16)
        y_attn = arena.alloc([4, S], BF16)
        offBC = arena.off
        Dg = arena.alloc([4, 31, 128], BF16)
        vf = arena.alloc([4, TT], F32)
        meanb = arena.alloc([TT], F32)
        m2b = arena.alloc([TT], F32)
        varb = arena.alloc([TT], F32)
        rstdb = arena.alloc([TT], F32)
        tmpB = arena.alloc([TT], F32)
        ddb = [arena.alloc([TT], F32) for _ in range(2)]
        nnb = [arena.alloc([TT], F32) for _ in range(4)]
        Dg_r, vf_r = Res("Dg"), Res("vf")
        Dg_cc_r = [Res("Dg%d" % i) for i in range(4)]
        vb_r = [Res("vb0"), Res("vb1")]
        vs_r = [Res("vs0"), Res("vs1")]
        mean_r, m2_r, var_r, rstdb_r, tmpB_r = Res("mean"), Res("m2"), Res("var"), Res("rstdb"), Res("tmpB")
        dd_r = [Res("dd0"), Res("dd1")]
        nn_r = [Res("nn%d" % i) for i in range(4)]
        ufin_r, yattn_r = Res("u_final"), Res("y_attn")
        for cc in range(4):
            col = pb + P_WCONV + cc * 31
            tt(dve, Dg[:, cc, :, :], ident[:, :].unsqueeze(1).to_broadcast([128, 31, 128]),
               prm[:, col:col + 31].unsqueeze(2).to_broadcast([128, 31, 128]), ALU.mult,
               reads=[const_r, prm_r], writes=[Dg_cc_r[cc]])
        vbq = [arena.alloc([TT], BF16) for _ in range(4)]
        vsq = [arena.alloc([TT], BF16) for _ in range(4)]
        vf2 = arena.alloc([4, TT], F32)
        vfs = [vf, vf2]
        vfs_r = [vf_r, Res("vf2")]
        vbq_r = [Res("vbq%d" % i) for i in range(4)]
        vsq_r = [Res("vsq%d" % i) for i in range(4)]
        assert arena.off <= TAIL_Q0, "phase B arena collides with head slot 0"
        stat_banks = {}
        cbk = [0]

        def conv_unit(i, cc):
            t0 = i * TT
            kv = i % 2
            if cc == 0:
                stat_banks[i] = (4, 5) if i % 2 == 0 else (6, 7)
            b = cbk[0] % 4
            cbk[0] += 1
            for k in range(31):
                mm(banks[b], Dg[:, cc, k, :], u_glu[:, cc, t0 + k:t0 + k + TT], k == 0, k == 30,
                   [Dg_cc_r[cc], uglu_r], [bank_res[b]])
            actf(vfs[kv][:, cc, :], banks[b], AF.Identity, bias=prm[:, pb + P_BCONV + cc:pb + P_BCONV + cc + 1],
                 reads=[bank_res[b], prm_r], writes=[vfs_r[kv]])
            cp(act, vbq[cc], vfs[kv][:, cc, :], reads=[vfs_r[kv]], writes=[vbq_r[cc]])
            tt(dve, vsq[cc], vfs[kv][:, cc, :], vfs[kv][:, cc, :], ALU.mult, reads=[vfs_r[kv]], writes=[vsq_r[cc]])

        def stat_unit(i, cc):
            bs1, bs2 = stat_banks[i]
            mm(banks[bs1], ones[:, :], vbq[cc], cc == 0, cc == 3, [vbq_r[cc], const_r], [bank_res[bs1]], sig=True)
            mm(banks[bs2], ones[:, :], vsq[cc], cc == 0, cc == 3, [vsq_r[cc], const_r], [bank_res[bs2]], sig=True)

        def ln_finish(i):
            t0 = i * TT
            kv = i % 2
            bs1, bs2 = stat_banks.pop(i)
            ts(dve, meanb, banks[bs1], 1.0 / 512, None, ALU.mult, reads=[bank_res[bs1]], writes=[mean_r])
            tt(dve, m2b, meanb, meanb, ALU.mult, reads=[mean_r], writes=[m2_r])
            stt(dve, varb, banks[bs2], 1.0 / 512, m2b, ALU.mult, ALU.subtract, reads=[bank_res[bs2], m2_r], writes=[var_r])
            actf(tmpB, varb, AF.Ln, bias=eps_ap, reads=[var_r, const_r], writes=[tmpB_r])
            actf(rstdb, tmpB, AF.Exp, scale=-0.5, reads=[tmpB_r], writes=[rstdb_r])
            for cc in range(4):
                k2 = cc % 2
                tt(dve, ddb[k2], vfs[kv][:, cc, :], meanb, ALU.subtract, reads=[vfs_r[kv], mean_r], writes=[dd_r[k2]])
                tt(dve, nnb[cc], ddb[k2], rstdb, ALU.mult, reads=[dd_r[k2], rstdb_r], writes=[nn_r[cc]])

        def ln_silu(i):
            t0 = i * TT
            for cc in range(4):
                actf(u_final[:, cc, t0:t0 + TT], nnb[cc], AF.Silu,
                     bias=prm[:, pb + P_BLN + cc:pb + P_BLN + cc + 1],
                     scale=prm[:, pb + P_GLN + cc:pb + P_GLN + cc + 1],
                     reads=[nn_r[cc], prm_r], writes=[ufin_r])

        units = [(i, cc) for i in range(NT) for cc in range(4)]
        prev = None
        pend_silu = None
        for (i, cc) in units:
            conv_unit(i, cc)
            if pend_silu is not None and cc == 1:
                ln_silu(pend_silu)
                pend_silu = None
            if prev is not None:
                stat_unit(*prev)
                if prev[1] == 3:
                    ln_finish(prev[0])
                    pend_silu = prev[0]
            prev = (i, cc)
        stat_unit(*prev)
        ln_finish(prev[0])
        ln_silu(prev[0])
        C.barrier()
        if debug:
            C.dma(dbg_u.rearrange("(c p) t -> p c t", p=128), u_final, reads=[ufin_r], sem=C.dsem("dbg"))

        arena.reset(offBC)
        Qh = [Qh0, arena.alloc([S], BF16)]
        Kh = [Kh0, arena.alloc([S], BF16)]
        Vh = [Vh0, arena.alloc([32 * 65], BF16)]
        PT = [arena.alloc([1024], BF16) for _ in range(3)]
        Osb = [arena.alloc([TT], F32) for _ in range(2)]
        recb = [arena.alloc([TT], F32) for _ in range(2)]
        PT_r = [Res("PT%d" % i) for i in range(3)]
        Osb_r = [Res("Osb0"), Res("Osb1")]
        rec_r = [Res("rec0"), Res("rec1")]
        hl_r = [Res("hl0"), Res("hl1")]
        pair_ps = [ps_all[:, p * 1024:(p + 1) * 1024] for p in range(3)]
        pair_r = [Res("pair%d" % p) for p in range(3)]
        ob_r = {6: bank_res[6], 7: bank_res[7]}

        def load_head(h):
            sl = h % 2
            sm = C.dsem("hd%d" % sl)
            C.dma(Qh[sl][0:96, :], Qd[h, :, :], writes=[Qh_r[sl]], sem=sm)
            C.dma(Kh[sl][0:96, :], Kd[h, :, :], writes=[Kh_r[sl]], sem=sm)
            C.dma(Vh[sl][:, :], Vd[h, :, :], writes=[Vh_r[sl]], sem=sm)

        cvb = Conv("b%d" % l, 2, [dve])
        if l == 0:
            cvb.add(bg_queue[0:bg_split])
        elif l == 1:
            cvb.add(bg_queue[bg_split:])
        pc = [0]
        NIT = NH * 8
        assert arena.off <= TAIL_WO, ("attention arena collides with tail buffers", arena.off, TAIL_WO)
        pair_of = {}

        def geom(n):
            h, qb = n // 8, n % 8
            return h, h % 2, qb * TT, 6 + (n % 2), n % 2

        def S_unit(n, u):
            h, sl, q0, ob, ko = geom(n)
            p = pc[0] % 3
            pc[0] += 1
            pair_of[(n, u)] = p
            for hf in range(2):
                kc = 2 * u + hf
                mm(pair_ps[p][:, hf * 512:(hf + 1) * 512], Kh[sl][0:96, kc * 128:(kc + 1) * 128],
                   Qh[sl][0:96, q0:q0 + TT], True, True, [Kh_r[sl], Qh_r[sl]], [pair_r[p]], sig=(hf == 1))

        def E_unit(n, u):
            p = pair_of[(n, u)]
            actf(PT[p], pair_ps[p], AF.Exp, scale=SCALE, reads=[pair_r[p]], writes=[PT_r[p]])

        def PV_unit(n, u):
            h, sl, q0, ob, ko = geom(n)
            p = pair_of.pop((n, u))
            for hf in range(2):
                kc = 2 * u + hf
                mm(banks[ob][0:65, :], Vh[sl][:, kc * 65:(kc + 1) * 65], PT[p][:, hf * 512:(hf + 1) * 512],
                   kc == 0, kc == 31, [Vh_r[sl], PT_r[p]], [ob_r[ob]])

        def tail1(n):
            h, sl, q0, ob, ko = geom(n)
            cp(dve, Osb[ko][0:65, :], banks[ob][0:65, :], reads=[ob_r[ob]], writes=[Osb_r[ko]])

        def tail2(n):
            h, sl, q0, ob, ko = geom(n)
            rb16 = recb[ko].bitcast(BF16)
            dhi, dlo = rb16[64:65, 0:TT], rb16[64:65, TT:2 * TT]
            cp(dve, dhi, Osb[ko][64:65, :], reads=[Osb_r[ko]], writes=[hl_r[ko]])
            tt(dve, dlo, Osb[ko][64:65, :], dhi, ALU.subtract, reads=[Osb_r[ko], hl_r[ko]], writes=[hl_r[ko]])
            mm(banks[ob][0:64, :], ones[64:65, 0:64], dhi, True, False, [const_r, hl_r[ko]], [ob_r[ob]])
            mm(banks[ob][0:64, :], ones[64:65, 0:64], dlo, False, True, [const_r, hl_r[ko]], [ob_r[ob]])
            C.op(dve, lambda: nc.vector.reciprocal(out=recb[ko][0:64, :], in_=banks[ob][0:64, :]),
                 [ob_r[ob]], [rec_r[ko]])
            po = (h % 2) * 64
            tt(dve, y_attn[po:po + 64, h // 2, q0:q0 + TT], Osb[ko][0:64, :], recb[ko][0:64, :], ALU.mult,
               reads=[Osb_r[ko], rec_r[ko]], writes=[yattn_r])

        G = [(n, u) for n in range(NIT) for u in range(16)]
        deferred = None
        S_unit(*G[0])
        S_unit(*G[1])
        for idx, (n, u) in enumerate(G):
            if u == 0 and n % 8 == 0 and n // 8 + 1 < NH:
                load_head(n // 8 + 1)
            if idx + 2 < len(G):
                S_unit(*G[idx + 2])
            E_unit(n, u)
            if deferred is not None and u == 0:
                tail1(deferred)
            if deferred is not None and u == 2:
                tail2(deferred)
                deferred = None
            PV_unit(n, u)
            if u == 8:
                cvb.step()
            if u == 0 and n == 16:
                C._wait(sp, {d_: d_.v for d_ in (C.dsem("b%dobf0" % l), C.dsem("b%dobf1" % l)) if d_.v > 0})
                C.dma(Wo.rearrange("p a b -> p (a b)"), Wo_d[l, :, :], writes=[Wo_r], sem=C.dsem("wD"))
            if u == 15:
                deferred = n
        tail1(deferred)
        tail2(deferred)
        cvb.flush()
        C.barrier()
        if debug:
            C.dma(dbg_y.rearrange("(c p) t -> p c t", p=128), y_attn, reads=[yattn_r], sem=C.dsem("dbg"))

        arena.reset(offBC)
        xd = [arena.alloc([8, TT], F32) for _ in range(2)]
        xd_r = [Res("xd0"), Res("xd1")]
        arena.reset()
        blk_pairs = [(0, 1), (1, 4), (4, 12), (12, 22)]
        Wup_b = [arena.alloc([8, 2 * 128 * (jb_ - ja_)], BF16) for (ja_, jb_) in blk_pairs]
        assert arena.off == 2 * 8 * DFF // 2, arena.off
        Wup_blk_r = [Res("Wup%d" % bi) for bi in range(4)]
        Wup_dv = Wup_d[l, :, :].rearrange("p (k n) -> p k n", k=8)

        def load_wup_block(bi):
            ja_, jb_ = blk_pairs[bi]
            ncb = 128 * (jb_ - ja_)
            for hf, base in enumerate((0, DFF)):
                C.dma(Wup_b[bi][:, :, hf * ncb:(hf + 1) * ncb], Wup_dv[:, :, base + 128 * ja_:base + 128 * jb_],
                      writes=[Wup_blk_r[bi]], sem=C.dsem("wE%d" % bi))

        def wup_cols(j, half):
            for bi, (ja_, jb_) in enumerate(blk_pairs):
                if ja_ <= j < jb_:
                    ncb = 128 * (jb_ - ja_)
                    o = half * ncb + 128 * (j - ja_)
                    return Wup_b[bi], o, Wup_blk_r[bi]
        Wdn = arena.alloc([NJ, D], BF16)
        xe = arena.alloc([8, TT], F32)
        h2 = arena.alloc([8, TT], BF16)
        rbe = arena.alloc([TT], F32)
        tmpE = arena.alloc([TT], F32)
        gE = arena.alloc([NJ, TT], BF16)
        tgb = [arena.alloc([TT], F32) for _ in range(2)]
        tvb = [arena.alloc([TT], F32) for _ in range(2)]
        sge = [arena.alloc([TT], F32) for _ in range(2)]
        xres = [arena.alloc([TT], F32) for _ in range(2)]
        Wup_r, Wdn_r, xe_r, h2_r, rbe_r, tmpE_r, gE_r = Res("Wup"), Res("Wdn"), Res("xe"), Res("h2"), Res("rbe"), Res("tmpE"), Res("gE")
        gE_rj = [Res("gE%d" % j_) for j_ in range(NJ)]
        tg_r = [Res("tg0"), Res("tg1")]
        tv_r = [Res("tv0"), Res("tv1")]
        sge_r = [Res("sge0"), Res("sge1")]
        xres_r = [Res("xres0"), Res("xres1")]
        X2v = X2.rearrange("(c p) t -> p c t", p=128)
        NTE = 9

        def tile_geom(i):
            c0 = 510 * i
            W = min(512, S + 2 - c0)
            return c0, W, W - 2

        sq_bufs = [tgb[0], tgb[1], tvb[0], tvb[1]]
        sq_res = [tg_r[0], tg_r[1], tv_r[0], tv_r[1]]

        def sq_slot(c, W):
            v = sq_bufs[c // 2].bitcast(BF16)
            o = (c % 2) * TT
            return v[:, o:o + W], sq_res[c // 2]

        def prep_load(i):
            c0, W, Wo_ = tile_geom(i)
            C.dma(xe[:, :, 0:W], X1v[:, :, c0:c0 + W], writes=[xe_r], sem=C.dsem("xe"))

        def prep_sq(i):
            c0, W, Wo_ = tile_geom(i)
            for c in range(8):
                ap_, r_ = sq_slot(c, W)
                tt(dve, ap_, xe[:, c, 0:W], xe[:, c, 0:W], ALU.mult, reads=[xe_r], writes=[r_])

        def prep_stats(i):
            c0, W, Wo_ = tile_geom(i)
            bs = nb()
            for c in range(8):
                ap_, r_ = sq_slot(c, W)
                mm(banks[bs][:, 0:W], ones[:, :], ap_, c == 0, c == 7, [r_, const_r], [bank_res[bs]], sig=True)
            rstd_from(banks[bs][:, 0:W], D, rbe[:, 0:W], tmpE[:, 0:W], [bank_res[bs]], [tmpE_r], [rbe_r])
            tt(dve, h2[:, :, 0:W], xe[:, :, 0:W], rbe[:, 0:W].unsqueeze(1).to_broadcast([128, 8, W]), ALU.mult,
               reads=[xe_r, rbe_r], writes=[h2_r])

        assert 33792 >= offBC + 2 * 4096, "FFN prologue buffers overlap phase D"
        C.dma(xd[0], xin[:, :, 0:TT], writes=[xd_r[0]], sem=C.dsem("xd0"))
        for i in range(NT):
            t0 = i * TT
            k = i % 2
            if i + 1 < NT:
                C.dma(xd[1 - k], xin[:, :, t0 + TT:t0 + 2 * TT], writes=[xd_r[1 - k]], sem=C.dsem("xd%d" % (1 - k)))
            for oc in range(8):
                b = nb()
                for kc in range(8):
                    rhs = y_attn[:, kc, t0:t0 + TT] if kc < 4 else u_final[:, kc - 4, t0:t0 + TT]
                    mm(banks[b], Wo[:, kc, oc * 128:(oc + 1) * 128], rhs, kc == 0, kc == 7,
                       [Wo_r, yattn_r, ufin_r], [bank_res[b]])
                tt(dve, xd[k][:, oc, :], xd[k][:, oc, :], banks[b], ALU.add, reads=[xd_r[k], bank_res[b]], writes=[xd_r[k]])
            C.dma(X1v[:, :, 1 + t0:1 + t0 + TT], xd[k], reads=[xd_r[k]], sem=C.dsem("xd%d" % k))
            if i == 0:
                st0_ev = (C.dsem("xd0"), C.dsem("xd0").v)
            if i == 2:
                C._wait(sp, {st0_ev[0]: st0_ev[1], C.dsem("pad"): C.dsem("pad").v})
                prep_load(0)
            if i == 4:
                prep_sq(0)
                assert 2 * 8 * 128 * 4 // 2 <= offA, "prefetched FFN weight blocks must stay inside the dead u_glu region"
            if i == 6:
                prep_stats(0)
                load_wup_block(0)
                load_wup_block(1)
        C.barrier()

        load_wup_block(2)
        load_wup_block(3)
        for q in range(2):
            C.dma(Wdn[:, q * 11:(q + 1) * 11, :].rearrange("p a b -> p (a b)"),
                  Wdn_d[l, :, q * 11 * D:(q + 1) * 11 * D], writes=[Wdn_r], sem=C.dsem("wE2"))
        prep_load(1)

        def down(i, oc):
            c0, W, Wo_ = tile_geom(i)
            k2 = oc % 2
            C.dma(xres[k2][:, 0:Wo_], X1v[:, oc, c0 + 1:c0 + 1 + Wo_], writes=[xres_r[k2]], sem=C.dsem("xr%d" % k2))
            b = nb()
            for j in range(NJ):
                mm(banks[b][:, 0:Wo_], Wdn[:, j, oc * 128:(oc + 1) * 128], gE[:, j, 0:Wo_], j == 0, j == NJ - 1,
                   [Wdn_r, gE_rj[j]], [bank_res[b]])
            tt(dve, xres[k2][:, 0:Wo_], xres[k2][:, 0:Wo_], banks[b][:, 0:Wo_], ALU.add,
               reads=[xres_r[k2], bank_res[b]], writes=[xres_r[k2]])
            C.dma(X2v[:, oc, c0:c0 + Wo_], xres[k2][:, 0:Wo_], reads=[xres_r[k2]], sem=C.dsem("xr%d" % k2))

        outv = outT.rearrange("(c p) t -> p c t", p=128)
        fsq = [xres[0].bitcast(BF16), xres[1].bitcast(BF16)]
        fsq_r = [Res("fsq0"), Res("fsq1")]
        rbF_r, tmpF2_r = Res("rbF"), Res("tmpF2")

        dl_bank = {}

        def dl_mm(i, oc, b=None):
            c0, W, Wo_ = tile_geom(i)
            if b is None:
                b = nb()
            dl_bank[oc] = b
            for j in range(NJ):
                mm(banks[b][:, 0:Wo_], Wdn[:, j, oc * 128:(oc + 1) * 128], gE[:, j, 0:Wo_], j == 0, j == NJ - 1,
                   [Wdn_r, gE_rj[j]], [bank_res[b]])

        def dl_res_loads(i):
            c0, W, Wo_ = tile_geom(i)
            for oc in range(8):
                C.dma(xe[:, oc, 0:Wo_], X1v[:, oc, c0 + 1:c0 + 1 + Wo_], writes=[xe_r], sem=C.dsem("xeo"))

        def dl_add(i, oc):
            c0, W, Wo_ = tile_geom(i)
            b = dl_bank[oc]
            tt(dve, xe[:, oc, 0:Wo_], xe[:, oc, 0:Wo_], banks[b][:, 0:Wo_], ALU.add,
               reads=[xe_r, bank_res[b]], writes=[xe_r])
            k2 = oc % 2
            actf(fsq[k2][:, 0:Wo_], xe[:, oc, 0:Wo_], AF.Square, reads=[xe_r], writes=[fsq_r[k2]])
            mm(banks[7][:, 0:Wo_], ones[:, :], fsq[k2][:, 0:Wo_], oc == 0, oc == 7, [fsq_r[k2], const_r], [bank_res[7]],
               sig=True)

        def dl_finish(i):
            c0, W, Wo_ = tile_geom(i)
            rstd_from(banks[7][:, 0:Wo_], D, rbe[:, 0:Wo_], tmpE[:, 0:Wo_], [bank_res[7]], [tmpE_r], [rbe_r])
            for oc in range(8):
                stt(dve, xe[:, oc, 0:Wo_], xe[:, oc, 0:Wo_], prm[:, P_GFINAL + oc:P_GFINAL + oc + 1], rbe[:, 0:Wo_],
                    ALU.mult, ALU.mult, reads=[xe_r, rbe_r, prm_r], writes=[xe_r])
            C.dma(outv[:, :, c0:c0 + Wo_], xe[:, :, 0:Wo_], reads=[xe_r], sem=C.dsem("xeo"))

        if last and final:
            def nb():
                b = bk[0] % 7
                bk[0] += 1
                return b

        pend_gate = [None]
        for i in range(NTE):
            c0, W, Wo_ = tile_geom(i)
            for j in range(NJ):
                k = j % 2
                wg_t, wg_o, wr = wup_cols(j, 0)
                wv_t, wv_o, _ = wup_cols(j, 1)
                bG = nb()
                for kc in range(8):
                    mm(banks[bG][:, 0:W], wg_t[:, kc, wg_o:wg_o + 128], h2[:, kc, 0:W], kc == 0, kc == 7,
                       [wr, h2_r], [bank_res[bG]])
                bV = nb()
                for kc in range(8):
                    mm(banks[bV][:, 0:W], wv_t[:, kc, wv_o:wv_o + 128], h2[:, kc, 0:W], kc == 0, kc == 7,
                       [wr, h2_r], [bank_res[bV]])
                cg = pb + P_WFFN + j * 3
                cv = pb + P_WFFN + (NJ + j) * 3
                bgc = pb + P_BFFN + j
                bvc = pb + P_BFFN + NJ + j
                tg, tv, sg = tgb[k][:, 0:Wo_], tvb[k][:, 0:Wo_], sge[k][:, 0:Wo_]
                actf(tg, banks[bG][:, 0:Wo_], AF.Identity, scale=prm[:, cg:cg + 1], reads=[bank_res[bG], prm_r], writes=[tg_r[k]])
                actf(tv, banks[bV][:, 0:Wo_], AF.Identity, scale=prm[:, cv:cv + 1], bias=prm[:, bvc:bvc + 1],
                     reads=[bank_res[bV], prm_r], writes=[tv_r[k]])
                if pend_gate[0] is not None:
                    pend_gate[0]()
                stt(dve, tg, banks[bG][:, 1:Wo_ + 1], prm[:, cg + 1:cg + 2], tg, ALU.mult, ALU.add,
                    reads=[bank_res[bG], tg_r[k], prm_r], writes=[tg_r[k]])
                stt(dve, tv, banks[bV][:, 1:Wo_ + 1], prm[:, cv + 1:cv + 2], tv, ALU.mult, ALU.add,
                    reads=[bank_res[bV], tv_r[k], prm_r], writes=[tv_r[k]])
                stt(dve, tg, banks[bG][:, 2:Wo_ + 2], prm[:, cg + 2:cg + 3], tg, ALU.mult, ALU.add,
                    reads=[bank_res[bG], tg_r[k], prm_r], writes=[tg_r[k]])
                stt(dve, tv, banks[bV][:, 2:Wo_ + 2], prm[:, cv + 2:cv + 3], tv, ALU.mult, ALU.add,
                    reads=[bank_res[bV], tv_r[k], prm_r], writes=[tv_r[k]])

                def fin(j=j, k=k, tg=tg, tv=tv, sg=sg, bgc=bgc, Wo_=Wo_, i=i):
                    if debug and i == 2 and l == 0 and j == 0:
                        C.dma(dbg_tg[:, 0:Wo_], tg, reads=[tg_r[k]], sem=C.dsem("dbg"))
                    actf(sg, tg, AF.Silu, bias=prm[:, bgc:bgc + 1], reads=[tg_r[k], prm_r], writes=[sge_r[k]])
                    tt(pool, gE[:, j, 0:Wo_], tv, sg, ALU.mult, reads=[tv_r[k], sge_r[k]], writes=[gE_rj[j]])
                pend_gate[0] = fin
            pend_gate[0]()
            pend_gate[0] = None
            if debug and i == 2 and l == 0:
                C.dma(dbg_g.rearrange("(c p) t -> p c t", p=128), gE, reads=gE_rj, sem=C.dsem("dbg"))
                C.dma(dbg_h2.rearrange("(c p) t -> p c t", p=128), h2, reads=[h2_r], sem=C.dsem("dbg"))
                C.barrier()
            if last and final:
                if i + 1 < NTE:
                    prep_sq(i + 1)
                for oc in range(3):
                    dl_mm(i, oc)
                stat_b = bk[0] % 7
                if i + 1 < NTE:
                    prep_stats(i + 1)
                else:
                    bk[0] += 1
                dl_res_loads(i)
                dl_mm(i, 3)
                dl_mm(i, 4)
                dl_mm(i, 5)
                dl_mm(i, 6, b=stat_b)
                dl_add(i, 0)
                dl_add(i, 1)
                dl_mm(i, 7, b=dl_bank[0])
                for oc in range(2, 8):
                    dl_add(i, oc)
                dl_finish(i)
                if i + 2 < NTE:
                    prep_load(i + 2)
            else:
                if i + 1 < NTE:
                    prep_sq(i + 1)
                for oc in range(4):
                    down(i, oc)
                if i + 1 < NTE:
                    prep_stats(i + 1)
                    if i + 2 < NTE:
                        prep_load(i + 2)
                for oc in range(4, 8):
                    down(i, oc)
        C.barrier()

    if final:
        return nc
    arena.reset()
    xf = [arena.alloc([8, TT], F32) for _ in range(2)]
    sqf = [arena.alloc([8, TT], BF16) for _ in range(2)]
    rbf = arena.alloc([TT], F32)
    tmpF = arena.alloc([TT], F32)
    xf_r = [Res("xf0"), Res("xf1")]
    sqf_r = [Res("sqf0"), Res("sqf1")]
    rbf_r, tmpF_r = Res("rbf"), Res("tmpF")
    X2v = X2.rearrange("(c p) t -> p c t", p=128)
    outv = outT.rearrange("(c p) t -> p c t", p=128)
    C.dma(xf[0], X2v[:, :, 0:TT], writes=[xf_r[0]], sem=C.dsem("xf0"))
    for i in range(NT):
        t0 = i * TT
        k = i % 2
        if i + 1 < NT:
            C.dma(xf[1 - k], X2v[:, :, t0 + TT:t0 + 2 * TT], writes=[xf_r[1 - k]], sem=C.dsem("xf%d" % (1 - k)))
        bs = i % 8
        for c in range(8):
            tt(dve, sqf[k][:, c, :], xf[k][:, c, :], xf[k][:, c, :], ALU.mult, reads=[xf_r[k]], writes=[sqf_r[k]])
        for c in range(8):
            mm(banks[bs], ones[:, :], sqf[k][:, c, :], c == 0, c == 7, [sqf_r[k], const_r], [bank_res[bs]], sig=True)
        rstd_from(banks[bs], D, rbf, tmpF, [bank_res[bs]], [tmpF_r], [rbf_r])
        for c in range(8):
            stt(dve, xf[k][:, c, :], xf[k][:, c, :], prm[:, P_GFINAL + c:P_GFINAL + c + 1], rbf, ALU.mult, ALU.mult,
                reads=[xf_r[k], rbf_r, prm_r], writes=[xf_r[k]])
        C.dma(outv[:, :, t0:t0 + TT], xf[k], reads=[xf_r[k]], sem=C.dsem("xf%d" % k))
    C.barrier()
    return nc


def _pack_params(inp):
    P = np.zeros((128, NP), np.float32)

    def colmajor(v, nchunk):
        return np.ascontiguousarray(v.reshape(nchunk, 128).T)

    for l in range(NL):
        pb = l * PL
        P[:, pb + P_GMIX:pb + P_GMIX + 8] = colmajor(inp["g_mix"][l], 8)
        P[:, pb + P_GQ:pb + P_GQ + 3] = colmajor(inp["g_q"][l], 3)
        P[:, pb + P_GKV:pb + P_GKV + 2] = colmajor(inp["g_kv"][l], 2)
        P[:, pb + P_GFFN:pb + P_GFFN + 8] = colmajor(inp["g_ffn"][l], 8)
        wc = inp["w_dw_conv"][l]
        P[:, pb + P_WCONV:pb + P_WCONV + 124] = wc.reshape(31, 4, 128).transpose(2, 1, 0).reshape(128, 124)
        P[:, pb + P_BCONV:pb + P_BCONV + 4] = colmajor(inp["b_dw_conv"][l], 4)
        P[:, pb + P_GLN:pb + P_GLN + 4] = colmajor(inp["g_conv_ln"][l], 4)
        P[:, pb + P_BLN:pb + P_BLN + 4] = colmajor(inp["b_conv_ln"][l], 4)
        wf = inp["w_dw_ffn"][l]
        P[:, pb + P_WFFN:pb + P_WFFN + 132] = wf.reshape(3, 44, 128).transpose(2, 1, 0).reshape(128, 132)
        P[:, pb + P_BFFN:pb + P_BFFN + 44] = colmajor(inp["b_dw_ffn"][l], 44)
    P[:, P_GFINAL:P_GFINAL + 8] = colmajor(inp["g_final"], 8)
    inv_freq = (1.0 / (np.float32(10000.0) ** (np.arange(0, 32, 2, dtype=np.float32) / np.float32(32.0)))).astype(np.float32)
    for p in range(64, 96):
        P[p, P_INVF] = inv_freq[(p - 64) % 16]
    return P


_NC_CACHE = {}


def make_in_maps(inp):
    inp = {k: np.asarray(v) for k, v in inp.items()}
    P = _pack_params(inp)
    maps = []
    for b in range(8):
        maps.append({
            "xT": np.ascontiguousarray(inp["x"][b].T),
            "pos": np.ascontiguousarray(inp["positions"][b].reshape(1, S).astype(np.int32)),
            "params": P,
            "w_in": inp["w_in"], "w_uq": inp["w_uq"], "w_ukv": inp["w_ukv"], "w_o": inp["w_o"],
            "w_up": inp["w_up"], "w_down": inp["w_down"],
        })
    return maps


def kernel(**inputs):
    if "nc" not in _NC_CACHE:
        _NC_CACHE["nc"] = build_program()
    nc = _NC_CACHE["nc"]
    maps = make_in_maps(inputs)
    res = run_bass_kernel_spmd(nc, maps, core_ids=list(range(8)))
    out = np.stack([np.ascontiguousarray(res.results[b]["outT"].T) for b in range(8)], axis=0)
    return out.astype(np.float32)
```
